# Optimizing a Trainium2 kernel written in Bass

```python
import math
import jax, jax.numpy as jnp
from jax import lax
import numpy as np

D_MODEL = 1024
BATCH = 2
SEQ = 8192
DEPTH = 1

CTX_LEN = 256
GRID_W = 64
MIX_W = D_MODEL
ATTN_W = MIX_W // 2
ATTN_HEAD_DIM = 64
ATTN_HEADS = ATTN_W // ATTN_HEAD_DIM
ATTN_KV_HEADS = ATTN_HEADS // 4
ATTN_GROUP = ATTN_HEADS // ATTN_KV_HEADS
ATTN_KV_W = ATTN_KV_HEADS * ATTN_HEAD_DIM
DN_W = MIX_W - ATTN_W
DN_HEAD_DIM = 128
DN_HEADS = DN_W // DN_HEAD_DIM
N_DIR = 2
CONV_K = 5
CHUNK = 64
Q_BLOCK = 128
D_FF = 4 * D_MODEL
N_MOD = 6
ROPE_THETA = 10000.0
NORM_EPS = 1e-6
IN_COLS = ATTN_W + 2 * ATTN_KV_W + 4 * DN_W + 2 * N_DIR * DN_HEADS

kernel_name = 'hybrid_attn_gdn_dit_block'


def _rms(x, w):
    xf = x.astype(jnp.float32)
    y = xf * lax.rsqrt(jnp.mean(xf * xf, axis=-1, keepdims=True) + NORM_EPS)
    return (y * w.astype(jnp.float32)).astype(x.dtype)


def _l2n(x):
    return x * lax.rsqrt(jnp.sum(x * x, axis=-1, keepdims=True) + NORM_EPS)


def _rope_axis(x, pos):
    half = x.shape[-1] // 2
    inv = ROPE_THETA ** (-jnp.arange(half, dtype=jnp.float32) / half)
    ang = pos.astype(jnp.float32)[:, None] * inv[None, :]
    cos = jnp.cos(ang)[None, :, None, :]
    sin = jnp.sin(ang)[None, :, None, :]
    x1, x2 = x[..., :half], x[..., half:]
    return jnp.concatenate([x1 * cos - x2 * sin, x2 * cos + x1 * sin], axis=-1)


def _rope_2d(x, row, col):
    xf = x.astype(jnp.float32)
    h = x.shape[-1] // 2
    y = jnp.concatenate([_rope_axis(xf[..., :h], row), _rope_axis(xf[..., h:], col)], axis=-1)
    return y.astype(x.dtype)


def _split_proj(p):
    widths = (ATTN_W, ATTN_KV_W, ATTN_KV_W, 3 * DN_W, DN_W, N_DIR * DN_HEADS, N_DIR * DN_HEADS)
    out, off = [], 0
    for wdt in widths:
        out.append(p[..., off:off + wdt])
        off += wdt
    return out


def _short_conv(x, w):
    y = lax.conv_general_dilated(
        x, w[:, None, :].astype(x.dtype), window_strides=(1,),
        padding=[(CONV_K // 2, CONV_K // 2)],
        dimension_numbers=('NWC', 'WIO', 'NWC'), feature_group_count=x.shape[-1])
    return jax.nn.silu(y)


def _gdn_gates(b_raw, a_raw, a_log, dt_bias):
    shp = b_raw.shape[:-1] + (N_DIR, DN_HEADS)
    beta = jax.nn.sigmoid(b_raw.astype(jnp.float32)).reshape(shp)
    g = -jnp.exp(a_log.astype(jnp.float32)) * jax.nn.softplus(
        a_raw.astype(jnp.float32).reshape(shp) + dt_bias.astype(jnp.float32))
    return beta, g


def _dir(a, d):
    return a[:, ::-1] if d == 1 else a


def _gdn_chunked(k, v, g, beta, s0, q=None):
    b, t, h, dk = k.shape
    n = t // CHUNK

    def chunks(a):
        a = a.reshape((b, n, CHUNK, h) + a.shape[3:])
        return jnp.moveaxis(jnp.moveaxis(a, 1, 0), 3, 2)

    kc, vc, bc = chunks(k), chunks(v), chunks(beta)
    gc = jnp.cumsum(chunks(g), axis=-1)
    idx = jnp.arange(CHUNK)
    incl = idx[:, None] >= idx[None, :]
    strict = idx[:, None] > idx[None, :]
    decay = jnp.exp(jnp.where(incl, gc[..., :, None] - gc[..., None, :], -jnp.inf))
    kb = kc * bc[..., None]
    lower = jnp.where(strict, jnp.einsum('nbhid,nbhjd->nbhij', kb, kc) * decay, 0.0)
    eye = jnp.eye(CHUNK, dtype=jnp.float32)
    t_inv = lax.linalg.triangular_solve(eye + lower, jnp.broadcast_to(eye, lower.shape),
                                        left_side=True, lower=True, unit_diagonal=True)
    u = t_inv @ (vc * bc[..., None])
    w = t_inv @ (kb * jnp.exp(gc)[..., None])
    g_last = gc[..., -1]
    k_dec = kc * jnp.exp(g_last[..., None] - gc)[..., None]
    xs = (u, w, k_dec, g_last)
    if q is not None:
        qs = chunks(q) * (dk ** -0.5)
        xs = xs + (qs * jnp.exp(gc)[..., None], jnp.einsum('nbhid,nbhjd->nbhij', qs, kc) * decay)

    def step(s, xs_i):
        u_i, w_i, kd_i, gl_i = xs_i[:4]
        v_new = u_i - jnp.einsum('bhcd,bhde->bhce', w_i, s)
        s_new = s * jnp.exp(gl_i)[..., None, None] + jnp.einsum('bhcd,bhce->bhde', kd_i, v_new)
        if q is None:
            return s_new, None
        qd_i, intra_i = xs_i[4:]
        o_i = jnp.einsum('bhcd,bhde->bhce', qd_i, s) + jnp.einsum('bhij,bhje->bhie', intra_i, v_new)
        return s_new, o_i

    s_final, o = lax.scan(step, s0, xs)
    if q is None:
        return None, s_final
    return o.transpose(1, 0, 3, 2, 4).reshape(b, t, h, v.shape[-1]), s_final


def _latent_attention(q, k_lat, v_lat, k_ctx, v_ctx):
    b, t, _, hd = q.shape
    k_all = jnp.concatenate([k_ctx, k_lat], axis=1)
    v_all = jnp.concatenate([v_ctx, v_lat], axis=1)
    n_blk = t // Q_BLOCK
    qb = q.reshape(b, n_blk, Q_BLOCK, ATTN_KV_HEADS, ATTN_GROUP, hd).transpose(1, 0, 2, 3, 4, 5)
    scale = hd ** -0.5

    def one_block(q_blk):
        s = jnp.einsum('bqkgd,bskd->bkgqs', q_blk, k_all).astype(jnp.float32) * scale
        p = jax.nn.softmax(s, axis=-1).astype(v_all.dtype)
        return jnp.einsum('bkgqs,bskd->bqkgd', p, v_all)

    o = lax.map(one_block, qb)
    return o.transpose(1, 0, 2, 3, 4, 5).reshape(b, t, ATTN_W)


def setup_inputs(seed: int = 0) -> dict:
    key = jax.random.key(seed)
    ks = jax.random.split(key, 20)
    f32 = jnp.float32

    def nrm(k, shape, scale):
        return jax.random.normal(k, shape, f32) * scale

    dt = jnp.exp(jax.random.uniform(ks[12], (DEPTH, N_DIR, DN_HEADS), f32,
                                    math.log(1e-3), math.log(1e-1)))
    return {
        'x': nrm(ks[0], (BATCH, SEQ, D_MODEL), 1.0),
        'c': nrm(ks[1], (BATCH, D_MODEL), 1.0),
        'ctx': nrm(ks[2], (BATCH, CTX_LEN, D_MODEL), 1.0),
        'c_ctx': nrm(ks[3], (D_MODEL,), 1.0),
        'w_mod': nrm(ks[4], (DEPTH, D_MODEL, N_MOD * D_MODEL), 0.5 * D_MODEL ** -0.5),
        'b_mod': nrm(ks[5], (DEPTH, N_MOD * D_MODEL), 0.01),
        'norm1_w': 1.0 + nrm(ks[6], (DEPTH, D_MODEL), 0.02),
        'w_in': nrm(ks[7], (DEPTH, D_MODEL, IN_COLS), D_MODEL ** -0.5),
        'q_norm_w': 1.0 + nrm(ks[8], (DEPTH, ATTN_HEAD_DIM), 0.02),
        'k_norm_w': 1.0 + nrm(ks[9], (DEPTH, ATTN_HEAD_DIM), 0.02),
        'conv_w': nrm(ks[10], (DEPTH, CONV_K, 3 * DN_W), CONV_K ** -0.5),
        'a_log': jnp.log(jax.random.uniform(ks[11], (DEPTH, N_DIR, DN_HEADS), f32, 1.0, 16.0)),
        'dt_bias': dt + jnp.log(-jnp.expm1(-dt)),
        'dn_norm_w': 1.0 + nrm(ks[13], (DEPTH, DN_HEAD_DIM), 0.02),
        'w_out': nrm(ks[14], (DEPTH, MIX_W, D_MODEL), MIX_W ** -0.5),
        'norm2_w': 1.0 + nrm(ks[15], (DEPTH, D_MODEL), 0.02),
        'w_mlp1': nrm(ks[16], (DEPTH, D_MODEL, D_FF), D_MODEL ** -0.5),
        'w_mlp2': nrm(ks[17], (DEPTH, D_FF, D_MODEL), D_FF ** -0.5),
    }


def reference(x, c, ctx, c_ctx, w_mod, b_mod, norm1_w, w_in, q_norm_w, k_norm_w, conv_w,
              a_log, dt_bias, dn_norm_w, w_out, norm2_w, w_mlp1, w_mlp2):
    f32 = jnp.float32
    b, t, _ = x.shape
    n_ctx = ctx.shape[1]
    rows = t // GRID_W
    row = jnp.repeat(jnp.arange(rows, dtype=jnp.int32), GRID_W, total_repeat_length=rows * GRID_W)
    col = jnp.arange(t, dtype=jnp.int32) % GRID_W
    silu_c = jax.nn.silu(c)
    silu_cc = jax.nn.silu(c_ctx)
    for l in range(DEPTH):
        mod = (silu_c @ w_mod[l] + b_mod[l]).reshape(b, N_MOD, D_MODEL)[:, :, None, :]
        sh1, sc1, g1, sh2, sc2, g2 = (mod[:, i] for i in range(N_MOD))
        mod_ctx = (silu_cc @ w_mod[l] + b_mod[l]).reshape(N_MOD, D_MODEL)

        h = _rms(x, norm1_w[l]) * (1.0 + sc1) + sh1
        hc = _rms(ctx, norm1_w[l]) * (1.0 + mod_ctx[1]) + mod_ctx[0]
        aq, ak, av, dqkv, dz, db, da = _split_proj(h @ w_in[l])
        _, cak, cav, cdqkv, _, cdb, cda = _split_proj(hc @ w_in[l])

        q_a = _rope_2d(_rms(aq.reshape(b, t, ATTN_HEADS, ATTN_HEAD_DIM), q_norm_w[l]), row, col)
        k_a = _rope_2d(_rms(ak.reshape(b, t, ATTN_KV_HEADS, ATTN_HEAD_DIM), k_norm_w[l]), row, col)
        v_a = av.reshape(b, t, ATTN_KV_HEADS, ATTN_HEAD_DIM)
        k_c = _rms(cak.reshape(b, n_ctx, ATTN_KV_HEADS, ATTN_HEAD_DIM), k_norm_w[l])
        v_c = cav.reshape(b, n_ctx, ATTN_KV_HEADS, ATTN_HEAD_DIM)
        o_attn = _latent_attention(q_a, k_a, v_a, k_c, v_c)

        qkv = _short_conv(dqkv, conv_w[l]).astype(f32).reshape(b, t, 3, DN_HEADS, DN_HEAD_DIM)
        dn_q, dn_k, dn_v = _l2n(qkv[:, :, 0]), _l2n(qkv[:, :, 1]), qkv[:, :, 2]
        kv_c = _short_conv(cdqkv[..., DN_W:], conv_w[l][:, DN_W:]).astype(f32)
        kv_c = kv_c.reshape(b, n_ctx, 2, DN_HEADS, DN_HEAD_DIM)
        ck, cv = _l2n(kv_c[:, :, 0]), kv_c[:, :, 1]
        beta, gdec = _gdn_gates(db, da, a_log[l], dt_bias[l])
        cbeta, cgdec = _gdn_gates(cdb, cda, a_log[l], dt_bias[l])
        s_zero = jnp.zeros((b, DN_HEADS, DN_HEAD_DIM, DN_HEAD_DIM), f32)
        o_dn = jnp.zeros((b, t, DN_HEADS, DN_HEAD_DIM), f32)
        for d in range(N_DIR):
            _, s_ctx = _gdn_chunked(_dir(ck, d), _dir(cv, d), _dir(cgdec[:, :, d], d),
                                    _dir(cbeta[:, :, d], d), s_zero)
            o_d, _ = _gdn_chunked(_dir(dn_k, d), _dir(dn_v, d), _dir(gdec[:, :, d], d),
                                  _dir(beta[:, :, d], d), s_ctx, q=_dir(dn_q, d))
            o_dn = o_dn + _dir(o_d, d)
        z = dz.astype(f32).reshape(b, t, DN_HEADS, DN_HEAD_DIM)
        o_dn = (_rms(o_dn, dn_norm_w[l]) * jax.nn.silu(z)).reshape(b, t, DN_W).astype(x.dtype)

        mixed = jnp.concatenate([o_attn, o_dn], axis=-1) @ w_out[l]
        x = x + g1 * mixed

        h2 = _rms(x, norm2_w[l]) * (1.0 + sc2) + sh2
        x = x + g2 * (jnp.square(jax.nn.relu(h2 @ w_mlp1[l])) @ w_mlp2[l])
    return x
```

```python
import numpy as np
from contextlib import ExitStack
import concourse.bass as bass
import concourse.mybir as mybir
from concourse.bass_utils import run_bass_kernel_spmd

F32 = mybir.dt.float32
BF16 = mybir.dt.bfloat16
AF = mybir.ActivationFunctionType
ALU = mybir.AluOpType
AX = mybir.AxisListType

D = 1024
T = 8192
NCTX = 256
NT = T // 128
QT = 16
EPS = 1e-6
BIG = 1.0e9
DEBUG = False
SAME_ENGINE_SYNC = True


class Buf:
    __slots__ = ("t", "name", "w", "r", "dsem", "dcnt")

    def __init__(self, t, name):
        self.t = t
        self.name = name
        self.w = None
        self.r = {}
        self.dsem = None
        self.dcnt = 0

    def __getitem__(self, k):
        return self.t[k]


class KB:
    def __init__(self, nc, es):
        self.nc = nc
        self.es = es
        self.eng = {"pe": nc.tensor, "act": nc.scalar, "dve": nc.vector, "pool": nc.gpsimd, "sp": nc.sync}
        self.sem = {e: es.enter_context(nc.semaphore("sem_" + e)) for e in ("pe", "act", "dve", "pool")}
        self.cnt = {e: 0 for e in self.sem}
        self.seen = {e: {} for e in self.eng}
        self.nbuf = 0
        self.banks = []
        self.bank_i = 0
        self.ninst = 0
        self.reserved = set()

    def sb(self, shape, dtype, name=None):
        self.nbuf += 1
        name = "s_" + (name or f"b{self.nbuf}")
        return Buf(self.es.enter_context(self.nc.sbuf_tensor(name, list(shape), dtype)), name)

    def init_psum(self):
        for i in range(8):
            t = self.es.enter_context(self.nc.psum_tensor(f"psb{i}", [128, 512], F32))
            self.banks.append(Buf(t, f"psb{i}"))

    def bank(self):
        while True:
            b = self.banks[self.bank_i % 8]
            self.bank_i += 1
            if b.name not in self.reserved:
                return b

    def reserve(self):
        b = self.bank()
        self.reserved.add(b.name)
        return b

    def release(self, b):
        self.reserved.discard(b.name)

    def _wait(self, e, dep):
        if dep is None:
            return
        kind = dep[0]
        if kind == "dma":
            key = ("dma", dep[3])
            semh = dep[1]
            val = dep[2]
        else:
            if kind == e and (e == "pe" or not SAME_ENGINE_SYNC):
                return
            key = kind
            semh = self.sem[kind]
            val = dep[1]
        if self.seen[e].get(key, 0) >= val:
            return
        self.eng[e].wait_ge(semh, val)
        self.seen[e][key] = val

    def _deps(self, e, reads, writes):
        for b in reads:
            self._wait(e, b.w)
        for b in writes:
            self._wait(e, b.w)
            for d in list(b.r.values()):
                self._wait(e, d)

    def op(self, e, fn, reads=(), writes=()):
        self._deps(e, reads, writes)
        ins = fn(self.eng[e])
        self.cnt[e] += 1
        self.ninst += 1
        ins.then_inc(self.sem[e], 1)
        dep = (e, self.cnt[e])
        for b in reads:
            b.r[e] = dep
        for b in writes:
            b.w = dep
            b.r = {}
        return ins

    def dma(self, q, out, in_, reads=(), writes=()):
        b = writes[0] if writes else reads[0]
        if b.dsem is None:
            b.dsem = self.es.enter_context(self.nc.semaphore("dsem_" + b.name))
        if b.dcnt > 0:
            self._wait(q, ("dma", b.dsem, b.dcnt, b.name))
        self._deps(q, reads, writes)
        ins = self.eng[q].dma_start(out=out, in_=in_)
        b.dcnt += 16
        self.ninst += 1
        ins.then_inc(b.dsem, 16)
        dep = ("dma", b.dsem, b.dcnt, b.name)
        for x in writes:
            x.w = dep
            x.r = {}
        for x in reads:
            x.r[("dma", b.name)] = dep
        return dep

    def mm(self, out_b, out_ap, lhsT_b, lhsT_ap, rhs_b, rhs_ap, start=True, stop=True):
        return self.op("pe", lambda g: g.matmul(out_ap, lhsT_ap, rhs_ap, start=start, stop=stop),
                       reads=[lhsT_b, rhs_b], writes=[out_b])

    def tr(self, out_b, out_ap, in_b, in_ap, id_b, id_ap):
        return self.op("pe", lambda g: g.transpose(out_ap, in_ap, id_ap), reads=[in_b, id_b], writes=[out_b])

    def act(self, out_b, out_ap, in_b, in_ap, func, bias=None, scale=None, accum=None, extra_reads=(), extra_writes=()):
        kw = {}
        if bias is not None:
            kw["bias"] = bias
        if scale is not None:
            kw["scale"] = scale
        if accum is not None:
            kw["accum_out"] = accum
        return self.op("act", lambda g: g.activation(out_ap, in_ap, func, **kw),
                       reads=[in_b] + list(extra_reads), writes=[out_b] + list(extra_writes))

    def tt(self, e, out_b, out_ap, a_b, a_ap, b_b, b_ap, op):
        return self.op(e, lambda g: g.tensor_tensor(out_ap, a_ap, b_ap, op), reads=[a_b, b_b], writes=[out_b])

    def ts(self, e, out_b, out_ap, a_b, a_ap, s1, s2, op0, op1=None, extra_reads=()):
        if op1 is None:
            f = lambda g: g.tensor_scalar(out_ap, a_ap, s1, None, op0)
        else:
            f = lambda g: g.tensor_scalar(out_ap, a_ap, s1, s2, op0, op1)
        return self.op(e, f, reads=[a_b] + list(extra_reads), writes=[out_b])

    def stt(self, out_b, out_ap, a_b, a_ap, scalar, b_b, b_ap, op0, op1, extra_reads=()):
        return self.op("dve", lambda g: g.scalar_tensor_tensor(out_ap, a_ap, scalar, b_ap, op0, op1),
                       reads=[a_b, b_b] + list(extra_reads), writes=[out_b])

    def cp(self, e, out_b, out_ap, in_b, in_ap):
        if e == "act":
            return self.op("act", lambda g: g.copy(out_ap, in_ap), reads=[in_b], writes=[out_b])
        return self.op(e, lambda g: g.tensor_copy(out_ap, in_ap), reads=[in_b], writes=[out_b])

    def memset(self, e, b, ap, val):
        return self.op(e, lambda g: g.memset(ap, val), writes=[b])


GOFF = 768
ZOFF = 768 + 1536
GATE = ZOFF + 512
NCOL = GATE + 16
CFG = {"pre1": 16, "pre2": 48, "own": 16}
N1 = 2 + CFG["pre1"] + CFG["own"]
N2 = 2 + CFG["pre2"] + CFG["own"]
DK = 128


def set_cfg(pre1, pre2, own):
    global N1, N2
    CFG.update(pre1=pre1, pre2=pre2, own=own)
    N1 = 2 + pre1 + own
    N2 = 2 + pre2 + own


def build_program():
    nc = bass.Bass("TRN2", target_bir_lowering=False)

    def din(name, shape, dt=F32):
        return nc.dram_tensor(name, list(shape), dt, kind="ExternalInput").ap()

    xs_d = [din("xs1", [N1, 128, D]), din("xs2", [N2, 128, D])]
    xh_d = [din("xh1", [N1, 4, D]), din("xh2", [N2, 4, D])]
    rope_d = din("rope2", [N2, 128, 1280])
    flg_d = din("flags", [128, 2, N2, 4])
    cc_d = din("cc", [128, 16])
    wmod_d = din("wmod", [6, 128, 8 * D])
    bmod_d = din("bmod", [128, 6 * D])
    n1w_d = din("n1w", [128, D])
    n2w_d = din("n2w", [128, D])
    win_d = din("win", [128, 8, NCOL])
    qkw_d = din("qkw", [128, 640])
    cw_d = din("convw", [128, 60])
    alog_d = din("alog", [128, 8])
    dtb_d = din("dtb", [128, 8])
    dnw_d = din("dnw", [128, 512])
    woa_d = din("woa", [64, 8, D])
    wod_d = din("wod", [128, 4, D])
    w1_d = din("w1", [128, 8, 4 * D])
    w2_d = din("w2", [128, 32, D])
    xown_d = din("xown", [CFG["own"], 128, D])
    ident_d = din("ident", [128, 128])
    tri_d = din("tri", [128, 2, 128])
    msk_d = din("masks", [128, 4, 512])
    imk_d = din("imask", [128, 4, 512])
    out_d = nc.dram_tensor("out", [CFG["own"], 128, D], F32, kind="ExternalOutput").ap()

    kT_d = nc.dram_tensor("kT_s", [128, N2 * 128], BF16).ap()
    v_d = nc.dram_tensor("v_s", [N2, 128, 130], BF16).ap()
    qT_d = nc.dram_tensor("qT_s", [4, 128, CFG["own"] * 128], BF16).ap()
    o1_d = nc.dram_tensor("o1_s", [CFG["own"], 128, 512], F32).ap()
    mdn_d = nc.dram_tensor("mdn_s", [CFG["own"], 128, 512], BF16).ap()
    mod_d = nc.dram_tensor("mod_s", [4, 128, D], F32).ap()
    oa_d = nc.dram_tensor("oa_s", [64, 8, CFG["own"] * 128], BF16).ap()
    w2b_d = nc.dram_tensor("w2b_s", [128, 32, D], BF16).ap()

    with ExitStack() as es:
        kb = KB(nc, es)
        kb.init_psum()

        def barrier():
            for e in ("pe", "act", "dve", "pool", "sp"):
                for o in ("pe", "act", "dve", "pool"):
                    if o != e:
                        kb._wait(e, (o, kb.cnt[o]))

        ident = kb.sb([128, 128], F32, "ident")
        identb = kb.sb([128, 128], BF16, "identb")
        onesf = kb.sb([128, 128], F32, "onesf")
        kb.dma("sp", ident[:, :], ident_d[:, :], writes=[ident])
        kb.cp("dve", identb, identb[:, :], ident, ident[:, :])
        kb.memset("dve", onesf, onesf[:, :], 1.0)

        S1 = kb.sb([128, D], F32, "S1")
        H1 = kb.sb([128, D], F32, "H1")
        S1c = kb.sb([128, D], F32, "S1c")
        H1c = kb.sb([128, D], F32, "H1c")

        with ExitStack() as esA:
            kbA = kb
            old_es = kb.es
            kb.es = esA
            cc = kb.sb([128, 16], F32, "cc")
            sg = kb.sb([128, 16], F32, "ccsg")
            Lc = kb.sb([128, 16, 128], F32, "Lc")
            bm = kb.sb([128, 6 * D], F32, "bm")
            nw = kb.sb([128, D], F32, "nw")
            stage = [kb.sb([128, 8 * D], F32, f"wmst{i}") for i in range(2)]
            res = kb.sb([128, D], F32, "modres")
            kb.dma("sp", cc[:, :], cc_d[:, :], writes=[cc])
            kb.dma("sp", bm[:, :], bmod_d[:, :], writes=[bm])
            kb.dma("sp", nw[:, :], n1w_d[:, :], writes=[nw])
            kb.act(sg, sg[:, :], cc, cc[:, :], AF.Exp, scale=-1.0)
            kb.ts("dve", sg, sg[:, :], sg, sg[:, :], 1.0, None, ALU.add)
            kb.op("dve", lambda g: g.reciprocal(sg[:, :], sg[:, :]), reads=[sg], writes=[sg])
            kb.tt("dve", sg, sg[:, :], sg, sg[:, :], cc, cc[:, :], ALU.mult)
            for j in range(16):
                kb.ts("dve", Lc, Lc[:, j, :], onesf, onesf[:, :], sg[:, j:j + 1], None, ALU.mult, extra_reads=[sg])

            def mod_block(blk, lofs, dst_fn):
                st = stage[blk % 2]
                for half in range(2):
                    pb = kb.bank()
                    for k in range(8):
                        kb.mm(pb, pb[:, :], Lc, Lc[:, lofs + k, :], st, st[:, k * D + half * 512:k * D + half * 512 + 512],
                              start=(k == 0), stop=(k == 7))
                    dst_fn(pb, half)

            for blk in range(6):
                st = stage[blk % 2]
                kb.dma("sp", st[:, :], wmod_d[blk, :, :], writes=[st])
                if blk == 3:
                    kb.dma("sp", nw[:, :], n2w_d[:, :], writes=[nw])

                def plain(dst):
                    def f(pb, half, dst=dst, blk=blk):
                        sl = slice(half * 512, half * 512 + 512)
                        kb.tt("dve", dst, dst[:, sl], pb, pb[:, :], bm, bm[:, blk * D + half * 512:blk * D + half * 512 + 512], ALU.add)
                    return f

                def scale(dst):
                    def f(pb, half, dst=dst, blk=blk):
                        sl = slice(half * 512, half * 512 + 512)
                        kb.tt("dve", dst, dst[:, sl], pb, pb[:, :], bm, bm[:, blk * D + half * 512:blk * D + half * 512 + 512], ALU.add)
                        kb.stt(dst, dst[:, sl], dst, dst[:, sl], 1.0, nw, nw[:, sl], ALU.add, ALU.mult)
                    return f

                def spill(idx, fn_maker):
                    def f(pb, half, idx=idx):
                        fn_maker(res)(pb, half)
                        if half == 1:
                            kb.dma("sp", mod_d[idx, :, :], res[:, :], reads=[res])
                    return f

                if blk == 0:
                    mod_block(blk, 0, plain(H1))
                    mod_block(blk, 8, plain(H1c))
                elif blk == 1:
                    mod_block(blk, 0, scale(S1))
                    mod_block(blk, 8, scale(S1c))
                elif blk == 2:
                    mod_block(blk, 0, spill(0, plain))
                elif blk == 3:
                    mod_block(blk, 0, spill(2, plain))
                elif blk == 4:
                    mod_block(blk, 0, spill(1, scale))
                else:
                    mod_block(blk, 0, spill(3, plain))
            if DEBUG:
                dbg = nc.dram_tensor("dbgA", [8, 128, D], F32, kind="ExternalOutput").ap()
                for n_, b_ in enumerate((S1, H1, S1c, H1c)):
                    dd = kb.dma("sp", dbg[n_, :, :], b_[:, :], reads=[b_])
                    kb._wait("sp", dd)
                tmpd = kb.sb([128, D], F32, "dbgtmp")
                for n_ in range(4):
                    kb.dma("sp", tmpd[:, :], mod_d[n_, :, :], writes=[tmpd])
                    dd = kb.dma("sp", dbg[4 + n_, :, :], tmpd[:, :], reads=[tmpd])
                    kb._wait("sp", dd)
            barrier()
            kb.es = old_es

        esB = ExitStack()
        kb.es = esB
        wb = kb.sb([128, 8, NCOL], BF16, "wb")
        cdiag = kb.sb([128, 12, 5, 128], BF16, "cdiag")
        flg = kb.sb([128, 2, N2, 4], F32, "flg")
        with ExitStack() as esW:
            old_es = kb.es
            kb.es = esW
            wst = kb.sb([128, 8, 708], F32, "wst")
            cw = kb.sb([128, 60], F32, "cw")
            for part in range(4):
                sl = slice(part * 708, (part + 1) * 708)
                kb.dma("sp", wst[:, :, :], win_d[:, :, sl], writes=[wst])
                kb.cp("dve" if part % 2 == 0 else "pool", wb, wb[:, :, sl], wst, wst[:, :, :])
            kb.dma("sp", cw[:, :], cw_d[:, :], writes=[cw])
            kb.dma("sp", flg[:, :, :, :], flg_d[:, :, :, :], writes=[flg])
            for blk in range(12):
                for j in range(5):
                    kb.ts("dve", cdiag, cdiag[:, blk, j, :], ident, ident[:, :], cw[:, blk * 5 + j:blk * 5 + j + 1], None,
                          ALU.mult, extra_reads=[cw])
            barrier()
            kb.es = old_es

        xt = [kb.sb([128, D], F32, f"xt{i}") for i in range(2)]
        xh = kb.sb([4, D], F32, "xh")
        junk = kb.sb([128, D], BF16, "junk")
        ssq = kb.sb([128, 1], F32, "ssq")
        rstd = kb.sb([128, 1], F32, "rstd")
        t32 = kb.sb([128, D], F32, "t32")
        hb = kb.sb([128, D], BF16, "hb")
        hT = kb.sb([128, 8, 132], BF16, "hT")
        preT = kb.sb([128, 12, 132], BF16, "preT")
        sgt = kb.sb([128, 512], F32, "sgt")
        cv = kb.sb([128, 12 * 128], F32, "cv")

        def norm_rows(xbuf, np_, Sb, Hb):
            kb.act(junk, junk[0:np_, :], xbuf, xbuf[0:np_, :], AF.Square, accum=ssq[0:np_, :], extra_writes=[ssq])
            kb.act(rstd, rstd[0:np_, :], ssq, ssq[0:np_, :], AF.Ln, bias=EPS, scale=1.0 / D)
            kb.act(rstd, rstd[0:np_, :], rstd, rstd[0:np_, :], AF.Exp, scale=-0.5)
            kb.stt(t32, t32[0:np_, :], xbuf, xbuf[0:np_, :], rstd[0:np_, 0:1], Sb, Sb[0:np_, :], ALU.mult, ALU.mult,
                   extra_reads=[rstd])
            kb.tt("dve", hb, hb[0:np_, :], t32, t32[0:np_, :], Hb, Hb[0:np_, :], ALU.add)

        def norm_tile(ds, vis, xbuf, Sb, Hb):
            kb.dma("sp", xbuf[:, :], xs_d[ds][vis, :, :], writes=[xbuf])
            kb.dma("sp", xh[:, :], xh_d[ds][vis, :, :], writes=[xh])
            norm_rows(xbuf, 128, Sb, Hb)
            pb = kb.bank()
            pbv = pb.t[:, :].bitcast(BF16)
            for k in range(8):
                kb.tr(pb, pbv[:, k * 128:(k + 1) * 128], hb, hb[:, k * 128:(k + 1) * 128], identb, identb[:, :])
            kb.cp("act", hT, hT[:, :, 2:130], pb, pbv.rearrange("p (k t) -> p k t", k=8))
            norm_rows(xh, 4, Sb, Hb)
            pb2 = kb.bank()
            pbv2 = pb2.t[:, :].bitcast(BF16)
            for k in range(8):
                kb.tr(pb2, pbv2[:, k * 4:(k + 1) * 4], hb, hb[0:4, k * 128:(k + 1) * 128], identb, identb[0:4, 0:4])
            hv = pbv2[:, 0:32].rearrange("p (k t) -> p k t", k=8)
            kb.cp("act", hT, hT[:, :, 0:2], pb2, hv[:, :, 0:2])
            kb.cp("act", hT, hT[:, :, 130:132], pb2, hv[:, :, 2:4])

        def project_tok(col0, ncols):
            pb = kb.bank()
            for k in range(8):
                kb.mm(pb, pb[:, 0:ncols], hT, hT[:, k, 2:130], wb, wb[:, k, col0:col0 + ncols], start=(k == 0), stop=(k == 7))
            return pb

        def gdn_pre_conv(ds, vis, nblk):
            for g0 in range(0, nblk, 3):
                pb = kb.bank()
                for j in range(3):
                    blk = g0 + j
                    for k in range(8):
                        kb.mm(pb, pb[:, j * 132:(j + 1) * 132], wb, wb[:, k, GOFF + blk * 128:GOFF + (blk + 1) * 128],
                              hT, hT[:, k, :], start=(k == 0), stop=(k == 7))
                kb.cp("act", preT, preT[:, g0:g0 + 3, :], pb, pb[:, 0:396].rearrange("p (j t) -> p j t", j=3))
            kb.ts("dve", preT, preT[:, 0:nblk, 0:2], preT, preT[:, 0:nblk, 0:2], flg[:, ds, vis, 0:1], None, ALU.mult, extra_reads=[flg])
            kb.ts("dve", preT, preT[:, 0:nblk, 130:132], preT, preT[:, 0:nblk, 130:132], flg[:, ds, vis, 1:2], None, ALU.mult, extra_reads=[flg])
            for g0 in range(0, nblk, 4):
                pb = kb.bank()
                for j4 in range(4):
                    blk = g0 + j4
                    for j in range(5):
                        kb.mm(pb, pb[:, j4 * 128:(j4 + 1) * 128], preT, preT[:, blk, j:j + 128], cdiag, cdiag[:, blk, j, :],
                              start=(j == 0), stop=(j == 4))
                kb.act(sgt, sgt[:, :], pb, pb[:, :], AF.Exp, scale=-1.0)
                kb.ts("dve", sgt, sgt[:, :], sgt, sgt[:, :], 1.0, None, ALU.add)
                kb.op("dve", lambda g: g.reciprocal(sgt[:, :], sgt[:, :]), reads=[sgt], writes=[sgt])
                kb.tt("dve", cv, cv[:, g0 * 128:(g0 + 4) * 128], pb, pb[:, :], sgt, sgt[:, :], ALU.mult)

        tri = kb.sb([128, 2, 128], F32, "tri")
        mskb = kb.sb([128, 4, 512], BF16, "mskb")
        negA = kb.sb([128, 8], F32, "negA")
        dtb = kb.sb([128, 8], F32, "dtb")
        identrep = kb.sb([128, 512], F32, "identrep")
        imk = kb.sb([128, 4, 512], BF16, "imk")
        with ExitStack() as esG:
            old_es = kb.es
            kb.es = esG
            mst = kb.sb([128, 4, 512], F32, "mst")
            kb.dma("sp", tri[:, :, :], tri_d[:, :, :], writes=[tri])
            kb.dma("sp", mst[:, :, :], msk_d[:, :, :], writes=[mst])
            kb.cp("dve", mskb, mskb[:, :, :], mst, mst[:, :, :])
            kb.dma("sp", mst[:, :, :], imk_d[:, :, :], writes=[mst])
            kb.cp("dve", imk, imk[:, :, :], mst, mst[:, :, :])
            kb.dma("sp", negA[:, :], alog_d[:, :], writes=[negA])
            kb.dma("sp", dtb[:, :], dtb_d[:, :], writes=[dtb])
            kb.act(negA, negA[:, :], negA, negA[:, :], AF.Exp)
            kb.ts("dve", negA, negA[:, :], negA, negA[:, :], -1.0, None, ALU.mult)
            for h in range(4):
                kb.cp("dve", identrep, identrep[:, h * 128:(h + 1) * 128], ident, ident[:, :])
            barrier()
            kb.es = old_es

        def sm(name, w=4):
            return kb.sb([128, w], F32, name)

        eb, beta, xa, ab, en, lp, g_ = sm("eb"), sm("beta"), sm("xa"), sm("ab"), sm("en"), sm("lp"), sm("g_")
        gcs, e_, egl, dgl, el, ngc, negbeta = sm("gcs", 8), sm("e_"), sm("egl"), sm("dgl"), sm("el"), sm("ngc"), sm("negbeta")
        HW = 256

        class CH:
            def __init__(self, hp):
                n = f"c{hp}"
                self.hp = hp
                for nm in ("ssk", "rk", "c1", "c2", "ssq_", "rq", "cq", "cqe"):
                    setattr(self, nm, kb.sb([128, 2], F32, n + nm))
                self.junk2 = kb.sb([128, 128], BF16, n + "junk2")
                for nm in ("kn", "vb", "kbg", "kd", "qs", "qd", "knT", "qsT", "qdT", "Nb", "Zb", "Nk", "Zk", "Xb", "Yb",
                           "Cb", "Eb", "U1b", "U2b", "wT", "vnew", "intraT"):
                    setattr(self, nm, kb.sb([128, HW], BF16, n + nm))
                for nm in ("dg", "Ds", "DT", "u_"):
                    setattr(self, nm, kb.sb([128, HW], F32, n + nm))

        chains = [CH(0), CH(1)]
        SstC = [[kb.sb([128, HW], F32, f"SstC{d_}{hp}") for hp in range(2)] for d_ in range(2)]
        SbfC = [[kb.sb([128, HW], BF16, f"SbfC{d_}{hp}") for hp in range(2)] for d_ in range(2)]
        for d_ in range(2):
            for hp in range(2):
                kb.memset("dve", SstC[d_][hp], SstC[d_][hp][:, :], 0.0)
                kb.memset("dve", SbfC[d_][hp], SbfC[d_][hp][:, :], 0.0)

        def H(hh):
            return slice(hh * 128, (hh + 1) * 128)

        def tr4(dst, src):
            pb = kb.bank()
            pv = pb.t[:, :].bitcast(BF16)
            for hh in range(4):
                kb.tr(pb, pv[:, H(hh)], src, src[:, H(hh)], identb, identb[:, :])
            kb.cp("act", dst, dst[:, :], pb, pv[:, 0:512])

        def tr2(dst, src, e="act"):
            pb = kb.bank()
            pv = pb.t[:, :].bitcast(BF16)
            for lh in range(2):
                kb.tr(pb, pv[:, H(lh)], src, src[:, H(lh)], identb, identb[:, :])
            kb.cp(e, dst, dst[:, :], pb, pv[:, 0:HW])

        def rsq(dst, src, add):
            kb.act(dst, dst[:, :], src, src[:, :], AF.Ln, bias=add)
            kb.act(dst, dst[:, :], dst, dst[:, :], AF.Exp, scale=-0.5)

        def gdn_gates(ds, vis, own):
            c4 = slice(ds * 4, ds * 4 + 4)
            a4 = slice(8 + ds * 4, 8 + ds * 4 + 4)
            pg = project_tok(GATE, 16)
            fg = flg[:, ds, vis, 2:3]
            kb.act(eb, eb[:, :], pg, pg[:, c4], AF.Exp, scale=-1.0)
            kb.ts("dve", eb, eb[:, :], eb, eb[:, :], 1.0, None, ALU.add)
            kb.op("dve", lambda g: g.reciprocal(beta[:, :], eb[:, :]), reads=[eb], writes=[beta])
            kb.ts("dve", beta, beta[:, :], beta, beta[:, :], fg, None, ALU.mult, extra_reads=[flg])
            kb.ts("dve", negbeta, negbeta[:, :], beta, beta[:, :], -1.0, None, ALU.mult)
            kb.tt("dve", xa, xa[:, :], pg, pg[:, a4], dtb, dtb[:, c4], ALU.add)
            kb.act(ab, ab[:, :], xa, xa[:, :], AF.Abs)
            kb.act(en, en[:, :], ab, ab[:, :], AF.Exp, scale=-1.0)
            kb.act(lp, lp[:, :], en, en[:, :], AF.Ln, bias=1.0)
            kb.stt(g_, g_[:, :], xa, xa[:, :], 0.0, lp, lp[:, :], ALU.max, ALU.add)
            kb.tt("dve", g_, g_[:, :], g_, g_[:, :], negA, negA[:, c4], ALU.mult)
            kb.ts("dve", g_, g_[:, :], g_, g_[:, :], fg, None, ALU.mult, extra_reads=[flg])
            pgc = kb.bank()
            kb.mm(pgc, pgc[:, 0:4], tri, tri[:, ds, :], g_, g_[:, :])
            kb.mm(pgc, pgc[:, 4:8], onesf, onesf[:, :], g_, g_[:, :])
            kb.cp("act", gcs, gcs[:, :], pgc, pgc[:, 0:8])
            kb.act(e_, e_[:, :], gcs, gcs[:, 0:4], AF.Exp)
            kb.act(egl, egl[:, :], gcs, gcs[:, 4:8], AF.Exp)
            kb.tt("dve", dgl, dgl[:, :], gcs, gcs[:, 4:8], gcs, gcs[:, 0:4], ALU.subtract)
            kb.act(el, el[:, :], dgl, dgl[:, :], AF.Exp)
            if own:
                kb.ts("dve", ngc, ngc[:, :], gcs, gcs[:, 0:4], -1.0, None, ALU.mult)

        def gdn_chain(ds, vis, own, C, res):
            hp = C.hp
            S, Sb = SstC[ds][hp], SbfC[ds][hp]
            hs = (2 * hp, 2 * hp + 1)
            g2 = slice(2 * hp, 2 * hp + 2)
            mk = slice(0, HW)

            def mm2(l_b, r_b, acc_b=None):
                pb_ = kb.bank()
                for lh in range(2):
                    kb.mm(pb_, pb_[:, H(lh)], l_b, l_b[:, H(lh)], r_b, r_b[:, H(lh)], start=True, stop=(acc_b is None))
                    if acc_b is not None:
                        kb.mm(pb_, pb_[:, H(lh)], identb, identb[:, :], acc_b, acc_b[:, H(lh)], start=False, stop=True)
                return pb_

            for lh, hh in enumerate(hs):
                kb.act(C.junk2, C.junk2[:, :], cv, cv[:, H(hh)], AF.Square, accum=C.ssk[:, lh:lh + 1], extra_writes=[C.ssk])
            rsq(C.rk, C.ssk, EPS)
            yield
            kb.tt("dve", C.c1, C.c1[:, :], C.rk, C.rk[:, :], beta, beta[:, g2], ALU.mult)
            kb.tt("dve", C.c1, C.c1[:, :], C.c1, C.c1[:, :], e_, e_[:, g2], ALU.mult)
            kb.tt("dve", C.c2, C.c2[:, :], C.rk, C.rk[:, :], el, el[:, g2], ALU.mult)
            yield
            def scl(dst, dcol, src_col, sc_b, sc_ap):
                kb.act(dst, dst[:, H(dcol)], cv, cv[:, H(src_col)], AF.Identity, scale=sc_ap, extra_reads=[sc_b])

            for lh, hh in enumerate(hs):
                scl(C.kn, lh, hh, C.rk, C.rk[:, lh:lh + 1])
                scl(C.vb, lh, 4 + hh, beta, beta[:, hh:hh + 1])
                scl(C.kbg, lh, hh, C.c1, C.c1[:, lh:lh + 1])
                scl(C.kd, lh, hh, C.c2, C.c2[:, lh:lh + 1])
                yield
            tr2(C.knT, C.kn)
            yield
            if own:
                for lh, hh in enumerate(hs):
                    kb.act(C.junk2, C.junk2[:, :], cv, cv[:, H(8 + hh)], AF.Square, accum=C.ssq_[:, lh:lh + 1], extra_writes=[C.ssq_])
                rsq(C.rq, C.ssq_, EPS)
                kb.ts("dve", C.cq, C.cq[:, :], C.rq, C.rq[:, :], float(DK) ** -0.5, None, ALU.mult)
                kb.tt("dve", C.cqe, C.cqe[:, :], C.cq, C.cq[:, :], e_, e_[:, g2], ALU.mult)
                yield
                for lh, hh in enumerate(hs):
                    scl(C.qs, lh, 8 + hh, C.cq, C.cq[:, lh:lh + 1])
                    scl(C.qd, lh, 8 + hh, C.cqe, C.cqe[:, lh:lh + 1])
                yield
                tr2(C.qsT, C.qs)
                yield
                tr2(C.qdT, C.qd)
                yield
            pA = mm2(C.knT, C.knT)
            for lh, hh in enumerate(hs):
                kb.act(C.dg, C.dg[:, H(lh)], ident, ident[:, :], AF.Identity, scale=gcs[:, hh:hh + 1], extra_reads=[gcs])
            pG = kb.bank()
            kb.mm(pG, pG[:, 0:HW], onesf, onesf[:, :], C.dg, C.dg[:, :], start=True, stop=False)
            kb.mm(pG, pG[:, 0:HW], identb, identb[:, :], mskb, mskb[:, ds * 2, mk], start=False, stop=True)
            yield
            for lh, hh in enumerate(hs):
                kb.act(C.Ds, C.Ds[:, H(lh)], pG, pG[:, H(lh)], AF.Exp, bias=gcs[:, hh:hh + 1], scale=-1.0, extra_reads=[gcs])
            yield
            for lh, hh in enumerate(hs):
                kb.stt(C.Nb, C.Nb[:, H(lh)], pA, pA[:, H(lh)], negbeta[:, hh:hh + 1], C.Ds, C.Ds[:, H(lh)], ALU.mult, ALU.mult,
                       extra_reads=[negbeta])
            yield
            tr2(C.Zb, C.Nb)
            yield
            kb.tt("pool", C.Nk, C.Nk[:, :], C.Nb, C.Nb[:, :], imk, imk[:, 0, mk], ALU.mult)
            kb.tt("pool", C.Zk, C.Zk[:, :], C.Zb, C.Zb[:, :], imk, imk[:, 0, mk], ALU.mult)
            yield
            kb.tt("dve", C.Xb, C.Xb[:, :], C.Nk, C.Nk[:, :], identrep, identrep[:, mk], ALU.add)
            kb.tt("dve", C.Yb, C.Yb[:, :], C.Zk, C.Zk[:, :], identrep, identrep[:, mk], ALU.add)
            yield
            for lvl in range(1, 4):
                pN = mm2(C.Zk, C.Nk)
                pZ = mm2(C.Nk, C.Zk)
                yield
                kb.cp("act", C.Nk, C.Nk[:, :], pN, pN[:, 0:HW])
                kb.cp("dve", C.Zk, C.Zk[:, :], pZ, pZ[:, 0:HW])
                yield
                pX = mm2(C.Yb, C.Nk, C.Xb)
                pP = mm2(C.Xb, C.Zk, C.Yb)
                yield
                kb.cp("act", C.Xb, C.Xb[:, :], pX, pX[:, 0:HW])
                kb.cp("dve", C.Yb, C.Yb[:, :], pP, pP[:, 0:HW])
                yield
            for mi in (1, 2, 3):
                last = mi == 3
                kb.tt("pool", C.Cb, C.Cb[:, :], C.Nb, C.Nb[:, :], imk, imk[:, mi, mk], ALU.mult)
                if not last:
                    kb.tt("pool", C.Eb, C.Eb[:, :], C.Zb, C.Zb[:, :], imk, imk[:, mi, mk], ALU.mult)
                yield
                if not last:
                    pU1 = mm2(C.Eb, C.Xb)
                pU2 = mm2(C.Cb, C.Yb)
                yield
                if not last:
                    kb.cp("dve", C.U1b, C.U1b[:, :], pU1, pU1[:, 0:HW])
                kb.cp("act", C.U2b, C.U2b[:, :], pU2, pU2[:, 0:HW])
                yield
                if not last:
                    pX = mm2(C.Yb, C.U1b, C.Xb)
                pP = mm2(C.Xb, C.U2b, C.Yb)
                yield
                if not last:
                    kb.cp("act", C.Xb, C.Xb[:, :], pX, pX[:, 0:HW])
                kb.cp("dve", C.Yb, C.Yb[:, :], pP, pP[:, 0:HW])
                yield
            pU = mm2(C.Yb, C.vb)
            pW = mm2(C.kbg, C.Yb)
            yield
            kb.cp("act", C.u_, C.u_[:, :], pU, pU[:, 0:HW])
            kb.cp("dve", C.wT, C.wT[:, :], pW, pW[:, 0:HW])
            yield
            pWS = mm2(C.wT, Sb)
            yield
            kb.tt("dve", C.vnew, C.vnew[:, :], C.u_, C.u_[:, :], pWS, pWS[:, 0:HW], ALU.subtract)
            yield
            if own:
                pGI = kb.bank()
                kb.mm(pGI, pGI[:, 0:HW], onesf, onesf[:, :], C.dg, C.dg[:, :], start=True, stop=False)
                kb.mm(pGI, pGI[:, 0:HW], identb, identb[:, :], mskb, mskb[:, ds * 2 + 1, mk], start=False, stop=True)
                pQK = mm2(C.knT, C.qsT)
                yield
                for lh, hh in enumerate(hs):
                    kb.act(C.DT, C.DT[:, H(lh)], pGI, pGI[:, H(lh)], AF.Exp, bias=ngc[:, hh:hh + 1], extra_reads=[ngc])
                yield
                kb.tt("dve", C.intraT, C.intraT[:, :], pQK, pQK[:, 0:HW], C.DT, C.DT[:, :], ALU.mult)
                yield
                pO = kb.reserve()
                for lh in range(2):
                    kb.mm(pO, pO[:, H(lh)], C.qdT, C.qdT[:, H(lh)], Sb, Sb[:, H(lh)], start=True, stop=False)
                    kb.mm(pO, pO[:, H(lh)], C.intraT, C.intraT[:, H(lh)], C.vnew, C.vnew[:, H(lh)], start=False, stop=True)
                res[hp] = pO
                yield
            pS = mm2(C.kd, C.vnew)
            yield
            for lh, hh in enumerate(hs):
                kb.stt(S, S[:, H(lh)], S, S[:, H(lh)], egl[:, hh:hh + 1], pS, pS[:, H(lh)], ALU.mult, ALU.add, extra_reads=[egl])
            kb.cp("act", Sb, Sb[:, :], S, S[:, :])
            yield

        def gdn_step(ds, vis, own):
            gdn_gates(ds, vis, own)
            res = [None, None]
            gens = [gdn_chain(ds, vis, own, chains[0], res), gdn_chain(ds, vis, own, chains[1], res)]
            while gens:
                for g in list(gens):
                    try:
                        next(g)
                    except StopIteration:
                        gens.remove(g)
            return res if own else None

        qkw = kb.sb([128, 640], F32, "qkw")
        dnw = kb.sb([128, 512], F32, "dnw")
        kb.dma("sp", qkw[:, :], qkw_d[:, :], writes=[qkw])
        kb.dma("sp", dnw[:, :], dnw_d[:, :], writes=[dnw])
        rp = kb.sb([128, 1280], F32, "rp")
        sqt = kb.sb([128, 640], F32, "sqt")
        ss10 = kb.sb([128, 10], F32, "ss10")
        rr10 = kb.sb([128, 10], F32, "rr10")
        xw = kb.sb([128, 640], F32, "xw")
        sw = kb.sb([128, 640], F32, "sw")
        t1 = kb.sb([128, 640], F32, "t1")
        qkb = kb.sb([128, 640], BF16, "qkb")
        kTt = kb.sb([128, 128], BF16, "kTt")
        qTt = kb.sb([128, 512], BF16, "qTt")
        vt = kb.sb([128, 130], BF16, "vt")
        kb.memset("dve", vt, vt[:, :], 1.0)

        def attn_proj(vis, own, m):
            kb.dma("sp", rp[:, :], rope_d[vis, :, :], writes=[rp])
            c0 = 0 if own else 512
            nh0 = 0 if own else 8
            pKV = project_tok(512, 256)
            pQ = project_tok(0, 512) if own else None
            if own:
                kb.act(sqt, sqt[:, 0:512], pQ, pQ[:, :], AF.Square)
            kb.act(sqt, sqt[:, 512:640], pKV, pKV[:, 0:128], AF.Square)
            kb.op("dve", lambda g: g.tensor_reduce(ss10[:, nh0:10], sqt[:, c0:640].rearrange("p (h d) -> p h d", d=64), AX.X, ALU.add),
                  reads=[sqt], writes=[ss10])
            kb.act(rr10, rr10[:, nh0:10], ss10, ss10[:, nh0:10], AF.Ln, bias=EPS, scale=1.0 / 64)
            kb.act(rr10, rr10[:, nh0:10], rr10, rr10[:, nh0:10], AF.Exp, scale=-0.5)
            for j in range(nh0, 10):
                src_b, src_ap = (pQ, pQ[:, j * 64:(j + 1) * 64]) if j < 8 else (pKV, pKV[:, (j - 8) * 64:(j - 7) * 64])
                kb.ts("dve", xw, xw[:, j * 64:(j + 1) * 64], src_b, src_ap, rr10[:, j:j + 1], None, ALU.mult, extra_reads=[rr10])
            kb.tt("dve", xw, xw[:, c0:640], xw, xw[:, c0:640], qkw, qkw[:, c0:640], ALU.mult)
            xv = xw[:, c0:640].rearrange("p (n two s) -> p n two s", two=2, s=16)
            sv = sw[:, c0:640].rearrange("p (n two s) -> p n two s", two=2, s=16)
            kb.cp("pool", sw, sv[:, :, 0, :], xw, xv[:, :, 1, :])
            kb.cp("pool", sw, sv[:, :, 1, :], xw, xv[:, :, 0, :])
            kb.tt("dve", t1, t1[:, c0:640], xw, xw[:, c0:640], rp, rp[:, c0:640], ALU.mult)
            kb.tt("dve", sw, sw[:, c0:640], sw, sw[:, c0:640], rp, rp[:, 640 + c0:1280], ALU.mult)
            kb.tt("dve", qkb, qkb[:, c0:640], t1, t1[:, c0:640], sw, sw[:, c0:640], ALU.add)
            pb = kb.bank()
            pv = pb.t[:, :].bitcast(BF16)
            kb.tr(pb, pv[:, 0:128], qkb, qkb[:, 512:640], identb, identb[:, :])
            kb.cp("act", kTt, kTt[:, :], pb, pv[:, 0:128])
            kb.dma("sp", kT_d[:, vis * 128:(vis + 1) * 128], kTt[:, :], reads=[kTt])
            kb.cp("act", vt, vt[:, :].rearrange("p (k c) -> p k c", k=2)[:, :, 0:64], pKV,
                  pKV[:, 128:256].rearrange("p (k c) -> p k c", k=2))
            kb.dma("sp", v_d[vis, :, :], vt[:, :], reads=[vt])
            if own:
                tr4(qTt, qkb)
                for pr in range(4):
                    kb.dma("sp", qT_d[pr, :, m * 128:(m + 1) * 128], qTt[:, H(pr)], reads=[qTt])

        o1t = kb.sb([128, 512], F32, "o1t")
        junk2 = kb.sb([128, 128], BF16, "junk2o")
        o1r = kb.sb([128, 512], F32, "o1r")
        osum = kb.sb([128, 512], F32, "osum")
        ssd, rsd = sm("ssd"), sm("rsd")
        zs = kb.sb([128, 512], F32, "zs")
        od = kb.sb([128, 512], BF16, "od")
        odT = kb.sb([128, 512], BF16, "odT")

        def own_out_scan1(pOs, m):
            for hp in range(2):
                kb.cp("act" if hp == 0 else "dve", o1t, o1t[:, hp * HW:(hp + 1) * HW], pOs[hp], pOs[hp][:, 0:HW])
                kb.release(pOs[hp])
            kb.dma("sp", o1_d[m, :, :], o1t[:, :], reads=[o1t])

        def own_out_scan2(pOs, m):
            kb._wait("sp", o1t.r.get(("dma", o1t.name)))
            kb.dma("sp", o1r[:, :], o1_d[m, :, :], writes=[o1r])
            for hp in range(2):
                kb.tt("dve", osum, osum[:, hp * HW:(hp + 1) * HW], o1r, o1r[:, hp * HW:(hp + 1) * HW], pOs[hp], pOs[hp][:, 0:HW], ALU.add)
                kb.release(pOs[hp])
            for hh in range(4):
                kb.act(junk2, junk2[:, :], osum, osum[:, H(hh)], AF.Square, accum=ssd[:, hh:hh + 1], extra_writes=[ssd])
            kb.act(rsd, rsd[:, :], ssd, ssd[:, :], AF.Ln, bias=EPS, scale=1.0 / 128)
            kb.act(rsd, rsd[:, :], rsd, rsd[:, :], AF.Exp, scale=-0.5)
            pZ = project_tok(ZOFF, 512)
            kb.act(zs, zs[:, :], pZ, pZ[:, :], AF.Exp, scale=-1.0)
            kb.ts("dve", zs, zs[:, :], zs, zs[:, :], 1.0, None, ALU.add)
            kb.op("dve", lambda g: g.reciprocal(zs[:, :], zs[:, :]), reads=[zs], writes=[zs])
            kb.tt("dve", zs, zs[:, :], pZ, pZ[:, :], zs, zs[:, :], ALU.mult)
            kb.tt("dve", zs, zs[:, :], zs, zs[:, :], dnw, dnw[:, :], ALU.mult)
            for hh in range(4):
                kb.stt(od, od[:, H(hh)], osum, osum[:, H(hh)], rsd[:, hh:hh + 1], zs, zs[:, H(hh)], ALU.mult, ALU.mult,
                       extra_reads=[rsd])
            tr4(odT, od)
            kb.dma("sp", mdn_d[m, :, :], odT[:, :], reads=[odT])

        nown, pre1, pre2 = CFG["own"], CFG["pre1"], CFG["pre2"]
        o1_store_dep = {}

        def visit(ds, vis):
            npre = pre1 if ds == 0 else pre2
            isctx = vis < 2
            own = vis >= 2 + npre
            sidx = vis - 2 - npre
            norm_tile(ds, vis, xt[(ds + vis) % 2], S1c if isctx else S1, H1c if isctx else H1)
            if ds == 1:
                attn_proj(vis, own, sidx if own else 0)
            gdn_pre_conv(ds, vis, 12 if own else 8)
            pO = gdn_step(ds, vis, own)
            if own and ds == 0:
                own_out_scan1(pO, nown - 1 - sidx)
            if own and ds == 1:
                own_out_scan2(pO, sidx)

        v1 = 0
        n2_head = 2 + pre2
        for v2 in range(N2):
            if v2 == n2_head:
                while v1 < N1:
                    visit(0, v1)
                    v1 += 1
            visit(1, v2)
            while v1 < N1 and v1 * n2_head < (v2 + 1) * N1 and v2 < n2_head:
                visit(0, v1)
                v1 += 1
        barrier()
        for e in ("sp",):
            for bb in (kTt, vt, qTt, odT, o1t):
                for dd in list(bb.r.values()):
                    kb._wait(e, dd)

        if DEBUG:
            dbgM = nc.dram_tensor("dbgM", [nown, 128, 512], BF16, kind="ExternalOutput").ap()
            dbgK = nc.dram_tensor("dbgK", [128, N2 * 128], BF16, kind="ExternalOutput").ap()
            dbgV = nc.dram_tensor("dbgV", [N2, 128, 130], BF16, kind="ExternalOutput").ap()
            dbgQ = nc.dram_tensor("dbgQ", [4, 128, nown * 128], BF16, kind="ExternalOutput").ap()
            tb = kb.sb([128, N2 * 128], BF16, "dbgtb")
            for src, dst, shp in ((kT_d[:, :], dbgK[:, :], None),):
                kb.dma("sp", tb[:, :], src, writes=[tb])
                dd = kb.dma("sp", dst, tb[:, :], reads=[tb]); kb._wait("sp", dd)
            for m in range(nown):
                kb.dma("sp", tb[:, 0:512], mdn_d[m, :, :], writes=[tb])
                dd = kb.dma("sp", dbgM[m, :, :], tb[:, 0:512], reads=[tb]); kb._wait("sp", dd)
            for v in range(N2):
                kb.dma("sp", tb[:, 0:130], v_d[v, :, :], writes=[tb])
                dd = kb.dma("sp", dbgV[v, :, :], tb[:, 0:130], reads=[tb]); kb._wait("sp", dd)
            for pr in range(4):
                kb.dma("sp", tb[:, 0:nown * 128], qT_d[pr, :, :], writes=[tb])
                dd = kb.dma("sp", dbgQ[pr, :, :], tb[:, 0:nown * 128], reads=[tb]); kb._wait("sp", dd)
        barrier()
        esB.close()
        kb.es = es

        nq = nown * 128
        NK = N2
        esC = ExitStack()
        kb.es = esC
        KTd = [kb.sb([128, NK * 128], BF16, f"KTd{i}") for i in range(2)]
        Vs = kb.sb([128, NK, 130], BF16, "Vs")
        QTs = kb.sb([128, 4, nq], BF16, "QTs")
        for kv in range(2):
            kb.dma("sp", KTd[kv][0:64, :], kT_d[kv * 64:(kv + 1) * 64, :], writes=[KTd[kv]])
            kb.dma("sp", KTd[kv][64:128, :], kT_d[kv * 64:(kv + 1) * 64, :], writes=[KTd[kv]])
        kb.dma("sp", Vs[:, :, :], v_d.rearrange("v p c -> p v c"), writes=[Vs])
        for pr in range(4):
            kb.dma("sp", QTs[:, pr, :], qT_d[pr, :, :], writes=[QTs])
        QB = min(512, nq)
        PTs = [kb.sb([128, QB], BF16, f"PT{i}") for i in range(4)]
        osb = kb.sb([65, QB], F32, "osb")
        rc = kb.sb([65, QB], F32, "rc")
        oab = kb.sb([64, QB], BF16, "oab")
        pti = 0
        for q0 in range(0, nq, QB):
            for hh in range(8):
                pr, half, kv = hh // 2, hh % 2, hh // 4
                ps_ = slice(half * 64, half * 64 + 64)
                pO = kb.reserve()

                def qk(kt):
                    pS_ = kb.bank()
                    kb.mm(pS_, pS_[:, 0:QB], KTd[kv], KTd[kv][ps_, kt * 128:(kt + 1) * 128], QTs, QTs[ps_, pr, q0:q0 + QB])
                    return pS_

                LOOK = 2
                pend = [qk(kt) for kt in range(min(LOOK, NK))]
                for kt in range(NK):
                    pS = pend.pop(0)
                    PT = PTs[pti % len(PTs)]
                    pti += 1
                    kb.act(PT, PT[:, :], pS, pS[:, 0:QB], AF.Exp, scale=0.125)
                    if kt + LOOK < NK:
                        pend.append(qk(kt + LOOK))
                    kb.mm(pO, pO[0:65, 0:QB], Vs, Vs[:, kt, kv * 65:(kv + 1) * 65], PT, PT[:, :], start=(kt == 0), stop=(kt == NK - 1))
                kb.cp("act", osb, osb[:, :], pO, pO[0:65, 0:QB])
                kb.release(pO)
                kb.op("dve", lambda g: g.reciprocal(rc[64:65, :], osb[64:65, :]), reads=[osb], writes=[rc])
                pB = kb.bank()
                kb.mm(pB, pB[0:64, 0:QB], onesf, onesf[64:65, 0:64], rc, rc[64:65, :])
                kb.tt("dve", oab, oab[:, :], osb, osb[0:64, :], pB, pB[0:64, 0:QB], ALU.mult)
                kb.dma("sp", oa_d[:, hh, q0:q0 + QB], oab[:, :], reads=[oab])
        barrier()
        for dd in list(oab.r.values()):
            kb._wait("sp", dd)
        esC.close()
        kb.es = es

        esD = ExitStack()
        kb.es = esD
        G1 = kb.sb([128, D], F32, "G1")
        S2 = kb.sb([128, D], F32, "S2")
        H2 = kb.sb([128, D], F32, "H2")
        G2 = kb.sb([128, D], F32, "G2")
        for i_, bb in enumerate((G1, S2, H2, G2)):
            kb.dma("sp", bb[:, :], mod_d[i_, :, :], writes=[bb])
        w1b = kb.sb([128, 8, 4 * D], BF16, "w1b")
        woab = kb.sb([64, 8, D], BF16, "woab")
        wodb = kb.sb([128, 4, D], BF16, "wodb")
        esW2 = ExitStack()
        kb.es = esW2
        wst2 = [kb.sb([128, 4096], F32, f"wst2_{i}") for i in range(2)]
        w2c = kb.sb([128, 4096], BF16, "w2c")
        ci = 0

        def conv_chunk(dst_b, dst_ap, src_ap, np_=128):
            nonlocal ci
            st = wst2[ci % 2]
            e = "pool" if ci % 2 == 0 else "dve"
            ci += 1
            kb.dma("sp", st[0:np_, :], src_ap, writes=[st])
            kb.cp(e, dst_b, dst_ap, st, st[0:np_, :])

        for k in range(8):
            conv_chunk(w1b, w1b[:, k, :], w1_d[:, k, :])
        for c in range(2):
            conv_chunk(woab, woab[:, c * 4:(c + 1) * 4, :].rearrange("p a b -> p (a b)"), woa_d[:, c * 4:(c + 1) * 4, :].rearrange("p a b -> p (a b)"), 64)
        conv_chunk(wodb, wodb[:, :, :].rearrange("p a b -> p (a b)"), wod_d.rearrange("p a b -> p (a b)"))
        for c in range(8):
            conv_chunk(w2c, w2c[:, :], w2_d[:, c * 4:(c + 1) * 4, :].rearrange("p a b -> p (a b)"))
            kb.dma("sp", w2b_d[:, c * 4:(c + 1) * 4, :].rearrange("p a b -> p (a b)"), w2c[:, :], reads=[w2c])
        for dd in list(w2c.r.values()):
            kb._wait("sp", dd)
        barrier()
        esW2.close()
        kb.es = esD

        TB = min(2, nown)
        NB = TB * 128
        oat = kb.sb([64, 8, NB], BF16, "oat")
        mdt = kb.sb([128, TB, 512], BF16, "mdt")
        xo = kb.sb([128, D], F32, "xo")
        x1b = kb.sb([128, TB, D], F32, "x1b")
        junkD = kb.sb([128, D], BF16, "junkD")
        ssqD = kb.sb([128, 1], F32, "ssqD")
        rstdD = kb.sb([128, 1], F32, "rstdD")
        t32D = kb.sb([128, D], F32, "t32D")
        hbD = kb.sb([128, D], BF16, "hbD")
        h2T = kb.sb([128, 8, NB], BF16, "h2T")
        w2s = [kb.sb([128, 4, D], BF16, f"w2s{i}") for i in range(2)]
        rls = [kb.sb([128, NB], F32, f"rl{i}") for i in range(2)]
        aT = [kb.sb([128, NB], BF16, f"aT{i}") for i in range(3)]
        yo = kb.sb([128, D], F32, "yo")
        ai = 0
        out_deps = []
        for blk0 in range(0, nown, TB):
            t0 = blk0 * 128
            kb.dma("sp", oat[:, :, :], oa_d[:, :, t0:t0 + NB], writes=[oat])
            for t in range(TB):
                kb.dma("sp", mdt[:, t, :], mdn_d[blk0 + t, :, :], writes=[mdt])
            for t in range(TB):
                m = blk0 + t
                kb.dma("sp", xo[:, :], xown_d[m, :, :], writes=[xo])
                for half in range(2):
                    cs = slice(half * 512, half * 512 + 512)
                    pM = kb.bank()
                    for hh in range(8):
                        kb.mm(pM, pM[:, :], oat, oat[:, hh, t * 128:(t + 1) * 128], woab, woab[:, hh, cs], start=(hh == 0), stop=False)
                    for hd in range(4):
                        kb.mm(pM, pM[:, :], mdt, mdt[:, t, hd * 128:(hd + 1) * 128], wodb, wodb[:, hd, cs], start=False, stop=(hd == 3))
                    kb.tt("dve", x1b, x1b[:, t, cs], pM, pM[:, :], G1, G1[:, cs], ALU.mult)
                kb.tt("dve", x1b, x1b[:, t, :], x1b, x1b[:, t, :], xo, xo[:, :], ALU.add)
                kb.act(junkD, junkD[:, :], x1b, x1b[:, t, :], AF.Square, accum=ssqD[:, :], extra_writes=[ssqD])
                kb.act(rstdD, rstdD[:, :], ssqD, ssqD[:, :], AF.Ln, bias=EPS, scale=1.0 / D)
                kb.act(rstdD, rstdD[:, :], rstdD, rstdD[:, :], AF.Exp, scale=-0.5)
                kb.stt(t32D, t32D[:, :], x1b, x1b[:, t, :], rstdD[:, 0:1], S2, S2[:, :], ALU.mult, ALU.mult, extra_reads=[rstdD])
                kb.tt("dve", hbD, hbD[:, :], t32D, t32D[:, :], H2, H2[:, :], ALU.add)
                pb = kb.bank()
                pbv = pb.t[:, :].bitcast(BF16)
                for k in range(8):
                    kb.tr(pb, pbv[:, k * 128:(k + 1) * 128], hbD, hbD[:, k * 128:(k + 1) * 128], identb, identb[:, :])
                kb.cp("act", h2T, h2T[:, :, t * 128:(t + 1) * 128], pb, pbv.rearrange("p (k t) -> p k t", k=8))
            acc = [[kb.reserve() for _ in range(2)] for _ in range(TB)]
            def mlp1(f):
                pA1_ = kb.bank()
                for k in range(8):
                    kb.mm(pA1_, pA1_[:, 0:NB], w1b, w1b[:, k, f * 128:(f + 1) * 128], h2T, h2T[:, k, :], start=(k == 0), stop=(k == 7))
                return pA1_

            pendA = [mlp1(0)]
            for f in range(32):
                fg, fj = f // 4, f % 4
                ws = w2s[fg % 2]
                if fj == 0:
                    kb.dma("sp", ws[:, :, :], w2b_d[:, fg * 4:(fg + 1) * 4, :], writes=[ws])
                pA1 = pendA.pop(0)
                r_ = rls[f % 2]
                kb.act(r_, r_[:, :], pA1, pA1[:, 0:NB], AF.Relu)
                a_ = aT[ai % 3]
                ai += 1
                kb.tt("dve", a_, a_[:, :], r_, r_[:, :], r_, r_[:, :], ALU.mult)
                if f + 1 < 32:
                    pendA.append(mlp1(f + 1))
                for t in range(TB):
                    for half in range(2):
                        kb.mm(acc[t][half], acc[t][half][:, :], a_, a_[:, t * 128:(t + 1) * 128], ws, ws[:, fj, half * 512:half * 512 + 512],
                              start=(f == 0), stop=(f == 31))
            for t in range(TB):
                m = blk0 + t
                for half in range(2):
                    cs = slice(half * 512, half * 512 + 512)
                    kb.tt("dve", yo, yo[:, cs], acc[t][half], acc[t][half][:, :], G2, G2[:, cs], ALU.mult)
                    kb.release(acc[t][half])
                kb.tt("dve", yo, yo[:, :], yo, yo[:, :], x1b, x1b[:, t, :], ALU.add)
                out_deps.append(kb.dma("sp", out_d[m, :, :], yo[:, :], reads=[yo]))
        for dd in out_deps:
            kb._wait("sp", dd)
        for dd in list(yo.r.values()):
            kb._wait("sp", dd)
        barrier()
        esD.close()
        kb.es = es
        return nc, kb, es


def _rep(v, n=128):
    v = np.asarray(v, np.float32)
    return np.ascontiguousarray(np.broadcast_to(v[None, :], (n, v.shape[0])))


def _win_perm(dirs=(0, 1)):
    perm = list(range(0, 768))
    for base in (1280, 1792, 768):
        for h in range(4):
            perm += list(range(base + h * 128, base + (h + 1) * 128))
    perm += list(range(2304, 2816))
    for base in (2816, 2824):
        for dr in dirs:
            perm += [base + dr * 4 + h for h in range(4)]
    return np.array(perm)


def _halo_rows(seq, t0):
    out = np.zeros((4, seq.shape[1]), np.float32)
    fl = 1.0 if t0 - 2 >= 0 else 0.0
    fr = 1.0 if t0 + 130 <= seq.shape[0] else 0.0
    if fl:
        out[0:2] = seq[t0 - 2:t0]
    if fr:
        out[2:4] = seq[t0 + 128:t0 + 130]
    return out, fl, fr


def _rope_table(tile):
    out = np.zeros((128, 1280), np.float32)
    if tile is None:
        out[:, 0:640] = 1.0
        return out
    t = tile * 128 + np.arange(128)
    pos = [(t // 64).astype(np.float32), (t % 64).astype(np.float32)]
    inv = (np.float32(10000.0) ** (-np.arange(16, dtype=np.float32) / np.float32(16))).astype(np.float32)
    C = np.zeros((128, 64), np.float32)
    S = np.zeros((128, 64), np.float32)
    for blk in range(2):
        ang = (pos[blk][:, None] * inv[None, :]).astype(np.float32)
        c, sn = np.cos(ang).astype(np.float32), np.sin(ang).astype(np.float32)
        C[:, blk * 32:blk * 32 + 16] = c
        C[:, blk * 32 + 16:blk * 32 + 32] = c
        S[:, blk * 32:blk * 32 + 16] = -sn
        S[:, blk * 32 + 16:blk * 32 + 32] = sn
    out[:, 0:640] = np.tile(C, (1, 10))
    out[:, 640:1280] = np.tile(S, (1, 10))
    return out


def _const_tables(dirs):
    p = np.arange(128)[:, None]
    f = np.arange(128)[None, :]
    tri = np.zeros((128, 2, 128), np.float32)
    masks = np.zeros((128, 4, 512), np.float32)
    for ds, dr in enumerate(dirs):
        if dr == 0:
            tri[:, ds, :] = (p <= f)
            okS, okIT = (p > f), (f >= p)
        else:
            tri[:, ds, :] = (p >= f)
            okS, okIT = (p < f), (f <= p)
        masks[:, ds * 2 + 0, :] = np.tile(np.where(okS, 0.0, BIG), (1, 4))
        masks[:, ds * 2 + 1, :] = np.tile(np.where(okIT, 0.0, -BIG), (1, 4))
    im = np.zeros((128, 4, 512), np.float32)
    im[:, 0, :] = np.tile((p // 16) == (f // 16), (1, 4))
    for mi, sz in ((1, 16), (2, 32), (3, 64)):
        same = (p // (2 * sz)) == (f // (2 * sz))
        im[:, mi, :] = np.tile(same & (((p % (2 * sz)) >= sz) != ((f % (2 * sz)) >= sz)), (1, 4))
    return tri, masks, im


def _plan(i):
    own_lo = 16 * i
    fwd_pre = list(range(0, own_lo))
    bwd_pre = list(range(63, own_lo + 15, -1))
    fwd_own = list(range(own_lo, own_lo + 16))
    bwd_own = fwd_own[::-1]
    fwd_ctx, bwd_ctx = [0, 1], [1, 0]
    if i < 2:
        dirs = (0, 1)
        s1_ctx, s1_pre, s1_own = fwd_ctx, fwd_pre, fwd_own
        s2_ctx, s2_pre, s2_own = bwd_ctx, bwd_pre, bwd_own
    else:
        dirs = (1, 0)
        s1_ctx, s1_pre, s1_own = bwd_ctx, bwd_pre, bwd_own
        s2_ctx, s2_pre, s2_own = fwd_ctx, fwd_pre, fwd_own
    scan1 = [("ctx", t, 1.0) for t in s1_ctx]
    scan1 += [("lat", 0, 0.0)] * (16 - len(s1_pre)) + [("lat", t, 1.0) for t in s1_pre]
    scan1 += [("lat", t, 1.0) for t in s1_own]
    scan2 = [("ctx", t, 1.0) for t in s2_ctx]
    scan2 += [("lat", t, 0.0) for t in s1_pre] if len(s2_pre) < 48 else []
    scan2 += [("lat", t, 1.0) for t in s2_pre]
    scan2 += [("lat", t, 1.0) for t in s2_own]
    assert len(scan1) == N1 and len(scan2) == N2, (len(scan1), len(scan2))
    assert sorted(t for k, t, _ in scan2 if k == "lat") == list(range(64))
    return dirs, scan1, scan2, s2_own


def kernel(**inputs):
    set_cfg(16, 48, 16)
    g = lambda k: np.asarray(inputs[k], np.float32)
    x, c, ctx, c_ctx = g("x"), g("c"), g("ctx"), g("c_ctx")
    w_mod, w_in, conv_w = g("w_mod")[0], g("w_in")[0], g("conv_w")[0]
    w_out, w1, w2 = g("w_out")[0], g("w_mlp1")[0], g("w_mlp2")[0]
    a_log, dt_bias = g("a_log")[0], g("dt_bias")[0]

    shared = {
        "wmod": np.ascontiguousarray(w_mod.reshape(8, 128, 6, D).transpose(2, 1, 0, 3).reshape(6, 128, 8 * D)),
        "bmod": _rep(g("b_mod")[0]), "n1w": _rep(g("norm1_w")[0]), "n2w": _rep(g("norm2_w")[0]),
        "qkw": _rep(np.concatenate([np.tile(g("q_norm_w")[0], 8), np.tile(g("k_norm_w")[0], 2)])),
        "dnw": _rep(np.tile(g("dn_norm_w")[0], 4)),
        "woa": np.ascontiguousarray(w_out[0:512].reshape(8, 64, D).transpose(1, 0, 2)),
        "wod": np.ascontiguousarray(w_out[512:1024].reshape(4, 128, D).transpose(1, 0, 2)),
        "w1": np.ascontiguousarray(w1.reshape(8, 128, 4 * D).transpose(1, 0, 2)),
        "w2": np.ascontiguousarray(w2.reshape(32, 128, D).transpose(1, 0, 2)),
        "ident": np.eye(128, dtype=np.float32),
    }
    cw = np.zeros((128, 12, 5), np.float32)
    blk = 0
    for base in (512, 1024, 0):
        for h in range(4):
            cw[:, blk, :] = conv_w[:, base + h * 128:base + (h + 1) * 128].T
            blk += 1
    shared["convw"] = np.ascontiguousarray(cw.reshape(128, 60))
    per_dirs = {}
    for dirs in ((0, 1), (1, 0)):
        tri, masks, im = _const_tables(dirs)
        per_dirs[dirs] = {
            "tri": tri, "masks": masks, "imask": im,
            "win": np.ascontiguousarray(w_in[:, _win_perm(dirs)].reshape(8, 128, NCOL).transpose(1, 0, 2)),
            "alog": _rep(np.concatenate([a_log[dirs[0]], a_log[dirs[1]]])),
            "dtb": _rep(np.concatenate([dt_bias[dirs[0]], dt_bias[dirs[1]]])),
        }
    rope_cache = {t: _rope_table(t) for t in list(range(64)) + [None]}

    in_maps, own_maps = [], []
    for core in range(8):
        b, i = core // 4, core % 4
        dirs, scan1, scan2, own_tiles = _plan(i)
        feed = dict(shared)
        feed.update(per_dirs[dirs])
        feed["cc"] = np.ascontiguousarray(np.concatenate([c[b].reshape(8, 128).T, c_ctx.reshape(8, 128).T], axis=1))
        flags = np.zeros((128, 2, N2, 4), np.float32)
        seqs = {"ctx": ctx[b], "lat": x[b]}
        for ds, (scan, n) in enumerate(((scan1, N1), (scan2, N2))):
            xs = np.zeros((n, 128, D), np.float32)
            xh = np.zeros((n, 4, D), np.float32)
            for vis, (kind, t, gflag) in enumerate(scan):
                seq = seqs[kind]
                xs[vis] = seq[t * 128:(t + 1) * 128]
                xh[vis], fl, fr = _halo_rows(seq, t * 128)
                flags[:, ds, vis, 0:3] = (fl, fr, gflag)
            feed["xs%d" % (ds + 1)] = xs
            feed["xh%d" % (ds + 1)] = xh
        feed["flags"] = flags
        feed["rope2"] = np.stack([rope_cache[None if kind == "ctx" else t] for kind, t, _ in scan2])
        feed["xown"] = np.stack([x[b, t * 128:(t + 1) * 128] for t in own_tiles])
        in_maps.append(feed)
        own_maps.append((b, own_tiles))

    nc, kb, _ = build_program()
    res = run_bass_kernel_spmd(nc, in_maps, core_ids=list(range(8)))
    out = np.zeros((2, T, D), np.float32)
    for core, (b, own_tiles) in enumerate(own_maps):
        o = np.asarray(res.results[core]["out"], np.float32)
        for m, t in enumerate(own_tiles):
            out[b, t * 128:(t + 1) * 128] = o[m]
    return out
```

```python
import numpy as np
from contextlib import ExitStack
import concourse.bass as bass
import concourse.mybir as mybir
from concourse.bass_utils import run_bass_kernel_spmd

F32 = mybir.dt.float32
BF16 = mybir.dt.bfloat16
AF = mybir.ActivationFunctionType
ALU = mybir.AluOpType
AX = mybir.AxisListType

D = 1024
T = 8192
NCTX = 256
NT = T // 128
QT = 16
EPS = 1e-6
BIG = 1.0e9
DEBUG = False
SAME_ENGINE_SYNC = True


class Buf:
    __slots__ = ("t", "name", "w", "r", "dsem", "dcnt")

    def __init__(self, t, name):
        self.t = t
        self.name = name
        self.w = None
        self.r = {}
        self.dsem = None
        self.dcnt = 0

    def __getitem__(self, k):
        return self.t[k]


class KB:
    def __init__(self, nc, es):
        self.nc = nc
        self.es = es
        self.eng = {"pe": nc.tensor, "act": nc.scalar, "dve": nc.vector, "pool": nc.gpsimd, "sp": nc.sync}
        self.sem = {e: es.enter_context(nc.semaphore("sem_" + e)) for e in ("pe", "act", "dve", "pool")}
        self.cnt = {e: 0 for e in self.sem}
        self.seen = {e: {} for e in self.eng}
        self.nbuf = 0
        self.banks = []
        self.bank_i = 0
        self.ninst = 0
        self.reserved = set()

    def sb(self, shape, dtype, name=None):
        self.nbuf += 1
        name = "s_" + (name or f"b{self.nbuf}")
        return Buf(self.es.enter_context(self.nc.sbuf_tensor(name, list(shape), dtype)), name)

    def init_psum(self):
        for i in range(8):
            t = self.es.enter_context(self.nc.psum_tensor(f"psb{i}", [128, 512], F32))
            self.banks.append(Buf(t, f"psb{i}"))

    def bank(self):
        while True:
            b = self.banks[self.bank_i % 8]
            self.bank_i += 1
            if b.name not in self.reserved:
                return b

    def reserve(self):
        b = self.bank()
        self.reserved.add(b.name)
        return b

    def release(self, b):
        self.reserved.discard(b.name)

    def _wait(self, e, dep):
        if dep is None:
            return
        kind = dep[0]
        if kind == "dma":
            key = ("dma", dep[3])
            semh = dep[1]
            val = dep[2]
        else:
            if kind == e and (e == "pe" or not SAME_ENGINE_SYNC):
                return
            key = kind
            semh = self.sem[kind]
            val = dep[1]
        if self.seen[e].get(key, 0) >= val:
            return
        self.eng[e].wait_ge(semh, val)
        self.seen[e][key] = val

    def _deps(self, e, reads, writes):
        for b in reads:
            self._wait(e, b.w)
        for b in writes:
            self._wait(e, b.w)
            for d in list(b.r.values()):
                self._wait(e, d)

    def op(self, e, fn, reads=(), writes=()):
        self._deps(e, reads, writes)
        ins = fn(self.eng[e])
        self.cnt[e] += 1
        self.ninst += 1
        ins.then_inc(self.sem[e], 1)
        dep = (e, self.cnt[e])
        for b in reads:
            b.r[e] = dep
        for b in writes:
            b.w = dep
            b.r = {}
        return ins

    def dma(self, q, out, in_, reads=(), writes=()):
        b = writes[0] if writes else reads[0]
        if b.dsem is None:
            b.dsem = self.es.enter_context(self.nc.semaphore("dsem_" + b.name))
        if b.dcnt > 0:
            self._wait(q, ("dma", b.dsem, b.dcnt, b.name))
        self._deps(q, reads, writes)
        ins = self.eng[q].dma_start(out=out, in_=in_)
        b.dcnt += 16
        self.ninst += 1
        ins.then_inc(b.dsem, 16)
        dep = ("dma", b.dsem, b.dcnt, b.name)
        for x in writes:
            x.w = dep
            x.r = {}
        for x in reads:
            x.r[("dma", b.name)] = dep
        return dep

    def mm(self, out_b, out_ap, lhsT_b, lhsT_ap, rhs_b, rhs_ap, start=True, stop=True):
        return self.op("pe", lambda g: g.matmul(out_ap, lhsT_ap, rhs_ap, start=start, stop=stop),
                       reads=[lhsT_b, rhs_b], writes=[out_b])

    def tr(self, out_b, out_ap, in_b, in_ap, id_b, id_ap):
        return self.op("pe", lambda g: g.transpose(out_ap, in_ap, id_ap), reads=[in_b, id_b], writes=[out_b])

    def act(self, out_b, out_ap, in_b, in_ap, func, bias=None, scale=None, accum=None, extra_reads=(), extra_writes=()):
        kw = {}
        if bias is not None:
            kw["bias"] = bias
        if scale is not None:
            kw["scale"] = scale
        if accum is not None:
            kw["accum_out"] = accum
        return self.op("act", lambda g: g.activation(out_ap, in_ap, func, **kw),
                       reads=[in_b] + list(extra_reads), writes=[out_b] + list(extra_writes))

    def tt(self, e, out_b, out_ap, a_b, a_ap, b_b, b_ap, op):
        return self.op(e, lambda g: g.tensor_tensor(out_ap, a_ap, b_ap, op), reads=[a_b, b_b], writes=[out_b])

    def ts(self, e, out_b, out_ap, a_b, a_ap, s1, s2, op0, op1=None, extra_reads=()):
        if op1 is None:
            f = lambda g: g.tensor_scalar(out_ap, a_ap, s1, None, op0)
        else:
            f = lambda g: g.tensor_scalar(out_ap, a_ap, s1, s2, op0, op1)
        return self.op(e, f, reads=[a_b] + list(extra_reads), writes=[out_b])

    def stt(self, out_b, out_ap, a_b, a_ap, scalar, b_b, b_ap, op0, op1, extra_reads=()):
        return self.op("dve", lambda g: g.scalar_tensor_tensor(out_ap, a_ap, scalar, b_ap, op0, op1),
                       reads=[a_b, b_b] + list(extra_reads), writes=[out_b])

    def cp(self, e, out_b, out_ap, in_b, in_ap):
        if e == "act":
            return self.op("act", lambda g: g.copy(out_ap, in_ap), reads=[in_b], writes=[out_b])
        return self.op(e, lambda g: g.tensor_copy(out_ap, in_ap), reads=[in_b], writes=[out_b])

    def memset(self, e, b, ap, val):
        return self.op(e, lambda g: g.memset(ap, val), writes=[b])


GOFF = 768
ZOFF = 768 + 1536
GATE = ZOFF + 512
NCOL = GATE + 16
CFG = {"pre1": 16, "pre2": 48, "own": 16}
N1 = 2 + CFG["pre1"] + CFG["own"]
N2 = 2 + CFG["pre2"] + CFG["own"]
DK = 128


def set_cfg(pre1, pre2, own):
    global N1, N2
    CFG.update(pre1=pre1, pre2=pre2, own=own)
    N1 = 2 + pre1 + own
    N2 = 2 + pre2 + own


def build_program():
    nc = bass.Bass("TRN2", target_bir_lowering=False)

    def din(name, shape, dt=F32):
        return nc.dram_tensor(name, list(shape), dt, kind="ExternalInput").ap()

    xs_d = [din("xs1", [N1, 128, D]), din("xs2", [N2, 128, D])]
    xh_d = [din("xh1", [N1, 4, D]), din("xh2", [N2, 4, D])]
    rope_d = din("rope2", [N2, 128, 1280])
    flg_d = din("flags", [128, 2, N2, 4])
    cc_d = din("cc", [128, 16])
    wmod_d = din("wmod", [6, 128, 8 * D])
    bmod_d = din("bmod", [128, 6 * D])
    n1w_d = din("n1w", [128, D])
    n2w_d = din("n2w", [128, D])
    win_d = din("win", [128, 8, NCOL])
    qkw_d = din("qkw", [128, 640])
    cw_d = din("convw", [128, 60])
    alog_d = din("alog", [128, 8])
    dtb_d = din("dtb", [128, 8])
    dnw_d = din("dnw", [128, 512])
    woa_d = din("woa", [64, 8, D])
    wod_d = din("wod", [128, 4, D])
    w1_d = din("w1", [128, 8, 4 * D])
    w2_d = din("w2", [128, 32, D])
    xown_d = din("xown", [CFG["own"], 128, D])
    ident_d = din("ident", [128, 128])
    tri_d = din("tri", [128, 2, 128])
    msk_d = din("masks", [128, 4, 512])
    imk_d = din("imask", [128, 4, 512])
    out_d = nc.dram_tensor("out", [CFG["own"], 128, D], F32, kind="ExternalOutput").ap()

    kT_d = nc.dram_tensor("kT_s", [128, N2 * 128], BF16).ap()
    v_d = nc.dram_tensor("v_s", [N2, 128, 130], BF16).ap()
    qT_d = nc.dram_tensor("qT_s", [4, 128, CFG["own"] * 128], BF16).ap()
    o1_d = nc.dram_tensor("o1_s", [CFG["own"], 128, 512], F32).ap()
    mdn_d = nc.dram_tensor("mdn_s", [CFG["own"], 128, 512], BF16).ap()
    mod_d = nc.dram_tensor("mod_s", [4, 128, D], F32).ap()
    oa_d = nc.dram_tensor("oa_s", [64, 8, CFG["own"] * 128], BF16).ap()
    w2b_d = nc.dram_tensor("w2b_s", [128, 32, D], BF16).ap()

    with ExitStack() as es:
        kb = KB(nc, es)
        kb.init_psum()

        def barrier():
            for e in ("pe", "act", "dve", "pool", "sp"):
                for o in ("pe", "act", "dve", "pool"):
                    if o != e:
                        kb._wait(e, (o, kb.cnt[o]))

        ident = kb.sb([128, 128], F32, "ident")
        identb = kb.sb([128, 128], BF16, "identb")
        onesf = kb.sb([128, 128], F32, "onesf")
        kb.dma("sp", ident[:, :], ident_d[:, :], writes=[ident])
        kb.cp("dve", identb, identb[:, :], ident, ident[:, :])
        kb.memset("dve", onesf, onesf[:, :], 1.0)

        S1 = kb.sb([128, D], F32, "S1")
        H1 = kb.sb([128, D], F32, "H1")
        S1c = kb.sb([128, D], F32, "S1c")
        H1c = kb.sb([128, D], F32, "H1c")

        with ExitStack() as esA:
            kbA = kb
            old_es = kb.es
            kb.es = esA
            cc = kb.sb([128, 16], F32, "cc")
            sg = kb.sb([128, 16], F32, "ccsg")
            Lc = kb.sb([128, 16, 128], F32, "Lc")
            bm = kb.sb([128, 6 * D], F32, "bm")
            nw = kb.sb([128, D], F32, "nw")
            stage = [kb.sb([128, 8 * D], F32, f"wmst{i}") for i in range(2)]
            res = kb.sb([128, D], F32, "modres")
            kb.dma("sp", cc[:, :], cc_d[:, :], writes=[cc])
            kb.dma("sp", bm[:, :], bmod_d[:, :], writes=[bm])
            kb.dma("sp", nw[:, :], n1w_d[:, :], writes=[nw])
            kb.act(sg, sg[:, :], cc, cc[:, :], AF.Exp, scale=-1.0)
            kb.ts("dve", sg, sg[:, :], sg, sg[:, :], 1.0, None, ALU.add)
            kb.op("dve", lambda g: g.reciprocal(sg[:, :], sg[:, :]), reads=[sg], writes=[sg])
            kb.tt("dve", sg, sg[:, :], sg, sg[:, :], cc, cc[:, :], ALU.mult)
            for j in range(16):
                kb.ts("dve", Lc, Lc[:, j, :], onesf, onesf[:, :], sg[:, j:j + 1], None, ALU.mult, extra_reads=[sg])

            def mod_block(blk, lofs, dst_fn):
                st = stage[blk % 2]
                for half in range(2):
                    pb = kb.bank()
                    for k in range(8):
                        kb.mm(pb, pb[:, :], Lc, Lc[:, lofs + k, :], st, st[:, k * D + half * 512:k * D + half * 512 + 512],
                              start=(k == 0), stop=(k == 7))
                    dst_fn(pb, half)

            for blk in range(6):
                st = stage[blk % 2]
                kb.dma("sp", st[:, :], wmod_d[blk, :, :], writes=[st])
                if blk == 3:
                    kb.dma("sp", nw[:, :], n2w_d[:, :], writes=[nw])

                def plain(dst):
                    def f(pb, half, dst=dst, blk=blk):
                        sl = slice(half * 512, half * 512 + 512)
                        kb.tt("dve", dst, dst[:, sl], pb, pb[:, :], bm, bm[:, blk * D + half * 512:blk * D + half * 512 + 512], ALU.add)
                    return f

                def scale(dst):
                    def f(pb, half, dst=dst, blk=blk):
                        sl = slice(half * 512, half * 512 + 512)
                        kb.tt("dve", dst, dst[:, sl], pb, pb[:, :], bm, bm[:, blk * D + half * 512:blk * D + half * 512 + 512], ALU.add)
                        kb.stt(dst, dst[:, sl], dst, dst[:, sl], 1.0, nw, nw[:, sl], ALU.add, ALU.mult)
                    return f

                def spill(idx, fn_maker):
                    def f(pb, half, idx=idx):
                        fn_maker(res)(pb, half)
                        if half == 1:
                            kb.dma("sp", mod_d[idx, :, :], res[:, :], reads=[res])
                    return f

                if blk == 0:
                    mod_block(blk, 0, plain(H1))
                    mod_block(blk, 8, plain(H1c))
                elif blk == 1:
                    mod_block(blk, 0, scale(S1))
                    mod_block(blk, 8, scale(S1c))
                elif blk == 2:
                    mod_block(blk, 0, spill(0, plain))
                elif blk == 3:
                    mod_block(blk, 0, spill(2, plain))
                elif blk == 4:
                    mod_block(blk, 0, spill(1, scale))
                else:
                    mod_block(blk, 0, spill(3, plain))
            if DEBUG:
                dbg = nc.dram_tensor("dbgA", [8, 128, D], F32, kind="ExternalOutput").ap()
                for n_, b_ in enumerate((S1, H1, S1c, H1c)):
                    dd = kb.dma("sp", dbg[n_, :, :], b_[:, :], reads=[b_])
                    kb._wait("sp", dd)
                tmpd = kb.sb([128, D], F32, "dbgtmp")
                for n_ in range(4):
                    kb.dma("sp", tmpd[:, :], mod_d[n_, :, :], writes=[tmpd])
                    dd = kb.dma("sp", dbg[4 + n_, :, :], tmpd[:, :], reads=[tmpd])
                    kb._wait("sp", dd)
            barrier()
            kb.es = old_es

        esB = ExitStack()
        kb.es = esB
        wb = kb.sb([128, 8, NCOL], BF16, "wb")
        cdiag = kb.sb([128, 12, 5, 128], BF16, "cdiag")
        flg = kb.sb([128, 2, N2, 4], F32, "flg")
        with ExitStack() as esW:
            old_es = kb.es
            kb.es = esW
            wst = kb.sb([128, 8, 708], F32, "wst")
            cw = kb.sb([128, 60], F32, "cw")
            for part in range(4):
                sl = slice(part * 708, (part + 1) * 708)
                kb.dma("sp", wst[:, :, :], win_d[:, :, sl], writes=[wst])
                kb.cp("dve" if part % 2 == 0 else "pool", wb, wb[:, :, sl], wst, wst[:, :, :])
            kb.dma("sp", cw[:, :], cw_d[:, :], writes=[cw])
            kb.dma("sp", flg[:, :, :, :], flg_d[:, :, :, :], writes=[flg])
            for blk in range(12):
                for j in range(5):
                    kb.ts("dve", cdiag, cdiag[:, blk, j, :], ident, ident[:, :], cw[:, blk * 5 + j:blk * 5 + j + 1], None,
                          ALU.mult, extra_reads=[cw])
            barrier()
            kb.es = old_es

        xt = [kb.sb([128, D], F32, f"xt{i}") for i in range(2)]
        xh = kb.sb([4, D], F32, "xh")
        junk = kb.sb([128, D], BF16, "junk")
        ssq = kb.sb([128, 1], F32, "ssq")
        rstd = kb.sb([128, 1], F32, "rstd")
        t32 = kb.sb([128, D], F32, "t32")
        hb = kb.sb([128, D], BF16, "hb")
        hT = kb.sb([128, 8, 132], BF16, "hT")
        preT = kb.sb([128, 12, 132], BF16, "preT")
        sgt = kb.sb([128, 512], F32, "sgt")
        class FS:
            def __init__(self, i):
                self.cv = kb.sb([128, 12 * 128], F32, f"cv{i}")
                for nm, w in (("eb", 4), ("beta", 4), ("xa", 4), ("ab", 4), ("en", 4), ("lp", 4), ("g_", 4), ("gcs", 8),
                              ("e_", 4), ("egl", 4), ("dgl", 4), ("el", 4), ("ngc", 4), ("negbeta", 4)):
                    setattr(self, nm, kb.sb([128, w], F32, f"{nm}{i}"))
                self.zs = kb.sb([128, 512], F32, f"zs{i}")

        fsets = [FS(0), FS(1)]

        def norm_rows(xbuf, np_, Sb, Hb):
            kb.act(junk, junk[0:np_, :], xbuf, xbuf[0:np_, :], AF.Square, accum=ssq[0:np_, :], extra_writes=[ssq])
            kb.act(rstd, rstd[0:np_, :], ssq, ssq[0:np_, :], AF.Ln, bias=EPS, scale=1.0 / D)
            kb.act(rstd, rstd[0:np_, :], rstd, rstd[0:np_, :], AF.Exp, scale=-0.5)
            kb.stt(t32, t32[0:np_, :], xbuf, xbuf[0:np_, :], rstd[0:np_, 0:1], Sb, Sb[0:np_, :], ALU.mult, ALU.mult,
                   extra_reads=[rstd])
            kb.tt("dve", hb, hb[0:np_, :], t32, t32[0:np_, :], Hb, Hb[0:np_, :], ALU.add)

        def norm_tile(ds, vis, xbuf, Sb, Hb):
            kb.dma("sp", xbuf[:, :], xs_d[ds][vis, :, :], writes=[xbuf])
            kb.dma("sp", xh[:, :], xh_d[ds][vis, :, :], writes=[xh])
            norm_rows(xbuf, 128, Sb, Hb)
            pb = kb.bank()
            pbv = pb.t[:, :].bitcast(BF16)
            for k in range(8):
                kb.tr(pb, pbv[:, k * 128:(k + 1) * 128], hb, hb[:, k * 128:(k + 1) * 128], identb, identb[:, :])
            kb.cp("act", hT, hT[:, :, 2:130], pb, pbv.rearrange("p (k t) -> p k t", k=8))
            norm_rows(xh, 4, Sb, Hb)
            pb2 = kb.bank()
            pbv2 = pb2.t[:, :].bitcast(BF16)
            for k in range(8):
                kb.tr(pb2, pbv2[:, k * 4:(k + 1) * 4], hb, hb[0:4, k * 128:(k + 1) * 128], identb, identb[0:4, 0:4])
            hv = pbv2[:, 0:32].rearrange("p (k t) -> p k t", k=8)
            kb.cp("act", hT, hT[:, :, 0:2], pb2, hv[:, :, 0:2])
            kb.cp("act", hT, hT[:, :, 130:132], pb2, hv[:, :, 2:4])

        def project_tok(col0, ncols):
            pb = kb.bank()
            for k in range(8):
                kb.mm(pb, pb[:, 0:ncols], hT, hT[:, k, 2:130], wb, wb[:, k, col0:col0 + ncols], start=(k == 0), stop=(k == 7))
            return pb

        def gdn_pre_conv(ds, vis, nblk, F):
            for g0 in range(0, nblk, 3):
                pb = kb.bank()
                for j in range(3):
                    blk = g0 + j
                    for k in range(8):
                        kb.mm(pb, pb[:, j * 132:(j + 1) * 132], wb, wb[:, k, GOFF + blk * 128:GOFF + (blk + 1) * 128],
                              hT, hT[:, k, :], start=(k == 0), stop=(k == 7))
                kb.cp("act", preT, preT[:, g0:g0 + 3, :], pb, pb[:, 0:396].rearrange("p (j t) -> p j t", j=3))
                yield
            kb.ts("dve", preT, preT[:, 0:nblk, 0:2], preT, preT[:, 0:nblk, 0:2], flg[:, ds, vis, 0:1], None, ALU.mult, extra_reads=[flg])
            kb.ts("dve", preT, preT[:, 0:nblk, 130:132], preT, preT[:, 0:nblk, 130:132], flg[:, ds, vis, 1:2], None, ALU.mult, extra_reads=[flg])
            for g0 in range(0, nblk, 4):
                pb = kb.bank()
                for j4 in range(4):
                    blk = g0 + j4
                    for j in range(5):
                        kb.mm(pb, pb[:, j4 * 128:(j4 + 1) * 128], preT, preT[:, blk, j:j + 128], cdiag, cdiag[:, blk, j, :],
                              start=(j == 0), stop=(j == 4))
                kb.act(sgt, sgt[:, :], pb, pb[:, :], AF.Exp, scale=-1.0)
                kb.ts("dve", sgt, sgt[:, :], sgt, sgt[:, :], 1.0, None, ALU.add)
                kb.op("dve", lambda g: g.reciprocal(sgt[:, :], sgt[:, :]), reads=[sgt], writes=[sgt])
                kb.tt("dve", F.cv, F.cv[:, g0 * 128:(g0 + 4) * 128], pb, pb[:, :], sgt, sgt[:, :], ALU.mult)
                yield

        tri = kb.sb([128, 2, 128], F32, "tri")
        mskb = kb.sb([128, 4, 512], BF16, "mskb")
        negA = kb.sb([128, 8], F32, "negA")
        dtb = kb.sb([128, 8], F32, "dtb")
        identrep = kb.sb([128, 512], F32, "identrep")
        imk = kb.sb([128, 4, 512], BF16, "imk")
        with ExitStack() as esG:
            old_es = kb.es
            kb.es = esG
            mst = kb.sb([128, 4, 512], F32, "mst")
            kb.dma("sp", tri[:, :, :], tri_d[:, :, :], writes=[tri])
            kb.dma("sp", mst[:, :, :], msk_d[:, :, :], writes=[mst])
            kb.cp("dve", mskb, mskb[:, :, :], mst, mst[:, :, :])
            kb.dma("sp", mst[:, :, :], imk_d[:, :, :], writes=[mst])
            kb.cp("dve", imk, imk[:, :, :], mst, mst[:, :, :])
            kb.dma("sp", negA[:, :], alog_d[:, :], writes=[negA])
            kb.dma("sp", dtb[:, :], dtb_d[:, :], writes=[dtb])
            kb.act(negA, negA[:, :], negA, negA[:, :], AF.Exp)
            kb.ts("dve", negA, negA[:, :], negA, negA[:, :], -1.0, None, ALU.mult)
            for h in range(4):
                kb.cp("dve", identrep, identrep[:, h * 128:(h + 1) * 128], ident, ident[:, :])
            barrier()
            kb.es = old_es

        def sm(name, w=4):
            return kb.sb([128, w], F32, name)

        HW = 256

        class CH:
            def __init__(self, hp):
                n = f"c{hp}"
                self.hp = hp
                for nm in ("ssk", "rk", "c1", "c2", "ssq_", "rq", "cq", "cqe"):
                    setattr(self, nm, kb.sb([128, 2], F32, n + nm))
                self.junk2 = kb.sb([128, 128], BF16, n + "junk2")
                for nm in ("kn", "vb", "kbg", "kd", "qs", "qd", "knT", "qsT", "qdT", "Nb", "Zb", "Nk", "Zk", "Xb", "Yb",
                           "Cb", "Eb", "U1b", "U2b", "wT", "vnew", "intraT"):
                    setattr(self, nm, kb.sb([128, HW], BF16, n + nm))
                for nm in ("dg", "Ds", "DT", "u_"):
                    setattr(self, nm, kb.sb([128, HW], F32, n + nm))

        chains = [CH(0), CH(1)]
        SstC = [[kb.sb([128, HW], F32, f"SstC{d_}{hp}") for hp in range(2)] for d_ in range(2)]
        SbfC = [[kb.sb([128, HW], BF16, f"SbfC{d_}{hp}") for hp in range(2)] for d_ in range(2)]
        for d_ in range(2):
            for hp in range(2):
                kb.memset("dve", SstC[d_][hp], SstC[d_][hp][:, :], 0.0)
                kb.memset("dve", SbfC[d_][hp], SbfC[d_][hp][:, :], 0.0)

        def H(hh):
            return slice(hh * 128, (hh + 1) * 128)

        def tr4(dst, src):
            pb = kb.bank()
            pv = pb.t[:, :].bitcast(BF16)
            for hh in range(4):
                kb.tr(pb, pv[:, H(hh)], src, src[:, H(hh)], identb, identb[:, :])
            kb.cp("act", dst, dst[:, :], pb, pv[:, 0:512])

        def tr2(dst, src, e="act"):
            pb = kb.bank()
            pv = pb.t[:, :].bitcast(BF16)
            for lh in range(2):
                kb.tr(pb, pv[:, H(lh)], src, src[:, H(lh)], identb, identb[:, :])
            kb.cp(e, dst, dst[:, :], pb, pv[:, 0:HW])

        def rsq(dst, src, add):
            kb.act(dst, dst[:, :], src, src[:, :], AF.Ln, bias=add)
            kb.act(dst, dst[:, :], dst, dst[:, :], AF.Exp, scale=-0.5)

        def gdn_gates(ds, vis, own, F):
            c4 = slice(ds * 4, ds * 4 + 4)
            a4 = slice(8 + ds * 4, 8 + ds * 4 + 4)
            pg = project_tok(GATE, 16)
            fg = flg[:, ds, vis, 2:3]
            kb.act(F.eb, F.eb[:, :], pg, pg[:, c4], AF.Exp, scale=-1.0)
            kb.ts("dve", F.eb, F.eb[:, :], F.eb, F.eb[:, :], 1.0, None, ALU.add)
            kb.op("dve", lambda g: g.reciprocal(F.beta[:, :], F.eb[:, :]), reads=[F.eb], writes=[F.beta])
            kb.ts("dve", F.beta, F.beta[:, :], F.beta, F.beta[:, :], fg, None, ALU.mult, extra_reads=[flg])
            kb.ts("dve", F.negbeta, F.negbeta[:, :], F.beta, F.beta[:, :], -1.0, None, ALU.mult)
            kb.tt("dve", F.xa, F.xa[:, :], pg, pg[:, a4], dtb, dtb[:, c4], ALU.add)
            kb.act(F.ab, F.ab[:, :], F.xa, F.xa[:, :], AF.Abs)
            kb.act(F.en, F.en[:, :], F.ab, F.ab[:, :], AF.Exp, scale=-1.0)
            kb.act(F.lp, F.lp[:, :], F.en, F.en[:, :], AF.Ln, bias=1.0)
            kb.stt(F.g_, F.g_[:, :], F.xa, F.xa[:, :], 0.0, F.lp, F.lp[:, :], ALU.max, ALU.add)
            kb.tt("dve", F.g_, F.g_[:, :], F.g_, F.g_[:, :], negA, negA[:, c4], ALU.mult)
            kb.ts("dve", F.g_, F.g_[:, :], F.g_, F.g_[:, :], fg, None, ALU.mult, extra_reads=[flg])
            pgc = kb.bank()
            kb.mm(pgc, pgc[:, 0:4], tri, tri[:, ds, :], F.g_, F.g_[:, :])
            kb.mm(pgc, pgc[:, 4:8], onesf, onesf[:, :], F.g_, F.g_[:, :])
            kb.cp("act", F.gcs, F.gcs[:, :], pgc, pgc[:, 0:8])
            kb.act(F.e_, F.e_[:, :], F.gcs, F.gcs[:, 0:4], AF.Exp)
            kb.act(F.egl, F.egl[:, :], F.gcs, F.gcs[:, 4:8], AF.Exp)
            kb.tt("dve", F.dgl, F.dgl[:, :], F.gcs, F.gcs[:, 4:8], F.gcs, F.gcs[:, 0:4], ALU.subtract)
            kb.act(F.el, F.el[:, :], F.dgl, F.dgl[:, :], AF.Exp)
            if own:
                kb.ts("dve", F.ngc, F.ngc[:, :], F.gcs, F.gcs[:, 0:4], -1.0, None, ALU.mult)

        def gdn_chain(ds, vis, own, C, res, F):
            hp = C.hp
            S, Sb = SstC[ds][hp], SbfC[ds][hp]
            hs = (2 * hp, 2 * hp + 1)
            g2 = slice(2 * hp, 2 * hp + 2)
            mk = slice(0, HW)

            def mm2(l_b, r_b, acc_b=None):
                pb_ = kb.bank()
                for lh in range(2):
                    kb.mm(pb_, pb_[:, H(lh)], l_b, l_b[:, H(lh)], r_b, r_b[:, H(lh)], start=True, stop=(acc_b is None))
                    if acc_b is not None:
                        kb.mm(pb_, pb_[:, H(lh)], identb, identb[:, :], acc_b, acc_b[:, H(lh)], start=False, stop=True)
                return pb_

            for lh, hh in enumerate(hs):
                kb.act(C.junk2, C.junk2[:, :], F.cv, F.cv[:, H(hh)], AF.Square, accum=C.ssk[:, lh:lh + 1], extra_writes=[C.ssk])
            rsq(C.rk, C.ssk, EPS)
            yield
            kb.tt("dve", C.c1, C.c1[:, :], C.rk, C.rk[:, :], F.beta, F.beta[:, g2], ALU.mult)
            kb.tt("dve", C.c1, C.c1[:, :], C.c1, C.c1[:, :], F.e_, F.e_[:, g2], ALU.mult)
            kb.tt("dve", C.c2, C.c2[:, :], C.rk, C.rk[:, :], F.el, F.el[:, g2], ALU.mult)
            yield
            def scl(dst, dcol, src_col, sc_b, sc_ap):
                kb.act(dst, dst[:, H(dcol)], F.cv, F.cv[:, H(src_col)], AF.Identity, scale=sc_ap, extra_reads=[sc_b])

            for lh, hh in enumerate(hs):
                scl(C.kn, lh, hh, C.rk, C.rk[:, lh:lh + 1])
                scl(C.vb, lh, 4 + hh, F.beta, F.beta[:, hh:hh + 1])
                scl(C.kbg, lh, hh, C.c1, C.c1[:, lh:lh + 1])
                scl(C.kd, lh, hh, C.c2, C.c2[:, lh:lh + 1])
                yield
            tr2(C.knT, C.kn)
            yield
            if own:
                for lh, hh in enumerate(hs):
                    kb.act(C.junk2, C.junk2[:, :], F.cv, F.cv[:, H(8 + hh)], AF.Square, accum=C.ssq_[:, lh:lh + 1], extra_writes=[C.ssq_])
                rsq(C.rq, C.ssq_, EPS)
                kb.ts("dve", C.cq, C.cq[:, :], C.rq, C.rq[:, :], float(DK) ** -0.5, None, ALU.mult)
                kb.tt("dve", C.cqe, C.cqe[:, :], C.cq, C.cq[:, :], F.e_, F.e_[:, g2], ALU.mult)
                yield
                for lh, hh in enumerate(hs):
                    scl(C.qs, lh, 8 + hh, C.cq, C.cq[:, lh:lh + 1])
                    scl(C.qd, lh, 8 + hh, C.cqe, C.cqe[:, lh:lh + 1])
                yield
                tr2(C.qsT, C.qs)
                yield
                tr2(C.qdT, C.qd)
                yield
            pA = mm2(C.knT, C.knT)
            for lh, hh in enumerate(hs):
                kb.act(C.dg, C.dg[:, H(lh)], ident, ident[:, :], AF.Identity, scale=F.gcs[:, hh:hh + 1], extra_reads=[F.gcs])
            pG = kb.bank()
            kb.mm(pG, pG[:, 0:HW], onesf, onesf[:, :], C.dg, C.dg[:, :], start=True, stop=False)
            kb.mm(pG, pG[:, 0:HW], identb, identb[:, :], mskb, mskb[:, ds * 2, mk], start=False, stop=True)
            yield
            for lh, hh in enumerate(hs):
                kb.act(C.Ds, C.Ds[:, H(lh)], pG, pG[:, H(lh)], AF.Exp, bias=F.gcs[:, hh:hh + 1], scale=-1.0, extra_reads=[F.gcs])
            yield
            for lh, hh in enumerate(hs):
                kb.stt(C.Nb, C.Nb[:, H(lh)], pA, pA[:, H(lh)], F.negbeta[:, hh:hh + 1], C.Ds, C.Ds[:, H(lh)], ALU.mult, ALU.mult,
                       extra_reads=[F.negbeta])
            yield
            tr2(C.Zb, C.Nb)
            yield
            kb.tt("pool", C.Nk, C.Nk[:, :], C.Nb, C.Nb[:, :], imk, imk[:, 0, mk], ALU.mult)
            kb.tt("pool", C.Zk, C.Zk[:, :], C.Zb, C.Zb[:, :], imk, imk[:, 0, mk], ALU.mult)
            yield
            kb.tt("dve", C.Xb, C.Xb[:, :], C.Nk, C.Nk[:, :], identrep, identrep[:, mk], ALU.add)
            kb.tt("dve", C.Yb, C.Yb[:, :], C.Zk, C.Zk[:, :], identrep, identrep[:, mk], ALU.add)
            yield
            for lvl in range(1, 4):
                pN = mm2(C.Zk, C.Nk)
                pZ = mm2(C.Nk, C.Zk)
                yield
                kb.cp("act", C.Nk, C.Nk[:, :], pN, pN[:, 0:HW])
                kb.cp("dve", C.Zk, C.Zk[:, :], pZ, pZ[:, 0:HW])
                yield
                pX = mm2(C.Yb, C.Nk, C.Xb)
                pP = mm2(C.Xb, C.Zk, C.Yb)
                yield
                kb.cp("act", C.Xb, C.Xb[:, :], pX, pX[:, 0:HW])
                kb.cp("dve", C.Yb, C.Yb[:, :], pP, pP[:, 0:HW])
                yield
            for mi in (1, 2, 3):
                last = mi == 3
                kb.tt("pool", C.Cb, C.Cb[:, :], C.Nb, C.Nb[:, :], imk, imk[:, mi, mk], ALU.mult)
                if not last:
                    kb.tt("pool", C.Eb, C.Eb[:, :], C.Zb, C.Zb[:, :], imk, imk[:, mi, mk], ALU.mult)
                yield
                if not last:
                    pU1 = mm2(C.Eb, C.Xb)
                pU2 = mm2(C.Cb, C.Yb)
                yield
                if not last:
                    kb.cp("dve", C.U1b, C.U1b[:, :], pU1, pU1[:, 0:HW])
                kb.cp("act", C.U2b, C.U2b[:, :], pU2, pU2[:, 0:HW])
                yield
                if not last:
                    pX = mm2(C.Yb, C.U1b, C.Xb)
                pP = mm2(C.Xb, C.U2b, C.Yb)
                yield
                if not last:
                    kb.cp("act", C.Xb, C.Xb[:, :], pX, pX[:, 0:HW])
                kb.cp("dve", C.Yb, C.Yb[:, :], pP, pP[:, 0:HW])
                yield
            pU = mm2(C.Yb, C.vb)
            pW = mm2(C.kbg, C.Yb)
            yield
            kb.cp("act", C.u_, C.u_[:, :], pU, pU[:, 0:HW])
            kb.cp("dve", C.wT, C.wT[:, :], pW, pW[:, 0:HW])
            yield
            pWS = mm2(C.wT, Sb)
            yield
            kb.tt("dve", C.vnew, C.vnew[:, :], C.u_, C.u_[:, :], pWS, pWS[:, 0:HW], ALU.subtract)
            yield
            if own:
                pGI = kb.bank()
                kb.mm(pGI, pGI[:, 0:HW], onesf, onesf[:, :], C.dg, C.dg[:, :], start=True, stop=False)
                kb.mm(pGI, pGI[:, 0:HW], identb, identb[:, :], mskb, mskb[:, ds * 2 + 1, mk], start=False, stop=True)
                pQK = mm2(C.knT, C.qsT)
                yield
                for lh, hh in enumerate(hs):
                    kb.act(C.DT, C.DT[:, H(lh)], pGI, pGI[:, H(lh)], AF.Exp, bias=F.ngc[:, hh:hh + 1], extra_reads=[F.ngc])
                yield
                kb.tt("dve", C.intraT, C.intraT[:, :], pQK, pQK[:, 0:HW], C.DT, C.DT[:, :], ALU.mult)
                yield
                pO = kb.reserve()
                for lh in range(2):
                    kb.mm(pO, pO[:, H(lh)], C.qdT, C.qdT[:, H(lh)], Sb, Sb[:, H(lh)], start=True, stop=False)
                    kb.mm(pO, pO[:, H(lh)], C.intraT, C.intraT[:, H(lh)], C.vnew, C.vnew[:, H(lh)], start=False, stop=True)
                res[hp] = pO
                yield
            pS = mm2(C.kd, C.vnew)
            yield
            for lh, hh in enumerate(hs):
                kb.stt(S, S[:, H(lh)], S, S[:, H(lh)], F.egl[:, hh:hh + 1], pS, pS[:, H(lh)], ALU.mult, ALU.add, extra_reads=[F.egl])
            kb.cp("act", Sb, Sb[:, :], S, S[:, :])
            yield

        qkw = kb.sb([128, 640], F32, "qkw")
        dnw = kb.sb([128, 512], F32, "dnw")
        kb.dma("sp", qkw[:, :], qkw_d[:, :], writes=[qkw])
        kb.dma("sp", dnw[:, :], dnw_d[:, :], writes=[dnw])
        rp = kb.sb([128, 1280], F32, "rp")
        sqt = kb.sb([128, 640], F32, "sqt")
        ss10 = kb.sb([128, 10], F32, "ss10")
        rr10 = kb.sb([128, 10], F32, "rr10")
        xw = kb.sb([128, 640], F32, "xw")
        sw = kb.sb([128, 640], F32, "sw")
        t1 = kb.sb([128, 640], F32, "t1")
        qkb = kb.sb([128, 640], BF16, "qkb")
        kTt = kb.sb([128, 128], BF16, "kTt")
        qTt = kb.sb([128, 512], BF16, "qTt")
        vt = kb.sb([128, 130], BF16, "vt")
        kb.memset("dve", vt, vt[:, :], 1.0)

        def attn_proj(vis, own, m):
            kb.dma("sp", rp[:, :], rope_d[vis, :, :], writes=[rp])
            c0 = 0 if own else 512
            nh0 = 0 if own else 8
            pKV = project_tok(512, 256)
            pQ = project_tok(0, 512) if own else None
            if own:
                kb.act(sqt, sqt[:, 0:512], pQ, pQ[:, :], AF.Square)
            kb.act(sqt, sqt[:, 512:640], pKV, pKV[:, 0:128], AF.Square)
            kb.op("dve", lambda g: g.tensor_reduce(ss10[:, nh0:10], sqt[:, c0:640].rearrange("p (h d) -> p h d", d=64), AX.X, ALU.add),
                  reads=[sqt], writes=[ss10])
            kb.act(rr10, rr10[:, nh0:10], ss10, ss10[:, nh0:10], AF.Ln, bias=EPS, scale=1.0 / 64)
            kb.act(rr10, rr10[:, nh0:10], rr10, rr10[:, nh0:10], AF.Exp, scale=-0.5)
            for j in range(nh0, 10):
                src_b, src_ap = (pQ, pQ[:, j * 64:(j + 1) * 64]) if j < 8 else (pKV, pKV[:, (j - 8) * 64:(j - 7) * 64])
                kb.ts("dve", xw, xw[:, j * 64:(j + 1) * 64], src_b, src_ap, rr10[:, j:j + 1], None, ALU.mult, extra_reads=[rr10])
            kb.tt("dve", xw, xw[:, c0:640], xw, xw[:, c0:640], qkw, qkw[:, c0:640], ALU.mult)
            yield
            xv = xw[:, c0:640].rearrange("p (n two s) -> p n two s", two=2, s=16)
            sv = sw[:, c0:640].rearrange("p (n two s) -> p n two s", two=2, s=16)
            kb.cp("pool", sw, sv[:, :, 0, :], xw, xv[:, :, 1, :])
            kb.cp("pool", sw, sv[:, :, 1, :], xw, xv[:, :, 0, :])
            kb.tt("dve", t1, t1[:, c0:640], xw, xw[:, c0:640], rp, rp[:, c0:640], ALU.mult)
            kb.tt("dve", sw, sw[:, c0:640], sw, sw[:, c0:640], rp, rp[:, 640 + c0:1280], ALU.mult)
            kb.tt("dve", qkb, qkb[:, c0:640], t1, t1[:, c0:640], sw, sw[:, c0:640], ALU.add)
            yield
            pb = kb.bank()
            pv = pb.t[:, :].bitcast(BF16)
            kb.tr(pb, pv[:, 0:128], qkb, qkb[:, 512:640], identb, identb[:, :])
            kb.cp("act", kTt, kTt[:, :], pb, pv[:, 0:128])
            kb.dma("sp", kT_d[:, vis * 128:(vis + 1) * 128], kTt[:, :], reads=[kTt])
            kb.cp("act", vt, vt[:, :].rearrange("p (k c) -> p k c", k=2)[:, :, 0:64], pKV,
                  pKV[:, 128:256].rearrange("p (k c) -> p k c", k=2))
            kb.dma("sp", v_d[vis, :, :], vt[:, :], reads=[vt])
            yield
            if own:
                tr4(qTt, qkb)
                for pr in range(4):
                    kb.dma("sp", qT_d[pr, :, m * 128:(m + 1) * 128], qTt[:, H(pr)], reads=[qTt])

        o1t = kb.sb([128, 512], F32, "o1t")
        junk2 = kb.sb([128, 128], BF16, "junk2o")
        o1r = kb.sb([128, 512], F32, "o1r")
        osum = kb.sb([128, 512], F32, "osum")
        ssd, rsd = sm("ssd"), sm("rsd")
        od = kb.sb([128, 512], BF16, "od")
        odT = kb.sb([128, 512], BF16, "odT")

        def own_out_scan1(pOs, m):
            for hp in range(2):
                kb.cp("act" if hp == 0 else "dve", o1t, o1t[:, hp * HW:(hp + 1) * HW], pOs[hp], pOs[hp][:, 0:HW])
                kb.release(pOs[hp])
            kb.dma("sp", o1_d[m, :, :], o1t[:, :], reads=[o1t])

        def own_out_scan2(pOs, m, F):
            zs = F.zs
            kb._wait("sp", o1t.r.get(("dma", o1t.name)))
            kb.dma("sp", o1r[:, :], o1_d[m, :, :], writes=[o1r])
            for hp in range(2):
                kb.tt("dve", osum, osum[:, hp * HW:(hp + 1) * HW], o1r, o1r[:, hp * HW:(hp + 1) * HW], pOs[hp], pOs[hp][:, 0:HW], ALU.add)
                kb.release(pOs[hp])
            for hh in range(4):
                kb.act(junk2, junk2[:, :], osum, osum[:, H(hh)], AF.Square, accum=ssd[:, hh:hh + 1], extra_writes=[ssd])
            kb.act(rsd, rsd[:, :], ssd, ssd[:, :], AF.Ln, bias=EPS, scale=1.0 / 128)
            kb.act(rsd, rsd[:, :], rsd, rsd[:, :], AF.Exp, scale=-0.5)
            for hh in range(4):
                kb.stt(od, od[:, H(hh)], osum, osum[:, H(hh)], rsd[:, hh:hh + 1], zs, zs[:, H(hh)], ALU.mult, ALU.mult,
                       extra_reads=[rsd])
            tr4(odT, od)
            kb.dma("sp", mdn_d[m, :, :], odT[:, :], reads=[odT])

        nown, pre1, pre2 = CFG["own"], CFG["pre1"], CFG["pre2"]

        def vinfo(ds, vis):
            npre = pre1 if ds == 0 else pre2
            return vis < 2, vis >= 2 + npre, vis - 2 - npre

        def front(ds, vis, F):
            isctx, own, sidx = vinfo(ds, vis)
            norm_tile(ds, vis, xt[ds], S1c if isctx else S1, H1c if isctx else H1)
            yield
            if ds == 1:
                yield from attn_proj(vis, own, sidx if own else 0)
                if own:
                    pZ = project_tok(ZOFF, 512)
                    kb.act(F.zs, F.zs[:, :], pZ, pZ[:, :], AF.Exp, scale=-1.0)
                    kb.ts("dve", F.zs, F.zs[:, :], F.zs, F.zs[:, :], 1.0, None, ALU.add)
                    kb.op("dve", lambda g: g.reciprocal(F.zs[:, :], F.zs[:, :]), reads=[F.zs], writes=[F.zs])
                    kb.tt("dve", F.zs, F.zs[:, :], pZ, pZ[:, :], F.zs, F.zs[:, :], ALU.mult)
                    kb.tt("dve", F.zs, F.zs[:, :], F.zs, F.zs[:, :], dnw, dnw[:, :], ALU.mult)
                    yield
            yield from gdn_pre_conv(ds, vis, 12 if own else 8, F)
            gdn_gates(ds, vis, own, F)
            yield

        def run_all(gens):
            gens = list(gens)
            while gens:
                for g in list(gens):
                    try:
                        next(g)
                    except StopIteration:
                        gens.remove(g)

        order = []
        v1 = 0
        n2_head = 2 + pre2
        for v2 in range(N2):
            if v2 == n2_head:
                while v1 < N1:
                    order.append((0, v1))
                    v1 += 1
            order.append((1, v2))
            while v1 < N1 and v1 * n2_head < (v2 + 1) * N1 and v2 < n2_head:
                order.append((0, v1))
                v1 += 1
        assert len(order) == N1 + N2

        run_all([front(order[0][0], order[0][1], fsets[0])])
        for n_, (ds, vis) in enumerate(order):
            F = fsets[n_ % 2]
            isctx, own, sidx = vinfo(ds, vis)
            res = [None, None]
            gens = [gdn_chain(ds, vis, own, chains[0], res, F), gdn_chain(ds, vis, own, chains[1], res, F)]
            if n_ + 1 < len(order):
                gens.append(front(order[n_ + 1][0], order[n_ + 1][1], fsets[(n_ + 1) % 2]))
            run_all(gens)
            if own and ds == 0:
                own_out_scan1(res, nown - 1 - sidx)
            if own and ds == 1:
                own_out_scan2(res, sidx, F)
        barrier()
        for e in ("sp",):
            for bb in (kTt, vt, qTt, odT, o1t):
                for dd in list(bb.r.values()):
                    kb._wait(e, dd)

        if DEBUG:
            dbgM = nc.dram_tensor("dbgM", [nown, 128, 512], BF16, kind="ExternalOutput").ap()
            dbgK = nc.dram_tensor("dbgK", [128, N2 * 128], BF16, kind="ExternalOutput").ap()
            dbgV = nc.dram_tensor("dbgV", [N2, 128, 130], BF16, kind="ExternalOutput").ap()
            dbgQ = nc.dram_tensor("dbgQ", [4, 128, nown * 128], BF16, kind="ExternalOutput").ap()
            tb = kb.sb([128, N2 * 128], BF16, "dbgtb")
            for src, dst, shp in ((kT_d[:, :], dbgK[:, :], None),):
                kb.dma("sp", tb[:, :], src, writes=[tb])
                dd = kb.dma("sp", dst, tb[:, :], reads=[tb]); kb._wait("sp", dd)
            for m in range(nown):
                kb.dma("sp", tb[:, 0:512], mdn_d[m, :, :], writes=[tb])
                dd = kb.dma("sp", dbgM[m, :, :], tb[:, 0:512], reads=[tb]); kb._wait("sp", dd)
            for v in range(N2):
                kb.dma("sp", tb[:, 0:130], v_d[v, :, :], writes=[tb])
                dd = kb.dma("sp", dbgV[v, :, :], tb[:, 0:130], reads=[tb]); kb._wait("sp", dd)
            for pr in range(4):
                kb.dma("sp", tb[:, 0:nown * 128], qT_d[pr, :, :], writes=[tb])
                dd = kb.dma("sp", dbgQ[pr, :, :], tb[:, 0:nown * 128], reads=[tb]); kb._wait("sp", dd)
        barrier()
        esB.close()
        kb.es = es

        nq = nown * 128
        NK = N2
        esC = ExitStack()
        kb.es = esC
        KTd = [kb.sb([128, NK * 128], BF16, f"KTd{i}") for i in range(2)]
        Vs = kb.sb([128, NK, 130], BF16, "Vs")
        QTs = kb.sb([128, 4, nq], BF16, "QTs")
        for kv in range(2):
            kb.dma("sp", KTd[kv][0:64, :], kT_d[kv * 64:(kv + 1) * 64, :], writes=[KTd[kv]])
            kb.dma("sp", KTd[kv][64:128, :], kT_d[kv * 64:(kv + 1) * 64, :], writes=[KTd[kv]])
        kb.dma("sp", Vs[:, :, :], v_d.rearrange("v p c -> p v c"), writes=[Vs])
        for pr in range(4):
            kb.dma("sp", QTs[:, pr, :], qT_d[pr, :, :], writes=[QTs])
        QB = min(512, nq)
        PTs = [kb.sb([128, QB], BF16, f"PT{i}") for i in range(4)]
        osb = kb.sb([65, QB], F32, "osb")
        rc = kb.sb([65, QB], F32, "rc")
        oab = kb.sb([64, QB], BF16, "oab")
        pti = 0
        for q0 in range(0, nq, QB):
            for hh in range(8):
                pr, half, kv = hh // 2, hh % 2, hh // 4
                ps_ = slice(half * 64, half * 64 + 64)
                pO = kb.reserve()

                def qk(kt):
                    pS_ = kb.bank()
                    kb.mm(pS_, pS_[:, 0:QB], KTd[kv], KTd[kv][ps_, kt * 128:(kt + 1) * 128], QTs, QTs[ps_, pr, q0:q0 + QB])
                    return pS_

                LOOK = 2
                pend = [qk(kt) for kt in range(min(LOOK, NK))]
                for kt in range(NK):
                    pS = pend.pop(0)
                    PT = PTs[pti % len(PTs)]
                    pti += 1
                    kb.act(PT, PT[:, :], pS, pS[:, 0:QB], AF.Exp, scale=0.125)
                    if kt + LOOK < NK:
                        pend.append(qk(kt + LOOK))
                    kb.mm(pO, pO[0:65, 0:QB], Vs, Vs[:, kt, kv * 65:(kv + 1) * 65], PT, PT[:, :], start=(kt == 0), stop=(kt == NK - 1))
                kb.cp("act", osb, osb[:, :], pO, pO[0:65, 0:QB])
                kb.release(pO)
                kb.op("dve", lambda g: g.reciprocal(rc[64:65, :], osb[64:65, :]), reads=[osb], writes=[rc])
                pB = kb.bank()
                kb.mm(pB, pB[0:64, 0:QB], onesf, onesf[64:65, 0:64], rc, rc[64:65, :])
                kb.tt("dve", oab, oab[:, :], osb, osb[0:64, :], pB, pB[0:64, 0:QB], ALU.mult)
                kb.dma("sp", oa_d[:, hh, q0:q0 + QB], oab[:, :], reads=[oab])
        barrier()
        for dd in list(oab.r.values()):
            kb._wait("sp", dd)
        esC.close()
        kb.es = es

        esD = ExitStack()
        kb.es = esD
        G1 = kb.sb([128, D], F32, "G1")
        S2 = kb.sb([128, D], F32, "S2")
        H2 = kb.sb([128, D], F32, "H2")
        G2 = kb.sb([128, D], F32, "G2")
        for i_, bb in enumerate((G1, S2, H2, G2)):
            kb.dma("sp", bb[:, :], mod_d[i_, :, :], writes=[bb])
        w1b = kb.sb([128, 8, 4 * D], BF16, "w1b")
        woab = kb.sb([64, 8, D], BF16, "woab")
        wodb = kb.sb([128, 4, D], BF16, "wodb")
        esW2 = ExitStack()
        kb.es = esW2
        wst2 = [kb.sb([128, 4096], F32, f"wst2_{i}") for i in range(2)]
        w2c = kb.sb([128, 4096], BF16, "w2c")
        ci = 0

        def conv_chunk(dst_b, dst_ap, src_ap, np_=128):
            nonlocal ci
            st = wst2[ci % 2]
            e = "pool" if ci % 2 == 0 else "dve"
            ci += 1
            kb.dma("sp", st[0:np_, :], src_ap, writes=[st])
            kb.cp(e, dst_b, dst_ap, st, st[0:np_, :])

        for k in range(8):
            conv_chunk(w1b, w1b[:, k, :], w1_d[:, k, :])
        for c in range(2):
            conv_chunk(woab, woab[:, c * 4:(c + 1) * 4, :].rearrange("p a b -> p (a b)"), woa_d[:, c * 4:(c + 1) * 4, :].rearrange("p a b -> p (a b)"), 64)
        conv_chunk(wodb, wodb[:, :, :].rearrange("p a b -> p (a b)"), wod_d.rearrange("p a b -> p (a b)"))
        for c in range(8):
            conv_chunk(w2c, w2c[:, :], w2_d[:, c * 4:(c + 1) * 4, :].rearrange("p a b -> p (a b)"))
            kb.dma("sp", w2b_d[:, c * 4:(c + 1) * 4, :].rearrange("p a b -> p (a b)"), w2c[:, :], reads=[w2c])
        for dd in list(w2c.r.values()):
            kb._wait("sp", dd)
        barrier()
        esW2.close()
        kb.es = esD

        TB = min(2, nown)
        NB = TB * 128
        oat = kb.sb([64, 8, NB], BF16, "oat")
        mdt = kb.sb([128, TB, 512], BF16, "mdt")
        xo = kb.sb([128, D], F32, "xo")
        x1b = kb.sb([128, TB, D], F32, "x1b")
        junkD = kb.sb([128, D], BF16, "junkD")
        ssqD = kb.sb([128, 1], F32, "ssqD")
        rstdD = kb.sb([128, 1], F32, "rstdD")
        t32D = kb.sb([128, D], F32, "t32D")
        hbD = kb.sb([128, D], BF16, "hbD")
        h2T = kb.sb([128, 8, NB], BF16, "h2T")
        w2s = [kb.sb([128, 4, D], BF16, f"w2s{i}") for i in range(2)]
        rls = [kb.sb([128, NB], F32, f"rl{i}") for i in range(2)]
        aT = [kb.sb([128, NB], BF16, f"aT{i}") for i in range(3)]
        yo = kb.sb([128, D], F32, "yo")
        ai = 0
        out_deps = []
        for blk0 in range(0, nown, TB):
            t0 = blk0 * 128
            kb.dma("sp", oat[:, :, :], oa_d[:, :, t0:t0 + NB], writes=[oat])
            for t in range(TB):
                kb.dma("sp", mdt[:, t, :], mdn_d[blk0 + t, :, :], writes=[mdt])
            for t in range(TB):
                m = blk0 + t
                kb.dma("sp", xo[:, :], xown_d[m, :, :], writes=[xo])
                for half in range(2):
                    cs = slice(half * 512, half * 512 + 512)
                    pM = kb.bank()
                    for hh in range(8):
                        kb.mm(pM, pM[:, :], oat, oat[:, hh, t * 128:(t + 1) * 128], woab, woab[:, hh, cs], start=(hh == 0), stop=False)
                    for hd in range(4):
                        kb.mm(pM, pM[:, :], mdt, mdt[:, t, hd * 128:(hd + 1) * 128], wodb, wodb[:, hd, cs], start=False, stop=(hd == 3))
                    kb.tt("dve", x1b, x1b[:, t, cs], pM, pM[:, :], G1, G1[:, cs], ALU.mult)
                kb.tt("dve", x1b, x1b[:, t, :], x1b, x1b[:, t, :], xo, xo[:, :], ALU.add)
                kb.act(junkD, junkD[:, :], x1b, x1b[:, t, :], AF.Square, accum=ssqD[:, :], extra_writes=[ssqD])
                kb.act(rstdD, rstdD[:, :], ssqD, ssqD[:, :], AF.Ln, bias=EPS, scale=1.0 / D)
                kb.act(rstdD, rstdD[:, :], rstdD, rstdD[:, :], AF.Exp, scale=-0.5)
                kb.stt(t32D, t32D[:, :], x1b, x1b[:, t, :], rstdD[:, 0:1], S2, S2[:, :], ALU.mult, ALU.mult, extra_reads=[rstdD])
                kb.tt("dve", hbD, hbD[:, :], t32D, t32D[:, :], H2, H2[:, :], ALU.add)
                pb = kb.bank()
                pbv = pb.t[:, :].bitcast(BF16)
                for k in range(8):
                    kb.tr(pb, pbv[:, k * 128:(k + 1) * 128], hbD, hbD[:, k * 128:(k + 1) * 128], identb, identb[:, :])
                kb.cp("act", h2T, h2T[:, :, t * 128:(t + 1) * 128], pb, pbv.rearrange("p (k t) -> p k t", k=8))
            acc = [[kb.reserve() for _ in range(2)] for _ in range(TB)]
            def mlp1(f):
                pA1_ = kb.bank()
                for k in range(8):
                    kb.mm(pA1_, pA1_[:, 0:NB], w1b, w1b[:, k, f * 128:(f + 1) * 128], h2T, h2T[:, k, :], start=(k == 0), stop=(k == 7))
                return pA1_

            pendA = [mlp1(0)]
            for f in range(32):
                fg, fj = f // 4, f % 4
                ws = w2s[fg % 2]
                if fj == 0:
                    kb.dma("sp", ws[:, :, :], w2b_d[:, fg * 4:(fg + 1) * 4, :], writes=[ws])
                pA1 = pendA.pop(0)
                r_ = rls[f % 2]
                kb.act(r_, r_[:, :], pA1, pA1[:, 0:NB], AF.Relu)
                a_ = aT[ai % 3]
                ai += 1
                kb.tt("dve", a_, a_[:, :], r_, r_[:, :], r_, r_[:, :], ALU.mult)
                if f + 1 < 32:
                    pendA.append(mlp1(f + 1))
                for t in range(TB):
                    for half in range(2):
                        kb.mm(acc[t][half], acc[t][half][:, :], a_, a_[:, t * 128:(t + 1) * 128], ws, ws[:, fj, half * 512:half * 512 + 512],
                              start=(f == 0), stop=(f == 31))
            for t in range(TB):
                m = blk0 + t
                for half in range(2):
                    cs = slice(half * 512, half * 512 + 512)
                    kb.tt("dve", yo, yo[:, cs], acc[t][half], acc[t][half][:, :], G2, G2[:, cs], ALU.mult)
                    kb.release(acc[t][half])
                kb.tt("dve", yo, yo[:, :], yo, yo[:, :], x1b, x1b[:, t, :], ALU.add)
                out_deps.append(kb.dma("sp", out_d[m, :, :], yo[:, :], reads=[yo]))
        for dd in out_deps:
            kb._wait("sp", dd)
        for dd in list(yo.r.values()):
            kb._wait("sp", dd)
        barrier()
        esD.close()
        kb.es = es
        return nc, kb, es


def _rep(v, n=128):
    v = np.asarray(v, np.float32)
    return np.ascontiguousarray(np.broadcast_to(v[None, :], (n, v.shape[0])))


def _win_perm(dirs=(0, 1)):
    perm = list(range(0, 768))
    for base in (1280, 1792, 768):
        for h in range(4):
            perm += list(range(base + h * 128, base + (h + 1) * 128))
    perm += list(range(2304, 2816))
    for base in (2816, 2824):
        for dr in dirs:
            perm += [base + dr * 4 + h for h in range(4)]
    return np.array(perm)


def _halo_rows(seq, t0):
    out = np.zeros((4, seq.shape[1]), np.float32)
    fl = 1.0 if t0 - 2 >= 0 else 0.0
    fr = 1.0 if t0 + 130 <= seq.shape[0] else 0.0
    if fl:
        out[0:2] = seq[t0 - 2:t0]
    if fr:
        out[2:4] = seq[t0 + 128:t0 + 130]
    return out, fl, fr


def _rope_table(tile):
    out = np.zeros((128, 1280), np.float32)
    if tile is None:
        out[:, 0:640] = 1.0
        return out
    t = tile * 128 + np.arange(128)
    pos = [(t // 64).astype(np.float32), (t % 64).astype(np.float32)]
    inv = (np.float32(10000.0) ** (-np.arange(16, dtype=np.float32) / np.float32(16))).astype(np.float32)
    C = np.zeros((128, 64), np.float32)
    S = np.zeros((128, 64), np.float32)
    for blk in range(2):
        ang = (pos[blk][:, None] * inv[None, :]).astype(np.float32)
        c, sn = np.cos(ang).astype(np.float32), np.sin(ang).astype(np.float32)
        C[:, blk * 32:blk * 32 + 16] = c
        C[:, blk * 32 + 16:blk * 32 + 32] = c
        S[:, blk * 32:blk * 32 + 16] = -sn
        S[:, blk * 32 + 16:blk * 32 + 32] = sn
    out[:, 0:640] = np.tile(C, (1, 10))
    out[:, 640:1280] = np.tile(S, (1, 10))
    return out


def _const_tables(dirs):
    p = np.arange(128)[:, None]
    f = np.arange(128)[None, :]
    tri = np.zeros((128, 2, 128), np.float32)
    masks = np.zeros((128, 4, 512), np.float32)
    for ds, dr in enumerate(dirs):
        if dr == 0:
            tri[:, ds, :] = (p <= f)
            okS, okIT = (p > f), (f >= p)
        else:
            tri[:, ds, :] = (p >= f)
            okS, okIT = (p < f), (f <= p)
        masks[:, ds * 2 + 0, :] = np.tile(np.where(okS, 0.0, BIG), (1, 4))
        masks[:, ds * 2 + 1, :] = np.tile(np.where(okIT, 0.0, -BIG), (1, 4))
    im = np.zeros((128, 4, 512), np.float32)
    im[:, 0, :] = np.tile((p // 16) == (f // 16), (1, 4))
    for mi, sz in ((1, 16), (2, 32), (3, 64)):
        same = (p // (2 * sz)) == (f // (2 * sz))
        im[:, mi, :] = np.tile(same & (((p % (2 * sz)) >= sz) != ((f % (2 * sz)) >= sz)), (1, 4))
    return tri, masks, im


def _plan(i):
    own_lo = 16 * i
    fwd_pre = list(range(0, own_lo))
    bwd_pre = list(range(63, own_lo + 15, -1))
    fwd_own = list(range(own_lo, own_lo + 16))
    bwd_own = fwd_own[::-1]
    fwd_ctx, bwd_ctx = [0, 1], [1, 0]
    if i < 2:
        dirs = (0, 1)
        s1_ctx, s1_pre, s1_own = fwd_ctx, fwd_pre, fwd_own
        s2_ctx, s2_pre, s2_own = bwd_ctx, bwd_pre, bwd_own
    else:
        dirs = (1, 0)
        s1_ctx, s1_pre, s1_own = bwd_ctx, bwd_pre, bwd_own
        s2_ctx, s2_pre, s2_own = fwd_ctx, fwd_pre, fwd_own
    scan1 = [("ctx", t, 1.0) for t in s1_ctx]
    scan1 += [("lat", 0, 0.0)] * (16 - len(s1_pre)) + [("lat", t, 1.0) for t in s1_pre]
    scan1 += [("lat", t, 1.0) for t in s1_own]
    scan2 = [("ctx", t, 1.0) for t in s2_ctx]
    scan2 += [("lat", t, 0.0) for t in s1_pre] if len(s2_pre) < 48 else []
    scan2 += [("lat", t, 1.0) for t in s2_pre]
    scan2 += [("lat", t, 1.0) for t in s2_own]
    assert len(scan1) == N1 and len(scan2) == N2, (len(scan1), len(scan2))
    assert sorted(t for k, t, _ in scan2 if k == "lat") == list(range(64))
    return dirs, scan1, scan2, s2_own


def kernel(**inputs):
    set_cfg(16, 48, 16)
    g = lambda k: np.asarray(inputs[k], np.float32)
    x, c, ctx, c_ctx = g("x"), g("c"), g("ctx"), g("c_ctx")
    w_mod, w_in, conv_w = g("w_mod")[0], g("w_in")[0], g("conv_w")[0]
    w_out, w1, w2 = g("w_out")[0], g("w_mlp1")[0], g("w_mlp2")[0]
    a_log, dt_bias = g("a_log")[0], g("dt_bias")[0]

    shared = {
        "wmod": np.ascontiguousarray(w_mod.reshape(8, 128, 6, D).transpose(2, 1, 0, 3).reshape(6, 128, 8 * D)),
        "bmod": _rep(g("b_mod")[0]), "n1w": _rep(g("norm1_w")[0]), "n2w": _rep(g("norm2_w")[0]),
        "qkw": _rep(np.concatenate([np.tile(g("q_norm_w")[0], 8), np.tile(g("k_norm_w")[0], 2)])),
        "dnw": _rep(np.tile(g("dn_norm_w")[0], 4)),
        "woa": np.ascontiguousarray(w_out[0:512].reshape(8, 64, D).transpose(1, 0, 2)),
        "wod": np.ascontiguousarray(w_out[512:1024].reshape(4, 128, D).transpose(1, 0, 2)),
        "w1": np.ascontiguousarray(w1.reshape(8, 128, 4 * D).transpose(1, 0, 2)),
        "w2": np.ascontiguousarray(w2.reshape(32, 128, D).transpose(1, 0, 2)),
        "ident": np.eye(128, dtype=np.float32),
    }
    cw = np.zeros((128, 12, 5), np.float32)
    blk = 0
    for base in (512, 1024, 0):
        for h in range(4):
            cw[:, blk, :] = conv_w[:, base + h * 128:base + (h + 1) * 128].T
            blk += 1
    shared["convw"] = np.ascontiguousarray(cw.reshape(128, 60))
    per_dirs = {}
    for dirs in ((0, 1), (1, 0)):
        tri, masks, im = _const_tables(dirs)
        per_dirs[dirs] = {
            "tri": tri, "masks": masks, "imask": im,
            "win": np.ascontiguousarray(w_in[:, _win_perm(dirs)].reshape(8, 128, NCOL).transpose(1, 0, 2)),
            "alog": _rep(np.concatenate([a_log[dirs[0]], a_log[dirs[1]]])),
            "dtb": _rep(np.concatenate([dt_bias[dirs[0]], dt_bias[dirs[1]]])),
        }
    rope_cache = {t: _rope_table(t) for t in list(range(64)) + [None]}

    in_maps, own_maps = [], []
    for core in range(8):
        b, i = core // 4, core % 4
        dirs, scan1, scan2, own_tiles = _plan(i)
        feed = dict(shared)
        feed.update(per_dirs[dirs])
        feed["cc"] = np.ascontiguousarray(np.concatenate([c[b].reshape(8, 128).T, c_ctx.reshape(8, 128).T], axis=1))
        flags = np.zeros((128, 2, N2, 4), np.float32)
        seqs = {"ctx": ctx[b], "lat": x[b]}
        for ds, (scan, n) in enumerate(((scan1, N1), (scan2, N2))):
            xs = np.zeros((n, 128, D), np.float32)
            xh = np.zeros((n, 4, D), np.float32)
            for vis, (kind, t, gflag) in enumerate(scan):
                seq = seqs[kind]
                xs[vis] = seq[t * 128:(t + 1) * 128]
                xh[vis], fl, fr = _halo_rows(seq, t * 128)
                flags[:, ds, vis, 0:3] = (fl, fr, gflag)
            feed["xs%d" % (ds + 1)] = xs
            feed["xh%d" % (ds + 1)] = xh
        feed["flags"] = flags
        feed["rope2"] = np.stack([rope_cache[None if kind == "ctx" else t] for kind, t, _ in scan2])
        feed["xown"] = np.stack([x[b, t * 128:(t + 1) * 128] for t in own_tiles])
        in_maps.append(feed)
        own_maps.append((b, own_tiles))

    nc, kb, _ = build_program()
    res = run_bass_kernel_spmd(nc, in_maps, core_ids=list(range(8)))
    out = np.zeros((2, T, D), np.float32)
    for core, (b, own_tiles) in enumerate(own_maps):
        o = np.asarray(res.results[core]["out"], np.float32)
        for m, t in enumerate(own_tiles):
            out[b, t * 128:(t + 1) * 128] = o[m]
    return out
```

```python
import numpy as np
from contextlib import ExitStack
import concourse.bass as bass
import concourse.mybir as mybir
from concourse.bass_utils import run_bass_kernel_spmd

F32 = mybir.dt.float32
BF16 = mybir.dt.bfloat16
AF = mybir.ActivationFunctionType
ALU = mybir.AluOpType
AX = mybir.AxisListType

D = 1024
T = 8192
NCTX = 256
NT = T // 128
QT = 16
EPS = 1e-6
BIG = 1.0e9
DEBUG = False
SAME_ENGINE_SYNC = True


class Buf:
    __slots__ = ("t", "name", "w", "r", "dsem", "dcnt")

    def __init__(self, t, name):
        self.t = t
        self.name = name
        self.w = None
        self.r = {}
        self.dsem = None
        self.dcnt = 0

    def __getitem__(self, k):
        return self.t[k]


class KB:
    def __init__(self, nc, es):
        self.nc = nc
        self.es = es
        self.eng = {"pe": nc.tensor, "act": nc.scalar, "dve": nc.vector, "pool": nc.gpsimd, "sp": nc.sync}
        self.sem = {e: es.enter_context(nc.semaphore("sem_" + e)) for e in ("pe", "act", "dve", "pool")}
        self.cnt = {e: 0 for e in self.sem}
        self.seen = {e: {} for e in self.eng}
        self.nbuf = 0
        self.banks = []
        self.bank_i = 0
        self.ninst = 0
        self.reserved = set()

    def sb(self, shape, dtype, name=None):
        self.nbuf += 1
        name = "s_" + (name or f"b{self.nbuf}")
        return Buf(self.es.enter_context(self.nc.sbuf_tensor(name, list(shape), dtype)), name)

    def init_psum(self):
        for i in range(8):
            t = self.es.enter_context(self.nc.psum_tensor(f"psb{i}", [128, 512], F32))
            self.banks.append(Buf(t, f"psb{i}"))

    def bank(self):
        while True:
            b = self.banks[self.bank_i % 8]
            self.bank_i += 1
            if b.name not in self.reserved:
                return b

    def reserve(self):
        b = self.bank()
        self.reserved.add(b.name)
        return b

    def release(self, b):
        self.reserved.discard(b.name)

    def _wait(self, e, dep):
        if dep is None:
            return
        kind = dep[0]
        if kind == "dma":
            key = ("dma", dep[3])
            semh = dep[1]
            val = dep[2]
        else:
            if kind == e and (e == "pe" or not SAME_ENGINE_SYNC):
                return
            key = kind
            semh = self.sem[kind]
            val = dep[1]
        if self.seen[e].get(key, 0) >= val:
            return
        self.eng[e].wait_ge(semh, val)
        self.seen[e][key] = val

    def _deps(self, e, reads, writes):
        for b in reads:
            self._wait(e, b.w)
        for b in writes:
            self._wait(e, b.w)
            for d in list(b.r.values()):
                self._wait(e, d)

    def op(self, e, fn, reads=(), writes=()):
        self._deps(e, reads, writes)
        ins = fn(self.eng[e])
        self.cnt[e] += 1
        self.ninst += 1
        ins.then_inc(self.sem[e], 1)
        dep = (e, self.cnt[e])
        for b in reads:
            b.r[e] = dep
        for b in writes:
            b.w = dep
            b.r = {}
        return ins

    def dma(self, q, out, in_, reads=(), writes=()):
        b = writes[0] if writes else reads[0]
        if b.dsem is None:
            b.dsem = self.es.enter_context(self.nc.semaphore("dsem_" + b.name))
        if b.dcnt > 0:
            self._wait(q, ("dma", b.dsem, b.dcnt, b.name))
        self._deps(q, reads, writes)
        ins = self.eng[q].dma_start(out=out, in_=in_)
        b.dcnt += 16
        self.ninst += 1
        ins.then_inc(b.dsem, 16)
        dep = ("dma", b.dsem, b.dcnt, b.name)
        for x in writes:
            x.w = dep
            x.r = {}
        for x in reads:
            x.r[("dma", b.name)] = dep
        return dep

    def mm(self, out_b, out_ap, lhsT_b, lhsT_ap, rhs_b, rhs_ap, start=True, stop=True):
        return self.op("pe", lambda g: g.matmul(out_ap, lhsT_ap, rhs_ap, start=start, stop=stop),
                       reads=[lhsT_b, rhs_b], writes=[out_b])

    def tr(self, out_b, out_ap, in_b, in_ap, id_b, id_ap):
        return self.op("pe", lambda g: g.transpose(out_ap, in_ap, id_ap), reads=[in_b, id_b], writes=[out_b])

    def act(self, out_b, out_ap, in_b, in_ap, func, bias=None, scale=None, accum=None, extra_reads=(), extra_writes=()):
        kw = {}
        if bias is not None:
            kw["bias"] = bias
        if scale is not None:
            kw["scale"] = scale
        if accum is not None:
            kw["accum_out"] = accum
        return self.op("act", lambda g: g.activation(out_ap, in_ap, func, **kw),
                       reads=[in_b] + list(extra_reads), writes=[out_b] + list(extra_writes))

    def tt(self, e, out_b, out_ap, a_b, a_ap, b_b, b_ap, op):
        return self.op(e, lambda g: g.tensor_tensor(out_ap, a_ap, b_ap, op), reads=[a_b, b_b], writes=[out_b])

    def ts(self, e, out_b, out_ap, a_b, a_ap, s1, s2, op0, op1=None, extra_reads=()):
        if op1 is None:
            f = lambda g: g.tensor_scalar(out_ap, a_ap, s1, None, op0)
        else:
            f = lambda g: g.tensor_scalar(out_ap, a_ap, s1, s2, op0, op1)
        return self.op(e, f, reads=[a_b] + list(extra_reads), writes=[out_b])

    def stt(self, out_b, out_ap, a_b, a_ap, scalar, b_b, b_ap, op0, op1, extra_reads=()):
        return self.op("dve", lambda g: g.scalar_tensor_tensor(out_ap, a_ap, scalar, b_ap, op0, op1),
                       reads=[a_b, b_b] + list(extra_reads), writes=[out_b])

    def cp(self, e, out_b, out_ap, in_b, in_ap):
        if e == "act":
            return self.op("act", lambda g: g.copy(out_ap, in_ap), reads=[in_b], writes=[out_b])
        return self.op(e, lambda g: g.tensor_copy(out_ap, in_ap), reads=[in_b], writes=[out_b])

    def memset(self, e, b, ap, val):
        return self.op(e, lambda g: g.memset(ap, val), writes=[b])


GOFF = 768
ZOFF = 768 + 1536
GATE = ZOFF + 512
NCOL = GATE + 16
CFG = {"pre1": 16, "pre2": 48, "own": 16}
N1 = 2 + CFG["pre1"] + CFG["own"]
N2 = 2 + CFG["pre2"] + CFG["own"]
DK = 128


def set_cfg(pre1, pre2, own):
    global N1, N2
    CFG.update(pre1=pre1, pre2=pre2, own=own)
    N1 = 2 + pre1 + own
    N2 = 2 + pre2 + own


def build_program():
    nc = bass.Bass("TRN2", target_bir_lowering=False)

    def din(name, shape, dt=F32):
        return nc.dram_tensor(name, list(shape), dt, kind="ExternalInput").ap()

    xs_d = [din("xs1", [N1, 128, D]), din("xs2", [N2, 128, D])]
    xh_d = [din("xh1", [N1, 4, D]), din("xh2", [N2, 4, D])]
    rope_d = din("rope2", [N2, 128, 1280])
    flg_d = din("flags", [128, 2, N2, 4])
    cc_d = din("cc", [128, 16])
    wmod_d = din("wmod", [6, 128, 8 * D])
    bmod_d = din("bmod", [128, 6 * D])
    n1w_d = din("n1w", [128, D])
    n2w_d = din("n2w", [128, D])
    win_d = din("win", [128, 8, NCOL])
    qkw_d = din("qkw", [128, 640])
    cw_d = din("convw", [128, 60])
    alog_d = din("alog", [128, 8])
    dtb_d = din("dtb", [128, 8])
    dnw_d = din("dnw", [128, 512])
    woa_d = din("woa", [64, 8, D])
    wod_d = din("wod", [128, 4, D])
    w1_d = din("w1", [128, 8, 4 * D])
    w2_d = din("w2", [128, 32, D])
    xown_d = din("xown", [CFG["own"], 128, D])
    ident_d = din("ident", [128, 128])
    tri_d = din("tri", [128, 2, 128])
    msk_d = din("masks", [128, 4, 512])
    imk_d = din("imask", [128, 4, 512])
    out_d = nc.dram_tensor("out", [CFG["own"], 128, D], F32, kind="ExternalOutput").ap()

    kT_d = nc.dram_tensor("kT_s", [128, N2 * 128], BF16).ap()
    v_d = nc.dram_tensor("v_s", [N2, 128, 130], BF16).ap()
    qT_d = nc.dram_tensor("qT_s", [4, 128, CFG["own"] * 128], BF16).ap()
    o1_d = nc.dram_tensor("o1_s", [CFG["own"], 128, 512], F32).ap()
    mdn_d = nc.dram_tensor("mdn_s", [CFG["own"], 128, 512], BF16).ap()
    mod_d = nc.dram_tensor("mod_s", [4, 128, D], F32).ap()
    oa_d = nc.dram_tensor("oa_s", [64, 8, CFG["own"] * 128], BF16).ap()
    w2b_d = nc.dram_tensor("w2b_s", [128, 32, D], BF16).ap()

    with ExitStack() as es:
        kb = KB(nc, es)
        kb.init_psum()

        def barrier():
            for e in ("pe", "act", "dve", "pool", "sp"):
                for o in ("pe", "act", "dve", "pool"):
                    if o != e:
                        kb._wait(e, (o, kb.cnt[o]))

        ident = kb.sb([128, 128], F32, "ident")
        identb = kb.sb([128, 128], BF16, "identb")
        onesf = kb.sb([128, 128], F32, "onesf")
        kb.dma("sp", ident[:, :], ident_d[:, :], writes=[ident])
        kb.cp("dve", identb, identb[:, :], ident, ident[:, :])
        kb.memset("dve", onesf, onesf[:, :], 1.0)

        S1 = kb.sb([128, D], F32, "S1")
        H1 = kb.sb([128, D], F32, "H1")
        S1c = kb.sb([128, D], F32, "S1c")
        H1c = kb.sb([128, D], F32, "H1c")

        with ExitStack() as esA:
            kbA = kb
            old_es = kb.es
            kb.es = esA
            cc = kb.sb([128, 16], F32, "cc")
            sg = kb.sb([128, 16], F32, "ccsg")
            Lc = kb.sb([128, 16, 128], F32, "Lc")
            bm = kb.sb([128, 6 * D], F32, "bm")
            nw = kb.sb([128, D], F32, "nw")
            stage = [kb.sb([128, 8 * D], F32, f"wmst{i}") for i in range(2)]
            res = kb.sb([128, D], F32, "modres")
            kb.dma("sp", cc[:, :], cc_d[:, :], writes=[cc])
            kb.dma("sp", bm[:, :], bmod_d[:, :], writes=[bm])
            kb.dma("sp", nw[:, :], n1w_d[:, :], writes=[nw])
            kb.act(sg, sg[:, :], cc, cc[:, :], AF.Exp, scale=-1.0)
            kb.ts("dve", sg, sg[:, :], sg, sg[:, :], 1.0, None, ALU.add)
            kb.op("dve", lambda g: g.reciprocal(sg[:, :], sg[:, :]), reads=[sg], writes=[sg])
            kb.tt("dve", sg, sg[:, :], sg, sg[:, :], cc, cc[:, :], ALU.mult)
            for j in range(16):
                kb.ts("dve", Lc, Lc[:, j, :], onesf, onesf[:, :], sg[:, j:j + 1], None, ALU.mult, extra_reads=[sg])

            def mod_block(blk, lofs, dst_fn):
                st = stage[blk % 2]
                for half in range(2):
                    pb = kb.bank()
                    for k in range(8):
                        kb.mm(pb, pb[:, :], Lc, Lc[:, lofs + k, :], st, st[:, k * D + half * 512:k * D + half * 512 + 512],
                              start=(k == 0), stop=(k == 7))
                    dst_fn(pb, half)

            for blk in range(6):
                st = stage[blk % 2]
                kb.dma("sp", st[:, :], wmod_d[blk, :, :], writes=[st])
                if blk == 3:
                    kb.dma("sp", nw[:, :], n2w_d[:, :], writes=[nw])

                def plain(dst):
                    def f(pb, half, dst=dst, blk=blk):
                        sl = slice(half * 512, half * 512 + 512)
                        kb.tt("dve", dst, dst[:, sl], pb, pb[:, :], bm, bm[:, blk * D + half * 512:blk * D + half * 512 + 512], ALU.add)
                    return f

                def scale(dst):
                    def f(pb, half, dst=dst, blk=blk):
                        sl = slice(half * 512, half * 512 + 512)
                        kb.tt("dve", dst, dst[:, sl], pb, pb[:, :], bm, bm[:, blk * D + half * 512:blk * D + half * 512 + 512], ALU.add)
                        kb.stt(dst, dst[:, sl], dst, dst[:, sl], 1.0, nw, nw[:, sl], ALU.add, ALU.mult)
                    return f

                def spill(idx, fn_maker):
                    def f(pb, half, idx=idx):
                        fn_maker(res)(pb, half)
                        if half == 1:
                            kb.dma("sp", mod_d[idx, :, :], res[:, :], reads=[res])
                    return f

                if blk == 0:
                    mod_block(blk, 0, plain(H1))
                    mod_block(blk, 8, plain(H1c))
                elif blk == 1:
                    mod_block(blk, 0, scale(S1))
                    mod_block(blk, 8, scale(S1c))
                elif blk == 2:
                    mod_block(blk, 0, spill(0, plain))
                elif blk == 3:
                    mod_block(blk, 0, spill(2, plain))
                elif blk == 4:
                    mod_block(blk, 0, spill(1, scale))
                else:
                    mod_block(blk, 0, spill(3, plain))
            if DEBUG:
                dbg = nc.dram_tensor("dbgA", [8, 128, D], F32, kind="ExternalOutput").ap()
                for n_, b_ in enumerate((S1, H1, S1c, H1c)):
                    dd = kb.dma("sp", dbg[n_, :, :], b_[:, :], reads=[b_])
                    kb._wait("sp", dd)
                tmpd = kb.sb([128, D], F32, "dbgtmp")
                for n_ in range(4):
                    kb.dma("sp", tmpd[:, :], mod_d[n_, :, :], writes=[tmpd])
                    dd = kb.dma("sp", dbg[4 + n_, :, :], tmpd[:, :], reads=[tmpd])
                    kb._wait("sp", dd)
            barrier()
            kb.es = old_es

        esB = ExitStack()
        kb.es = esB
        wb = kb.sb([128, 8, NCOL], BF16, "wb")
        cdiag = kb.sb([128, 12, 5, 128], BF16, "cdiag")
        flg = kb.sb([128, 2, N2, 4], F32, "flg")
        with ExitStack() as esW:
            old_es = kb.es
            kb.es = esW
            wst = kb.sb([128, 8, 708], F32, "wst")
            cw = kb.sb([128, 60], F32, "cw")
            for part in range(4):
                sl = slice(part * 708, (part + 1) * 708)
                kb.dma("sp", wst[:, :, :], win_d[:, :, sl], writes=[wst])
                kb.cp("dve" if part % 2 == 0 else "pool", wb, wb[:, :, sl], wst, wst[:, :, :])
            kb.dma("sp", cw[:, :], cw_d[:, :], writes=[cw])
            kb.dma("sp", flg[:, :, :, :], flg_d[:, :, :, :], writes=[flg])
            for blk in range(12):
                for j in range(5):
                    kb.ts("dve", cdiag, cdiag[:, blk, j, :], ident, ident[:, :], cw[:, blk * 5 + j:blk * 5 + j + 1], None,
                          ALU.mult, extra_reads=[cw])
            barrier()
            kb.es = old_es

        xt = [kb.sb([128, D], F32, f"xt{i}") for i in range(2)]
        xh = kb.sb([4, D], F32, "xh")
        junk = kb.sb([128, D], BF16, "junk")
        ssq = kb.sb([128, 1], F32, "ssq")
        rstd = kb.sb([128, 1], F32, "rstd")
        t32 = kb.sb([128, D], F32, "t32")
        hb = kb.sb([128, D], BF16, "hb")
        hT = kb.sb([128, 8, 132], BF16, "hT")
        preT = kb.sb([128, 12, 132], BF16, "preT")
        sgt = kb.sb([128, 512], F32, "sgt")
        class FS:
            def __init__(self, i):
                self.cv = kb.sb([128, 12 * 128], F32, f"cv{i}")
                for nm, w in (("eb", 4), ("beta", 4), ("xa", 4), ("ab", 4), ("en", 4), ("lp", 4), ("g_", 4), ("gcs", 8),
                              ("e_", 4), ("egl", 4), ("dgl", 4), ("el", 4), ("ngc", 4), ("negbeta", 4)):
                    setattr(self, nm, kb.sb([128, w], F32, f"{nm}{i}"))
                self.zs = kb.sb([128, 512], F32, f"zs{i}")

        fsets = [FS(0), FS(1)]

        def norm_rows(xbuf, np_, Sb, Hb):
            kb.act(junk, junk[0:np_, :], xbuf, xbuf[0:np_, :], AF.Square, accum=ssq[0:np_, :], extra_writes=[ssq])
            kb.act(rstd, rstd[0:np_, :], ssq, ssq[0:np_, :], AF.Ln, bias=EPS, scale=1.0 / D)
            kb.act(rstd, rstd[0:np_, :], rstd, rstd[0:np_, :], AF.Exp, scale=-0.5)
            kb.stt(t32, t32[0:np_, :], xbuf, xbuf[0:np_, :], rstd[0:np_, 0:1], Sb, Sb[0:np_, :], ALU.mult, ALU.mult,
                   extra_reads=[rstd])
            kb.tt("dve", hb, hb[0:np_, :], t32, t32[0:np_, :], Hb, Hb[0:np_, :], ALU.add)

        def norm_tile(ds, vis, xbuf, Sb, Hb):
            kb.dma("sp", xbuf[:, :], xs_d[ds][vis, :, :], writes=[xbuf])
            kb.dma("sp", xh[:, :], xh_d[ds][vis, :, :], writes=[xh])
            norm_rows(xbuf, 128, Sb, Hb)
            pb = kb.bank()
            pbv = pb.t[:, :].bitcast(BF16)
            for k in range(8):
                kb.tr(pb, pbv[:, k * 128:(k + 1) * 128], hb, hb[:, k * 128:(k + 1) * 128], identb, identb[:, :])
            kb.cp("act", hT, hT[:, :, 2:130], pb, pbv.rearrange("p (k t) -> p k t", k=8))
            norm_rows(xh, 4, Sb, Hb)
            pb2 = kb.bank()
            pbv2 = pb2.t[:, :].bitcast(BF16)
            for k in range(8):
                kb.tr(pb2, pbv2[:, k * 4:(k + 1) * 4], hb, hb[0:4, k * 128:(k + 1) * 128], identb, identb[0:4, 0:4])
            hv = pbv2[:, 0:32].rearrange("p (k t) -> p k t", k=8)
            kb.cp("act", hT, hT[:, :, 0:2], pb2, hv[:, :, 0:2])
            kb.cp("act", hT, hT[:, :, 130:132], pb2, hv[:, :, 2:4])

        def project_tok(col0, ncols):
            pb = kb.bank()
            for k in range(8):
                kb.mm(pb, pb[:, 0:ncols], hT, hT[:, k, 2:130], wb, wb[:, k, col0:col0 + ncols], start=(k == 0), stop=(k == 7))
            return pb

        def gdn_pre_conv(ds, vis, nblk, F):
            for g0 in range(0, nblk, 3):
                pb = kb.bank()
                for j in range(3):
                    blk = g0 + j
                    for k in range(8):
                        kb.mm(pb, pb[:, j * 132:(j + 1) * 132], wb, wb[:, k, GOFF + blk * 128:GOFF + (blk + 1) * 128],
                              hT, hT[:, k, :], start=(k == 0), stop=(k == 7))
                kb.cp("act", preT, preT[:, g0:g0 + 3, :], pb, pb[:, 0:396].rearrange("p (j t) -> p j t", j=3))
                yield
            kb.ts("dve", preT, preT[:, 0:nblk, 0:2], preT, preT[:, 0:nblk, 0:2], flg[:, ds, vis, 0:1], None, ALU.mult, extra_reads=[flg])
            kb.ts("dve", preT, preT[:, 0:nblk, 130:132], preT, preT[:, 0:nblk, 130:132], flg[:, ds, vis, 1:2], None, ALU.mult, extra_reads=[flg])
            for g0 in range(0, nblk, 4):
                pb = kb.bank()
                for j4 in range(4):
                    blk = g0 + j4
                    for j in range(5):
                        kb.mm(pb, pb[:, j4 * 128:(j4 + 1) * 128], preT, preT[:, blk, j:j + 128], cdiag, cdiag[:, blk, j, :],
                              start=(j == 0), stop=(j == 4))
                kb.act(sgt, sgt[:, :], pb, pb[:, :], AF.Exp, scale=-1.0)
                kb.ts("dve", sgt, sgt[:, :], sgt, sgt[:, :], 1.0, None, ALU.add)
                kb.op("dve", lambda g: g.reciprocal(sgt[:, :], sgt[:, :]), reads=[sgt], writes=[sgt])
                kb.tt("dve", F.cv, F.cv[:, g0 * 128:(g0 + 4) * 128], pb, pb[:, :], sgt, sgt[:, :], ALU.mult)
                yield

        tri = kb.sb([128, 2, 128], F32, "tri")
        mskb = kb.sb([128, 4, 512], BF16, "mskb")
        negA = kb.sb([128, 8], F32, "negA")
        dtb = kb.sb([128, 8], F32, "dtb")
        identrep = kb.sb([128, 512], F32, "identrep")
        imk = kb.sb([128, 4, 512], BF16, "imk")
        with ExitStack() as esG:
            old_es = kb.es
            kb.es = esG
            mst = kb.sb([128, 4, 512], F32, "mst")
            kb.dma("sp", tri[:, :, :], tri_d[:, :, :], writes=[tri])
            kb.dma("sp", mst[:, :, :], msk_d[:, :, :], writes=[mst])
            kb.cp("dve", mskb, mskb[:, :, :], mst, mst[:, :, :])
            kb.dma("sp", mst[:, :, :], imk_d[:, :, :], writes=[mst])
            kb.cp("dve", imk, imk[:, :, :], mst, mst[:, :, :])
            kb.dma("sp", negA[:, :], alog_d[:, :], writes=[negA])
            kb.dma("sp", dtb[:, :], dtb_d[:, :], writes=[dtb])
            kb.act(negA, negA[:, :], negA, negA[:, :], AF.Exp)
            kb.ts("dve", negA, negA[:, :], negA, negA[:, :], -1.0, None, ALU.mult)
            for h in range(4):
                kb.cp("dve", identrep, identrep[:, h * 128:(h + 1) * 128], ident, ident[:, :])
            barrier()
            kb.es = old_es

        def sm(name, w=4):
            return kb.sb([128, w], F32, name)

        HW = 256

        class CH:
            def __init__(self, hp):
                n = f"c{hp}"
                self.hp = hp
                for nm in ("ssk", "rk", "c1", "c2", "ssq_", "rq", "cq", "cqe"):
                    setattr(self, nm, kb.sb([128, 2], F32, n + nm))
                self.junk2 = kb.sb([128, 128], BF16, n + "junk2")
                for nm in ("kn", "vb", "kbg", "kd", "qs", "qd", "knT", "qsT", "qdT", "Nb", "Zb", "Nk", "Zk", "Xb", "Yb",
                           "Cb", "Eb", "U1b", "U2b", "wT", "vnew", "intraT"):
                    setattr(self, nm, kb.sb([128, HW], BF16, n + nm))
                for nm in ("dg", "Ds", "DT", "u_"):
                    setattr(self, nm, kb.sb([128, HW], F32, n + nm))

        chains = [CH(0), CH(1)]
        SstC = [[kb.sb([128, HW], F32, f"SstC{d_}{hp}") for hp in range(2)] for d_ in range(2)]
        SbfC = [[kb.sb([128, HW], BF16, f"SbfC{d_}{hp}") for hp in range(2)] for d_ in range(2)]
        for d_ in range(2):
            for hp in range(2):
                kb.memset("dve", SstC[d_][hp], SstC[d_][hp][:, :], 0.0)
                kb.memset("dve", SbfC[d_][hp], SbfC[d_][hp][:, :], 0.0)

        def H(hh):
            return slice(hh * 128, (hh + 1) * 128)

        def tr4(dst, src):
            pb = kb.bank()
            pv = pb.t[:, :].bitcast(BF16)
            for hh in range(4):
                kb.tr(pb, pv[:, H(hh)], src, src[:, H(hh)], identb, identb[:, :])
            kb.cp("act", dst, dst[:, :], pb, pv[:, 0:512])

        def tr2(dst, src, e="act"):
            pb = kb.bank()
            pv = pb.t[:, :].bitcast(BF16)
            for lh in range(2):
                kb.tr(pb, pv[:, H(lh)], src, src[:, H(lh)], identb, identb[:, :])
            kb.cp(e, dst, dst[:, :], pb, pv[:, 0:HW])

        def rsq(dst, src, add):
            kb.act(dst, dst[:, :], src, src[:, :], AF.Ln, bias=add)
            kb.act(dst, dst[:, :], dst, dst[:, :], AF.Exp, scale=-0.5)

        def gdn_gates(ds, vis, own, F):
            c4 = slice(ds * 4, ds * 4 + 4)
            a4 = slice(8 + ds * 4, 8 + ds * 4 + 4)
            pg = project_tok(GATE, 16)
            fg = flg[:, ds, vis, 2:3]
            kb.act(F.eb, F.eb[:, :], pg, pg[:, c4], AF.Exp, scale=-1.0)
            kb.ts("dve", F.eb, F.eb[:, :], F.eb, F.eb[:, :], 1.0, None, ALU.add)
            kb.op("dve", lambda g: g.reciprocal(F.beta[:, :], F.eb[:, :]), reads=[F.eb], writes=[F.beta])
            kb.ts("dve", F.beta, F.beta[:, :], F.beta, F.beta[:, :], fg, None, ALU.mult, extra_reads=[flg])
            kb.ts("dve", F.negbeta, F.negbeta[:, :], F.beta, F.beta[:, :], -1.0, None, ALU.mult)
            kb.tt("dve", F.xa, F.xa[:, :], pg, pg[:, a4], dtb, dtb[:, c4], ALU.add)
            kb.act(F.ab, F.ab[:, :], F.xa, F.xa[:, :], AF.Abs)
            kb.act(F.en, F.en[:, :], F.ab, F.ab[:, :], AF.Exp, scale=-1.0)
            kb.act(F.lp, F.lp[:, :], F.en, F.en[:, :], AF.Ln, bias=1.0)
            kb.stt(F.g_, F.g_[:, :], F.xa, F.xa[:, :], 0.0, F.lp, F.lp[:, :], ALU.max, ALU.add)
            kb.tt("dve", F.g_, F.g_[:, :], F.g_, F.g_[:, :], negA, negA[:, c4], ALU.mult)
            kb.ts("dve", F.g_, F.g_[:, :], F.g_, F.g_[:, :], fg, None, ALU.mult, extra_reads=[flg])
            pgc = kb.bank()
            kb.mm(pgc, pgc[:, 0:4], tri, tri[:, ds, :], F.g_, F.g_[:, :])
            kb.mm(pgc, pgc[:, 4:8], onesf, onesf[:, :], F.g_, F.g_[:, :])
            kb.cp("act", F.gcs, F.gcs[:, :], pgc, pgc[:, 0:8])
            kb.act(F.e_, F.e_[:, :], F.gcs, F.gcs[:, 0:4], AF.Exp)
            kb.act(F.egl, F.egl[:, :], F.gcs, F.gcs[:, 4:8], AF.Exp)
            kb.tt("dve", F.dgl, F.dgl[:, :], F.gcs, F.gcs[:, 4:8], F.gcs, F.gcs[:, 0:4], ALU.subtract)
            kb.act(F.el, F.el[:, :], F.dgl, F.dgl[:, :], AF.Exp)
            if own:
                kb.ts("dve", F.ngc, F.ngc[:, :], F.gcs, F.gcs[:, 0:4], -1.0, None, ALU.mult)

        def gdn_chain(ds, vis, own, C, res, F):
            hp = C.hp
            S, Sb = SstC[ds][hp], SbfC[ds][hp]
            hs = (2 * hp, 2 * hp + 1)
            g2 = slice(2 * hp, 2 * hp + 2)
            mk = slice(0, HW)

            def mm2(l_b, r_b, acc_b=None):
                pb_ = kb.bank()
                for lh in range(2):
                    kb.mm(pb_, pb_[:, H(lh)], l_b, l_b[:, H(lh)], r_b, r_b[:, H(lh)], start=True, stop=(acc_b is None))
                    if acc_b is not None:
                        kb.mm(pb_, pb_[:, H(lh)], identb, identb[:, :], acc_b, acc_b[:, H(lh)], start=False, stop=True)
                return pb_

            for lh, hh in enumerate(hs):
                kb.act(C.junk2, C.junk2[:, :], F.cv, F.cv[:, H(hh)], AF.Square, accum=C.ssk[:, lh:lh + 1], extra_writes=[C.ssk])
            rsq(C.rk, C.ssk, EPS)
            yield
            kb.tt("dve", C.c1, C.c1[:, :], C.rk, C.rk[:, :], F.beta, F.beta[:, g2], ALU.mult)
            kb.tt("dve", C.c1, C.c1[:, :], C.c1, C.c1[:, :], F.e_, F.e_[:, g2], ALU.mult)
            kb.tt("dve", C.c2, C.c2[:, :], C.rk, C.rk[:, :], F.el, F.el[:, g2], ALU.mult)
            yield
            def scl(dst, dcol, src_col, sc_b, sc_ap):
                kb.act(dst, dst[:, H(dcol)], F.cv, F.cv[:, H(src_col)], AF.Identity, scale=sc_ap, extra_reads=[sc_b])

            for lh, hh in enumerate(hs):
                scl(C.kn, lh, hh, C.rk, C.rk[:, lh:lh + 1])
                scl(C.vb, lh, 4 + hh, F.beta, F.beta[:, hh:hh + 1])
                scl(C.kbg, lh, hh, C.c1, C.c1[:, lh:lh + 1])
                scl(C.kd, lh, hh, C.c2, C.c2[:, lh:lh + 1])
                yield
            tr2(C.knT, C.kn)
            yield
            if own:
                for lh, hh in enumerate(hs):
                    kb.act(C.junk2, C.junk2[:, :], F.cv, F.cv[:, H(8 + hh)], AF.Square, accum=C.ssq_[:, lh:lh + 1], extra_writes=[C.ssq_])
                rsq(C.rq, C.ssq_, EPS)
                kb.ts("dve", C.cq, C.cq[:, :], C.rq, C.rq[:, :], float(DK) ** -0.5, None, ALU.mult)
                kb.tt("dve", C.cqe, C.cqe[:, :], C.cq, C.cq[:, :], F.e_, F.e_[:, g2], ALU.mult)
                yield
                for lh, hh in enumerate(hs):
                    scl(C.qs, lh, 8 + hh, C.cq, C.cq[:, lh:lh + 1])
                    scl(C.qd, lh, 8 + hh, C.cqe, C.cqe[:, lh:lh + 1])
                yield
                tr2(C.qsT, C.qs)
                yield
                tr2(C.qdT, C.qd)
                yield
            pA = mm2(C.knT, C.knT)
            for lh, hh in enumerate(hs):
                kb.act(C.dg, C.dg[:, H(lh)], ident, ident[:, :], AF.Identity, scale=F.gcs[:, hh:hh + 1], extra_reads=[F.gcs])
            pG = kb.bank()
            kb.mm(pG, pG[:, 0:HW], onesf, onesf[:, :], C.dg, C.dg[:, :], start=True, stop=False)
            kb.mm(pG, pG[:, 0:HW], identb, identb[:, :], mskb, mskb[:, ds * 2, mk], start=False, stop=True)
            yield
            for lh, hh in enumerate(hs):
                kb.act(C.Ds, C.Ds[:, H(lh)], pG, pG[:, H(lh)], AF.Exp, bias=F.gcs[:, hh:hh + 1], scale=-1.0, extra_reads=[F.gcs])
            yield
            for lh, hh in enumerate(hs):
                kb.stt(C.Nb, C.Nb[:, H(lh)], pA, pA[:, H(lh)], F.negbeta[:, hh:hh + 1], C.Ds, C.Ds[:, H(lh)], ALU.mult, ALU.mult,
                       extra_reads=[F.negbeta])
            yield
            tr2(C.Zb, C.Nb)
            yield
            kb.tt("pool", C.Nk, C.Nk[:, :], C.Nb, C.Nb[:, :], imk, imk[:, 0, mk], ALU.mult)
            kb.tt("pool", C.Zk, C.Zk[:, :], C.Zb, C.Zb[:, :], imk, imk[:, 0, mk], ALU.mult)
            yield
            kb.tt("dve", C.Xb, C.Xb[:, :], C.Nk, C.Nk[:, :], identrep, identrep[:, mk], ALU.add)
            kb.tt("dve", C.Yb, C.Yb[:, :], C.Zk, C.Zk[:, :], identrep, identrep[:, mk], ALU.add)
            yield
            for lvl in range(1, 4):
                pN = mm2(C.Zk, C.Nk)
                pZ = mm2(C.Nk, C.Zk)
                yield
                kb.cp("act", C.Nk, C.Nk[:, :], pN, pN[:, 0:HW])
                kb.cp("dve", C.Zk, C.Zk[:, :], pZ, pZ[:, 0:HW])
                yield
                pX = mm2(C.Yb, C.Nk, C.Xb)
                pP = mm2(C.Xb, C.Zk, C.Yb)
                yield
                kb.cp("act", C.Xb, C.Xb[:, :], pX, pX[:, 0:HW])
                kb.cp("dve", C.Yb, C.Yb[:, :], pP, pP[:, 0:HW])
                yield
            for mi in (1, 2, 3):
                last = mi == 3
                kb.tt("pool", C.Cb, C.Cb[:, :], C.Nb, C.Nb[:, :], imk, imk[:, mi, mk], ALU.mult)
                if not last:
                    kb.tt("pool", C.Eb, C.Eb[:, :], C.Zb, C.Zb[:, :], imk, imk[:, mi, mk], ALU.mult)
                yield
                if not last:
                    pU1 = mm2(C.Eb, C.Xb)
                pU2 = mm2(C.Cb, C.Yb)
                yield
                if not last:
                    kb.cp("dve", C.U1b, C.U1b[:, :], pU1, pU1[:, 0:HW])
                kb.cp("act", C.U2b, C.U2b[:, :], pU2, pU2[:, 0:HW])
                yield
                if not last:
                    pX = mm2(C.Yb, C.U1b, C.Xb)
                pP = mm2(C.Xb, C.U2b, C.Yb)
                yield
                if not last:
                    kb.cp("act", C.Xb, C.Xb[:, :], pX, pX[:, 0:HW])
                kb.cp("dve", C.Yb, C.Yb[:, :], pP, pP[:, 0:HW])
                yield
            pU = mm2(C.Yb, C.vb)
            pW = mm2(C.kbg, C.Yb)
            yield
            kb.cp("act", C.u_, C.u_[:, :], pU, pU[:, 0:HW])
            kb.cp("dve", C.wT, C.wT[:, :], pW, pW[:, 0:HW])
            yield
            pWS = mm2(C.wT, Sb)
            yield
            kb.tt("dve", C.vnew, C.vnew[:, :], C.u_, C.u_[:, :], pWS, pWS[:, 0:HW], ALU.subtract)
            yield
            if own:
                pGI = kb.bank()
                kb.mm(pGI, pGI[:, 0:HW], onesf, onesf[:, :], C.dg, C.dg[:, :], start=True, stop=False)
                kb.mm(pGI, pGI[:, 0:HW], identb, identb[:, :], mskb, mskb[:, ds * 2 + 1, mk], start=False, stop=True)
                pQK = mm2(C.knT, C.qsT)
                yield
                for lh, hh in enumerate(hs):
                    kb.act(C.DT, C.DT[:, H(lh)], pGI, pGI[:, H(lh)], AF.Exp, bias=F.ngc[:, hh:hh + 1], extra_reads=[F.ngc])
                yield
                kb.tt("dve", C.intraT, C.intraT[:, :], pQK, pQK[:, 0:HW], C.DT, C.DT[:, :], ALU.mult)
                yield
                pO = kb.reserve()
                for lh in range(2):
                    kb.mm(pO, pO[:, H(lh)], C.qdT, C.qdT[:, H(lh)], Sb, Sb[:, H(lh)], start=True, stop=False)
                    kb.mm(pO, pO[:, H(lh)], C.intraT, C.intraT[:, H(lh)], C.vnew, C.vnew[:, H(lh)], start=False, stop=True)
                res[hp] = pO
                yield
            pS = mm2(C.kd, C.vnew)
            yield
            for lh, hh in enumerate(hs):
                kb.stt(S, S[:, H(lh)], S, S[:, H(lh)], F.egl[:, hh:hh + 1], pS, pS[:, H(lh)], ALU.mult, ALU.add, extra_reads=[F.egl])
            kb.cp("act", Sb, Sb[:, :], S, S[:, :])
            yield

        qkw = kb.sb([128, 640], F32, "qkw")
        dnw = kb.sb([128, 512], F32, "dnw")
        kb.dma("sp", qkw[:, :], qkw_d[:, :], writes=[qkw])
        kb.dma("sp", dnw[:, :], dnw_d[:, :], writes=[dnw])
        rp = kb.sb([128, 1280], F32, "rp")
        sqt = kb.sb([128, 640], F32, "sqt")
        ss10 = kb.sb([128, 10], F32, "ss10")
        rr10 = kb.sb([128, 10], F32, "rr10")
        xw = kb.sb([128, 640], F32, "xw")
        sw = kb.sb([128, 640], F32, "sw")
        t1 = kb.sb([128, 640], F32, "t1")
        qkb = kb.sb([128, 640], BF16, "qkb")
        kTt = kb.sb([128, 128], BF16, "kTt")
        qTt = kb.sb([128, 512], BF16, "qTt")
        vt = kb.sb([128, 130], BF16, "vt")
        kb.memset("dve", vt, vt[:, :], 1.0)

        def attn_proj(vis, own, m):
            kb.dma("sp", rp[:, :], rope_d[vis, :, :], writes=[rp])
            c0 = 0 if own else 512
            nh0 = 0 if own else 8
            pKV = project_tok(512, 256)
            pQ = project_tok(0, 512) if own else None
            if own:
                kb.act(sqt, sqt[:, 0:512], pQ, pQ[:, :], AF.Square)
            kb.act(sqt, sqt[:, 512:640], pKV, pKV[:, 0:128], AF.Square)
            kb.op("dve", lambda g: g.tensor_reduce(ss10[:, nh0:10], sqt[:, c0:640].rearrange("p (h d) -> p h d", d=64), AX.X, ALU.add),
                  reads=[sqt], writes=[ss10])
            kb.act(rr10, rr10[:, nh0:10], ss10, ss10[:, nh0:10], AF.Ln, bias=EPS, scale=1.0 / 64)
            kb.act(rr10, rr10[:, nh0:10], rr10, rr10[:, nh0:10], AF.Exp, scale=-0.5)
            for j in range(nh0, 10):
                src_b, src_ap = (pQ, pQ[:, j * 64:(j + 1) * 64]) if j < 8 else (pKV, pKV[:, (j - 8) * 64:(j - 7) * 64])
                kb.ts("dve", xw, xw[:, j * 64:(j + 1) * 64], src_b, src_ap, rr10[:, j:j + 1], None, ALU.mult, extra_reads=[rr10])
            kb.tt("dve", xw, xw[:, c0:640], xw, xw[:, c0:640], qkw, qkw[:, c0:640], ALU.mult)
            yield
            xv = xw[:, c0:640].rearrange("p (n two s) -> p n two s", two=2, s=16)
            sv = sw[:, c0:640].rearrange("p (n two s) -> p n two s", two=2, s=16)
            kb.cp("pool", sw, sv[:, :, 0, :], xw, xv[:, :, 1, :])
            kb.cp("pool", sw, sv[:, :, 1, :], xw, xv[:, :, 0, :])
            kb.tt("dve", t1, t1[:, c0:640], xw, xw[:, c0:640], rp, rp[:, c0:640], ALU.mult)
            kb.tt("dve", sw, sw[:, c0:640], sw, sw[:, c0:640], rp, rp[:, 640 + c0:1280], ALU.mult)
            kb.tt("dve", qkb, qkb[:, c0:640], t1, t1[:, c0:640], sw, sw[:, c0:640], ALU.add)
            yield
            pb = kb.bank()
            pv = pb.t[:, :].bitcast(BF16)
            kb.tr(pb, pv[:, 0:128], qkb, qkb[:, 512:640], identb, identb[:, :])
            kb.cp("act", kTt, kTt[:, :], pb, pv[:, 0:128])
            kb.dma("sp", kT_d[:, vis * 128:(vis + 1) * 128], kTt[:, :], reads=[kTt])
            kb.cp("act", vt, vt[:, :].rearrange("p (k c) -> p k c", k=2)[:, :, 0:64], pKV,
                  pKV[:, 128:256].rearrange("p (k c) -> p k c", k=2))
            kb.dma("sp", v_d[vis, :, :], vt[:, :], reads=[vt])
            yield
            if own:
                tr4(qTt, qkb)
                for pr in range(4):
                    kb.dma("sp", qT_d[pr, :, m * 128:(m + 1) * 128], qTt[:, H(pr)], reads=[qTt])

        o1t = kb.sb([128, 512], F32, "o1t")
        junk2 = kb.sb([128, 128], BF16, "junk2o")
        o1r = kb.sb([128, 512], F32, "o1r")
        osum = kb.sb([128, 512], F32, "osum")
        ssd, rsd = sm("ssd"), sm("rsd")
        od = kb.sb([128, 512], BF16, "od")
        odT = kb.sb([128, 512], BF16, "odT")

        def own_out_scan1(pOs, m):
            for hp in range(2):
                kb.cp("act" if hp == 0 else "dve", o1t, o1t[:, hp * HW:(hp + 1) * HW], pOs[hp], pOs[hp][:, 0:HW])
                kb.release(pOs[hp])
            kb.dma("sp", o1_d[m, :, :], o1t[:, :], reads=[o1t])

        def own_out_scan2(pOs, m, F):
            zs = F.zs
            kb._wait("sp", o1t.r.get(("dma", o1t.name)))
            kb.dma("sp", o1r[:, :], o1_d[m, :, :], writes=[o1r])
            for hp in range(2):
                kb.tt("dve", osum, osum[:, hp * HW:(hp + 1) * HW], o1r, o1r[:, hp * HW:(hp + 1) * HW], pOs[hp], pOs[hp][:, 0:HW], ALU.add)
                kb.release(pOs[hp])
            for hh in range(4):
                kb.act(junk2, junk2[:, :], osum, osum[:, H(hh)], AF.Square, accum=ssd[:, hh:hh + 1], extra_writes=[ssd])
            kb.act(rsd, rsd[:, :], ssd, ssd[:, :], AF.Ln, bias=EPS, scale=1.0 / 128)
            kb.act(rsd, rsd[:, :], rsd, rsd[:, :], AF.Exp, scale=-0.5)
            for hh in range(4):
                kb.stt(od, od[:, H(hh)], osum, osum[:, H(hh)], rsd[:, hh:hh + 1], zs, zs[:, H(hh)], ALU.mult, ALU.mult,
                       extra_reads=[rsd])
            tr4(odT, od)
            kb.dma("sp", mdn_d[m, :, :], odT[:, :], reads=[odT])

        nown, pre1, pre2 = CFG["own"], CFG["pre1"], CFG["pre2"]

        def vinfo(ds, vis):
            npre = pre1 if ds == 0 else pre2
            return vis < 2, vis >= 2 + npre, vis - 2 - npre

        def front(ds, vis, F):
            isctx, own, sidx = vinfo(ds, vis)
            norm_tile(ds, vis, xt[ds], S1c if isctx else S1, H1c if isctx else H1)
            yield
            if ds == 1:
                yield from attn_proj(vis, own, sidx if own else 0)
                if own:
                    pZ = project_tok(ZOFF, 512)
                    kb.act(F.zs, F.zs[:, :], pZ, pZ[:, :], AF.Exp, scale=-1.0)
                    kb.ts("dve", F.zs, F.zs[:, :], F.zs, F.zs[:, :], 1.0, None, ALU.add)
                    kb.op("dve", lambda g: g.reciprocal(F.zs[:, :], F.zs[:, :]), reads=[F.zs], writes=[F.zs])
                    kb.tt("dve", F.zs, F.zs[:, :], pZ, pZ[:, :], F.zs, F.zs[:, :], ALU.mult)
                    kb.tt("dve", F.zs, F.zs[:, :], F.zs, F.zs[:, :], dnw, dnw[:, :], ALU.mult)
                    yield
            yield from gdn_pre_conv(ds, vis, 12 if own else 8, F)
            gdn_gates(ds, vis, own, F)
            yield

        def run_all(gens):
            gens = list(gens)
            while gens:
                for g in list(gens):
                    try:
                        next(g)
                    except StopIteration:
                        gens.remove(g)

        order = []
        v1 = 0
        n2_head = 2 + pre2
        for v2 in range(N2):
            if v2 == n2_head:
                while v1 < N1:
                    order.append((0, v1))
                    v1 += 1
            order.append((1, v2))
            while v1 < N1 and v1 * n2_head < (v2 + 1) * N1 and v2 < n2_head:
                order.append((0, v1))
                v1 += 1
        assert len(order) == N1 + N2

        run_all([front(order[0][0], order[0][1], fsets[0])])
        for n_, (ds, vis) in enumerate(order):
            F = fsets[n_ % 2]
            isctx, own, sidx = vinfo(ds, vis)
            res = [None, None]
            gens = [gdn_chain(ds, vis, own, chains[0], res, F), gdn_chain(ds, vis, own, chains[1], res, F)]
            if n_ + 1 < len(order):
                gens.append(front(order[n_ + 1][0], order[n_ + 1][1], fsets[(n_ + 1) % 2]))
            run_all(gens)
            if own and ds == 0:
                own_out_scan1(res, nown - 1 - sidx)
            if own and ds == 1:
                own_out_scan2(res, sidx, F)
        barrier()
        for e in ("sp",):
            for bb in (kTt, vt, qTt, odT, o1t):
                for dd in list(bb.r.values()):
                    kb._wait(e, dd)

        if DEBUG:
            dbgM = nc.dram_tensor("dbgM", [nown, 128, 512], BF16, kind="ExternalOutput").ap()
            dbgK = nc.dram_tensor("dbgK", [128, N2 * 128], BF16, kind="ExternalOutput").ap()
            dbgV = nc.dram_tensor("dbgV", [N2, 128, 130], BF16, kind="ExternalOutput").ap()
            dbgQ = nc.dram_tensor("dbgQ", [4, 128, nown * 128], BF16, kind="ExternalOutput").ap()
            tb = kb.sb([128, N2 * 128], BF16, "dbgtb")
            for src, dst, shp in ((kT_d[:, :], dbgK[:, :], None),):
                kb.dma("sp", tb[:, :], src, writes=[tb])
                dd = kb.dma("sp", dst, tb[:, :], reads=[tb]); kb._wait("sp", dd)
            for m in range(nown):
                kb.dma("sp", tb[:, 0:512], mdn_d[m, :, :], writes=[tb])
                dd = kb.dma("sp", dbgM[m, :, :], tb[:, 0:512], reads=[tb]); kb._wait("sp", dd)
            for v in range(N2):
                kb.dma("sp", tb[:, 0:130], v_d[v, :, :], writes=[tb])
                dd = kb.dma("sp", dbgV[v, :, :], tb[:, 0:130], reads=[tb]); kb._wait("sp", dd)
            for pr in range(4):
                kb.dma("sp", tb[:, 0:nown * 128], qT_d[pr, :, :], writes=[tb])
                dd = kb.dma("sp", dbgQ[pr, :, :], tb[:, 0:nown * 128], reads=[tb]); kb._wait("sp", dd)
        barrier()
        esB.close()
        kb.es = es

        nq = nown * 128
        NK = N2
        esCD = ExitStack()
        kb.es = esCD
        w1b = kb.sb([128, 8, 4 * D], BF16, "w1b")
        woab = kb.sb([64, 8, D], BF16, "woab")
        wodb = kb.sb([128, 4, D], BF16, "wodb")
        esW2 = ExitStack()
        kb.es = esW2
        wst2 = [kb.sb([128, 2048], F32, f"wst2_{i}") for i in range(2)]
        w2c = kb.sb([128, 2048], BF16, "w2c")
        kb.es = esCD
        ci = 0

        def conv_chunk(dst_b, dst_ap, src_ap, np_=128):
            nonlocal ci
            st = wst2[ci % 2]
            e = "pool" if ci % 2 == 0 else "dve"
            ci += 1
            kb.dma("sp", st[0:np_, :], src_ap, writes=[st])
            kb.cp(e, dst_b, dst_ap, st, st[0:np_, :])

        def emit_weight_prep():
            for k in range(8):
                for hf in range(2):
                    conv_chunk(w1b, w1b[:, k, hf * 2048:(hf + 1) * 2048], w1_d[:, k, hf * 2048:(hf + 1) * 2048])
            for c in range(4):
                conv_chunk(woab, woab[:, c * 2:(c + 1) * 2, :].rearrange("p a b -> p (a b)"),
                           woa_d[:, c * 2:(c + 1) * 2, :].rearrange("p a b -> p (a b)"), 64)
            for c in range(2):
                conv_chunk(wodb, wodb[:, c * 2:(c + 1) * 2, :].rearrange("p a b -> p (a b)"), wod_d[:, c * 2:(c + 1) * 2, :].rearrange("p a b -> p (a b)"))
            for c in range(16):
                conv_chunk(w2c, w2c[:, :], w2_d[:, c * 2:(c + 1) * 2, :].rearrange("p a b -> p (a b)"))
                kb.dma("sp", w2b_d[:, c * 2:(c + 1) * 2, :].rearrange("p a b -> p (a b)"), w2c[:, :], reads=[w2c])

        esC = ExitStack()
        kb.es = esC
        KTd = [kb.sb([128, NK * 128], BF16, f"KTd{i}") for i in range(2)]
        Vs = kb.sb([128, NK, 130], BF16, "Vs")
        QTs = kb.sb([128, 4, nq], BF16, "QTs")
        for kv in range(2):
            kb.dma("sp", KTd[kv][0:64, :], kT_d[kv * 64:(kv + 1) * 64, :], writes=[KTd[kv]])
            kb.dma("sp", KTd[kv][64:128, :], kT_d[kv * 64:(kv + 1) * 64, :], writes=[KTd[kv]])
        kb.dma("sp", Vs[:, :, :], v_d.rearrange("v p c -> p v c"), writes=[Vs])
        for pr in range(4):
            kb.dma("sp", QTs[:, pr, :], qT_d[pr, :, :], writes=[QTs])
        emit_weight_prep()
        QB = min(512, nq)
        PTs = [kb.sb([128, QB], BF16, f"PT{i}") for i in range(5)]
        osb = kb.sb([65, QB], F32, "osb")
        rc = kb.sb([65, QB], F32, "rc")
        oab = kb.sb([64, QB], BF16, "oab")
        pti = 0
        for q0 in range(0, nq, QB):
            for hh in range(8):
                pr, half, kv = hh // 2, hh % 2, hh // 4
                ps_ = slice(half * 64, half * 64 + 64)
                pO = kb.reserve()

                def qk(kt):
                    pS_ = kb.bank()
                    kb.mm(pS_, pS_[:, 0:QB], KTd[kv], KTd[kv][ps_, kt * 128:(kt + 1) * 128], QTs, QTs[ps_, pr, q0:q0 + QB])
                    return pS_

                LOOK = 3
                pend = [qk(kt) for kt in range(min(LOOK, NK))]
                for kt in range(NK):
                    pS = pend.pop(0)
                    PT = PTs[pti % len(PTs)]
                    pti += 1
                    kb.act(PT, PT[:, :], pS, pS[:, 0:QB], AF.Exp, scale=0.125)
                    if kt + LOOK < NK:
                        pend.append(qk(kt + LOOK))
                    kb.mm(pO, pO[0:65, 0:QB], Vs, Vs[:, kt, kv * 65:(kv + 1) * 65], PT, PT[:, :], start=(kt == 0), stop=(kt == NK - 1))
                kb.cp("act", osb, osb[:, :], pO, pO[0:65, 0:QB])
                kb.release(pO)
                kb.op("dve", lambda g: g.reciprocal(rc[64:65, :], osb[64:65, :]), reads=[osb], writes=[rc])
                pB = kb.bank()
                kb.mm(pB, pB[0:64, 0:QB], onesf, onesf[64:65, 0:64], rc, rc[64:65, :])
                kb.tt("dve", oab, oab[:, :], osb, osb[0:64, :], pB, pB[0:64, 0:QB], ALU.mult)
                kb.dma("sp", oa_d[:, hh, q0:q0 + QB], oab[:, :], reads=[oab])
        barrier()
        for dd in list(oab.r.values()):
            kb._wait("sp", dd)
        for dd in list(w2c.r.values()):
            kb._wait("sp", dd)
        esC.close()
        kb.es = esCD

        esD = ExitStack()
        kb.es = esD
        G1 = kb.sb([128, D], F32, "G1")
        S2 = kb.sb([128, D], F32, "S2")
        H2 = kb.sb([128, D], F32, "H2")
        G2 = kb.sb([128, D], F32, "G2")
        for i_, bb in enumerate((G1, S2, H2, G2)):
            kb.dma("sp", bb[:, :], mod_d[i_, :, :], writes=[bb])
        TB = min(2, nown)
        NB = TB * 128
        oat = kb.sb([64, 8, NB], BF16, "oat")
        mdt = kb.sb([128, TB, 512], BF16, "mdt")
        xo = kb.sb([128, D], F32, "xo")
        x1b = kb.sb([128, TB, D], F32, "x1b")
        junkD = kb.sb([128, D], BF16, "junkD")
        ssqD = kb.sb([128, 1], F32, "ssqD")
        rstdD = kb.sb([128, 1], F32, "rstdD")
        t32D = kb.sb([128, D], F32, "t32D")
        hbD = kb.sb([128, D], BF16, "hbD")
        h2T = kb.sb([128, 8, NB], BF16, "h2T")
        w2s = [kb.sb([128, 4, D], BF16, f"w2s{i}") for i in range(2)]
        rls = [kb.sb([128, NB], F32, f"rl{i}") for i in range(2)]
        aT = [kb.sb([128, NB], BF16, f"aT{i}") for i in range(3)]
        yo = kb.sb([128, D], F32, "yo")
        ai = 0
        out_deps = []
        for blk0 in range(0, nown, TB):
            t0 = blk0 * 128
            kb.dma("sp", oat[:, :, :], oa_d[:, :, t0:t0 + NB], writes=[oat])
            for t in range(TB):
                kb.dma("sp", mdt[:, t, :], mdn_d[blk0 + t, :, :], writes=[mdt])
            for t in range(TB):
                m = blk0 + t
                kb.dma("sp", xo[:, :], xown_d[m, :, :], writes=[xo])
                for half in range(2):
                    cs = slice(half * 512, half * 512 + 512)
                    pM = kb.bank()
                    for hh in range(8):
                        kb.mm(pM, pM[:, :], oat, oat[:, hh, t * 128:(t + 1) * 128], woab, woab[:, hh, cs], start=(hh == 0), stop=False)
                    for hd in range(4):
                        kb.mm(pM, pM[:, :], mdt, mdt[:, t, hd * 128:(hd + 1) * 128], wodb, wodb[:, hd, cs], start=False, stop=(hd == 3))
                    kb.tt("dve", x1b, x1b[:, t, cs], pM, pM[:, :], G1, G1[:, cs], ALU.mult)
                kb.tt("dve", x1b, x1b[:, t, :], x1b, x1b[:, t, :], xo, xo[:, :], ALU.add)
                kb.act(junkD, junkD[:, :], x1b, x1b[:, t, :], AF.Square, accum=ssqD[:, :], extra_writes=[ssqD])
                kb.act(rstdD, rstdD[:, :], ssqD, ssqD[:, :], AF.Ln, bias=EPS, scale=1.0 / D)
                kb.act(rstdD, rstdD[:, :], rstdD, rstdD[:, :], AF.Exp, scale=-0.5)
                kb.stt(t32D, t32D[:, :], x1b, x1b[:, t, :], rstdD[:, 0:1], S2, S2[:, :], ALU.mult, ALU.mult, extra_reads=[rstdD])
                kb.tt("dve", hbD, hbD[:, :], t32D, t32D[:, :], H2, H2[:, :], ALU.add)
                pb = kb.bank()
                pbv = pb.t[:, :].bitcast(BF16)
                for k in range(8):
                    kb.tr(pb, pbv[:, k * 128:(k + 1) * 128], hbD, hbD[:, k * 128:(k + 1) * 128], identb, identb[:, :])
                kb.cp("act", h2T, h2T[:, :, t * 128:(t + 1) * 128], pb, pbv.rearrange("p (k t) -> p k t", k=8))
            acc = [[kb.reserve() for _ in range(2)] for _ in range(TB)]
            def mlp1(f):
                pA1_ = kb.bank()
                for k in range(8):
                    kb.mm(pA1_, pA1_[:, 0:NB], w1b, w1b[:, k, f * 128:(f + 1) * 128], h2T, h2T[:, k, :], start=(k == 0), stop=(k == 7))
                return pA1_

            pendA = [mlp1(0)]
            for f in range(32):
                fg, fj = f // 4, f % 4
                ws = w2s[fg % 2]
                if fj == 0:
                    kb.dma("sp", ws[:, :, :], w2b_d[:, fg * 4:(fg + 1) * 4, :], writes=[ws])
                pA1 = pendA.pop(0)
                r_ = rls[f % 2]
                kb.act(r_, r_[:, :], pA1, pA1[:, 0:NB], AF.Relu)
                a_ = aT[ai % 3]
                ai += 1
                kb.tt("dve", a_, a_[:, :], r_, r_[:, :], r_, r_[:, :], ALU.mult)
                if f + 1 < 32:
                    pendA.append(mlp1(f + 1))
                for t in range(TB):
                    for half in range(2):
                        kb.mm(acc[t][half], acc[t][half][:, :], a_, a_[:, t * 128:(t + 1) * 128], ws, ws[:, fj, half * 512:half * 512 + 512],
                              start=(f == 0), stop=(f == 31))
            for t in range(TB):
                m = blk0 + t
                for half in range(2):
                    cs = slice(half * 512, half * 512 + 512)
                    kb.tt("dve", yo, yo[:, cs], acc[t][half], acc[t][half][:, :], G2, G2[:, cs], ALU.mult)
                    kb.release(acc[t][half])
                kb.tt("dve", yo, yo[:, :], yo, yo[:, :], x1b, x1b[:, t, :], ALU.add)
                out_deps.append(kb.dma("sp", out_d[m, :, :], yo[:, :], reads=[yo]))
        for dd in out_deps:
            kb._wait("sp", dd)
        for dd in list(yo.r.values()):
            kb._wait("sp", dd)
        barrier()
        esD.close()
        esW2.close()
        esCD.close()
        kb.es = es
        return nc, kb, es


def _rep(v, n=128):
    v = np.asarray(v, np.float32)
    return np.ascontiguousarray(np.broadcast_to(v[None, :], (n, v.shape[0])))


def _win_perm(dirs=(0, 1)):
    perm = list(range(0, 768))
    for base in (1280, 1792, 768):
        for h in range(4):
            perm += list(range(base + h * 128, base + (h + 1) * 128))
    perm += list(range(2304, 2816))
    for base in (2816, 2824):
        for dr in dirs:
            perm += [base + dr * 4 + h for h in range(4)]
    return np.array(perm)


def _halo_rows(seq, t0):
    out = np.zeros((4, seq.shape[1]), np.float32)
    fl = 1.0 if t0 - 2 >= 0 else 0.0
    fr = 1.0 if t0 + 130 <= seq.shape[0] else 0.0
    if fl:
        out[0:2] = seq[t0 - 2:t0]
    if fr:
        out[2:4] = seq[t0 + 128:t0 + 130]
    return out, fl, fr


def _rope_table(tile):
    out = np.zeros((128, 1280), np.float32)
    if tile is None:
        out[:, 0:640] = 1.0
        return out
    t = tile * 128 + np.arange(128)
    pos = [(t // 64).astype(np.float32), (t % 64).astype(np.float32)]
    inv = (np.float32(10000.0) ** (-np.arange(16, dtype=np.float32) / np.float32(16))).astype(np.float32)
    C = np.zeros((128, 64), np.float32)
    S = np.zeros((128, 64), np.float32)
    for blk in range(2):
        ang = (pos[blk][:, None] * inv[None, :]).astype(np.float32)
        c, sn = np.cos(ang).astype(np.float32), np.sin(ang).astype(np.float32)
        C[:, blk * 32:blk * 32 + 16] = c
        C[:, blk * 32 + 16:blk * 32 + 32] = c
        S[:, blk * 32:blk * 32 + 16] = -sn
        S[:, blk * 32 + 16:blk * 32 + 32] = sn
    out[:, 0:640] = np.tile(C, (1, 10))
    out[:, 640:1280] = np.tile(S, (1, 10))
    return out


def _const_tables(dirs):
    p = np.arange(128)[:, None]
    f = np.arange(128)[None, :]
    tri = np.zeros((128, 2, 128), np.float32)
    masks = np.zeros((128, 4, 512), np.float32)
    for ds, dr in enumerate(dirs):
        if dr == 0:
            tri[:, ds, :] = (p <= f)
            okS, okIT = (p > f), (f >= p)
        else:
            tri[:, ds, :] = (p >= f)
            okS, okIT = (p < f), (f <= p)
        masks[:, ds * 2 + 0, :] = np.tile(np.where(okS, 0.0, BIG), (1, 4))
        masks[:, ds * 2 + 1, :] = np.tile(np.where(okIT, 0.0, -BIG), (1, 4))
    im = np.zeros((128, 4, 512), np.float32)
    im[:, 0, :] = np.tile((p // 16) == (f // 16), (1, 4))
    for mi, sz in ((1, 16), (2, 32), (3, 64)):
        same = (p // (2 * sz)) == (f // (2 * sz))
        im[:, mi, :] = np.tile(same & (((p % (2 * sz)) >= sz) != ((f % (2 * sz)) >= sz)), (1, 4))
    return tri, masks, im


def _plan(i):
    own_lo = 16 * i
    fwd_pre = list(range(0, own_lo))
    bwd_pre = list(range(63, own_lo + 15, -1))
    fwd_own = list(range(own_lo, own_lo + 16))
    bwd_own = fwd_own[::-1]
    fwd_ctx, bwd_ctx = [0, 1], [1, 0]
    if i < 2:
        dirs = (0, 1)
        s1_ctx, s1_pre, s1_own = fwd_ctx, fwd_pre, fwd_own
        s2_ctx, s2_pre, s2_own = bwd_ctx, bwd_pre, bwd_own
    else:
        dirs = (1, 0)
        s1_ctx, s1_pre, s1_own = bwd_ctx, bwd_pre, bwd_own
        s2_ctx, s2_pre, s2_own = fwd_ctx, fwd_pre, fwd_own
    scan1 = [("ctx", t, 1.0) for t in s1_ctx]
    scan1 += [("lat", 0, 0.0)] * (16 - len(s1_pre)) + [("lat", t, 1.0) for t in s1_pre]
    scan1 += [("lat", t, 1.0) for t in s1_own]
    scan2 = [("ctx", t, 1.0) for t in s2_ctx]
    scan2 += [("lat", t, 0.0) for t in s1_pre] if len(s2_pre) < 48 else []
    scan2 += [("lat", t, 1.0) for t in s2_pre]
    scan2 += [("lat", t, 1.0) for t in s2_own]
    assert len(scan1) == N1 and len(scan2) == N2, (len(scan1), len(scan2))
    assert sorted(t for k, t, _ in scan2 if k == "lat") == list(range(64))
    return dirs, scan1, scan2, s2_own


def kernel(**inputs):
    set_cfg(16, 48, 16)
    g = lambda k: np.asarray(inputs[k], np.float32)
    x, c, ctx, c_ctx = g("x"), g("c"), g("ctx"), g("c_ctx")
    w_mod, w_in, conv_w = g("w_mod")[0], g("w_in")[0], g("conv_w")[0]
    w_out, w1, w2 = g("w_out")[0], g("w_mlp1")[0], g("w_mlp2")[0]
    a_log, dt_bias = g("a_log")[0], g("dt_bias")[0]

    shared = {
        "wmod": np.ascontiguousarray(w_mod.reshape(8, 128, 6, D).transpose(2, 1, 0, 3).reshape(6, 128, 8 * D)),
        "bmod": _rep(g("b_mod")[0]), "n1w": _rep(g("norm1_w")[0]), "n2w": _rep(g("norm2_w")[0]),
        "qkw": _rep(np.concatenate([np.tile(g("q_norm_w")[0], 8), np.tile(g("k_norm_w")[0], 2)])),
        "dnw": _rep(np.tile(g("dn_norm_w")[0], 4)),
        "woa": np.ascontiguousarray(w_out[0:512].reshape(8, 64, D).transpose(1, 0, 2)),
        "wod": np.ascontiguousarray(w_out[512:1024].reshape(4, 128, D).transpose(1, 0, 2)),
        "w1": np.ascontiguousarray(w1.reshape(8, 128, 4 * D).transpose(1, 0, 2)),
        "w2": np.ascontiguousarray(w2.reshape(32, 128, D).transpose(1, 0, 2)),
        "ident": np.eye(128, dtype=np.float32),
    }
    cw = np.zeros((128, 12, 5), np.float32)
    blk = 0
    for base in (512, 1024, 0):
        for h in range(4):
            cw[:, blk, :] = conv_w[:, base + h * 128:base + (h + 1) * 128].T
            blk += 1
    shared["convw"] = np.ascontiguousarray(cw.reshape(128, 60))
    per_dirs = {}
    for dirs in ((0, 1), (1, 0)):
        tri, masks, im = _const_tables(dirs)
        per_dirs[dirs] = {
            "tri": tri, "masks": masks, "imask": im,
            "win": np.ascontiguousarray(w_in[:, _win_perm(dirs)].reshape(8, 128, NCOL).transpose(1, 0, 2)),
            "alog": _rep(np.concatenate([a_log[dirs[0]], a_log[dirs[1]]])),
            "dtb": _rep(np.concatenate([dt_bias[dirs[0]], dt_bias[dirs[1]]])),
        }
    rope_cache = {t: _rope_table(t) for t in list(range(64)) + [None]}

    in_maps, own_maps = [], []
    for core in range(8):
        b, i = core // 4, core % 4
        dirs, scan1, scan2, own_tiles = _plan(i)
        feed = dict(shared)
        feed.update(per_dirs[dirs])
        feed["cc"] = np.ascontiguousarray(np.concatenate([c[b].reshape(8, 128).T, c_ctx.reshape(8, 128).T], axis=1))
        flags = np.zeros((128, 2, N2, 4), np.float32)
        seqs = {"ctx": ctx[b], "lat": x[b]}
        for ds, (scan, n) in enumerate(((scan1, N1), (scan2, N2))):
            xs = np.zeros((n, 128, D), np.float32)
            xh = np.zeros((n, 4, D), np.float32)
            for vis, (kind, t, gflag) in enumerate(scan):
                seq = seqs[kind]
                xs[vis] = seq[t * 128:(t + 1) * 128]
                xh[vis], fl, fr = _halo_rows(seq, t * 128)
                flags[:, ds, vis, 0:3] = (fl, fr, gflag)
            feed["xs%d" % (ds + 1)] = xs
            feed["xh%d" % (ds + 1)] = xh
        feed["flags"] = flags
        feed["rope2"] = np.stack([rope_cache[None if kind == "ctx" else t] for kind, t, _ in scan2])
        feed["xown"] = np.stack([x[b, t * 128:(t + 1) * 128] for t in own_tiles])
        in_maps.append(feed)
        own_maps.append((b, own_tiles))

    nc, kb, _ = build_program()
    res = run_bass_kernel_spmd(nc, in_maps, core_ids=list(range(8)))
    out = np.zeros((2, T, D), np.float32)
    for core, (b, own_tiles) in enumerate(own_maps):
        o = np.asarray(res.results[core]["out"], np.float32)
        for m, t in enumerate(own_tiles):
            out[b, t * 128:(t + 1) * 128] = o[m]
    return out
```

```python
import numpy as np
from contextlib import ExitStack
import concourse.bass as bass
import concourse.mybir as mybir
from concourse.bass_utils import run_bass_kernel_spmd

F32 = mybir.dt.float32
BF16 = mybir.dt.bfloat16
AF = mybir.ActivationFunctionType
ALU = mybir.AluOpType
AX = mybir.AxisListType

D = 1024
T = 8192
NCTX = 256
NT = T // 128
QT = 16
EPS = 1e-6
BIG = 1.0e9
DEBUG = False
SAME_ENGINE_SYNC = True


class Buf:
    __slots__ = ("t", "name", "w", "r", "dsem", "dcnt")

    def __init__(self, t, name):
        self.t = t
        self.name = name
        self.w = None
        self.r = {}
        self.dsem = None
        self.dcnt = 0

    def __getitem__(self, k):
        return self.t[k]


class KB:
    def __init__(self, nc, es):
        self.nc = nc
        self.es = es
        self.eng = {"pe": nc.tensor, "act": nc.scalar, "dve": nc.vector, "pool": nc.gpsimd, "sp": nc.sync}
        self.sem = {e: es.enter_context(nc.semaphore("sem_" + e)) for e in ("pe", "act", "dve", "pool")}
        self.cnt = {e: 0 for e in self.sem}
        self.seen = {e: {} for e in self.eng}
        self.nbuf = 0
        self.banks = []
        self.bank_i = 0
        self.ninst = 0
        self.reserved = set()

    def sb(self, shape, dtype, name=None):
        self.nbuf += 1
        name = "s_" + (name or f"b{self.nbuf}")
        return Buf(self.es.enter_context(self.nc.sbuf_tensor(name, list(shape), dtype)), name)

    def init_psum(self):
        for i in range(8):
            t = self.es.enter_context(self.nc.psum_tensor(f"psb{i}", [128, 512], F32))
            self.banks.append(Buf(t, f"psb{i}"))

    def bank(self):
        while True:
            b = self.banks[self.bank_i % 8]
            self.bank_i += 1
            if b.name not in self.reserved:
                return b

    def reserve(self):
        b = self.bank()
        self.reserved.add(b.name)
        return b

    def release(self, b):
        self.reserved.discard(b.name)

    def _wait(self, e, dep):
        if dep is None:
            return
        kind = dep[0]
        if kind == "dma":
            key = ("dma", dep[3])
            semh = dep[1]
            val = dep[2]
        else:
            if kind == e and (e == "pe" or not SAME_ENGINE_SYNC):
                return
            key = kind
            semh = self.sem[kind]
            val = dep[1]
        if self.seen[e].get(key, 0) >= val:
            return
        self.eng[e].wait_ge(semh, val)
        self.seen[e][key] = val

    def _deps(self, e, reads, writes):
        for b in reads:
            self._wait(e, b.w)
        for b in writes:
            self._wait(e, b.w)
            for d in list(b.r.values()):
                self._wait(e, d)

    def op(self, e, fn, reads=(), writes=()):
        self._deps(e, reads, writes)
        ins = fn(self.eng[e])
        self.cnt[e] += 1
        self.ninst += 1
        ins.then_inc(self.sem[e], 1)
        dep = (e, self.cnt[e])
        for b in reads:
            b.r[e] = dep
        for b in writes:
            b.w = dep
            b.r = {}
        return ins

    def dma(self, q, out, in_, reads=(), writes=()):
        b = writes[0] if writes else reads[0]
        if b.dsem is None:
            b.dsem = self.es.enter_context(self.nc.semaphore("dsem_" + b.name))
        if b.dcnt > 0:
            self._wait(q, ("dma", b.dsem, b.dcnt, b.name))
        self._deps(q, reads, writes)
        ins = self.eng[q].dma_start(out=out, in_=in_)
        b.dcnt += 16
        self.ninst += 1
        ins.then_inc(b.dsem, 16)
        dep = ("dma", b.dsem, b.dcnt, b.name)
        for x in writes:
            x.w = dep
            x.r = {}
        for x in reads:
            x.r[("dma", b.name)] = dep
        return dep

    def mm(self, out_b, out_ap, lhsT_b, lhsT_ap, rhs_b, rhs_ap, start=True, stop=True):
        return self.op("pe", lambda g: g.matmul(out_ap, lhsT_ap, rhs_ap, start=start, stop=stop),
                       reads=[lhsT_b, rhs_b], writes=[out_b])

    def tr(self, out_b, out_ap, in_b, in_ap, id_b, id_ap):
        return self.op("pe", lambda g: g.transpose(out_ap, in_ap, id_ap), reads=[in_b, id_b], writes=[out_b])

    def act(self, out_b, out_ap, in_b, in_ap, func, bias=None, scale=None, accum=None, extra_reads=(), extra_writes=()):
        kw = {}
        if bias is not None:
            kw["bias"] = bias
        if scale is not None:
            kw["scale"] = scale
        if accum is not None:
            kw["accum_out"] = accum
        return self.op("act", lambda g: g.activation(out_ap, in_ap, func, **kw),
                       reads=[in_b] + list(extra_reads), writes=[out_b] + list(extra_writes))

    def tt(self, e, out_b, out_ap, a_b, a_ap, b_b, b_ap, op):
        return self.op(e, lambda g: g.tensor_tensor(out_ap, a_ap, b_ap, op), reads=[a_b, b_b], writes=[out_b])

    def ts(self, e, out_b, out_ap, a_b, a_ap, s1, s2, op0, op1=None, extra_reads=()):
        if op1 is None:
            f = lambda g: g.tensor_scalar(out_ap, a_ap, s1, None, op0)
        else:
            f = lambda g: g.tensor_scalar(out_ap, a_ap, s1, s2, op0, op1)
        return self.op(e, f, reads=[a_b] + list(extra_reads), writes=[out_b])

    def stt(self, out_b, out_ap, a_b, a_ap, scalar, b_b, b_ap, op0, op1, extra_reads=()):
        return self.op("dve", lambda g: g.scalar_tensor_tensor(out_ap, a_ap, scalar, b_ap, op0, op1),
                       reads=[a_b, b_b] + list(extra_reads), writes=[out_b])

    def cp(self, e, out_b, out_ap, in_b, in_ap):
        if e == "act":
            return self.op("act", lambda g: g.copy(out_ap, in_ap), reads=[in_b], writes=[out_b])
        return self.op(e, lambda g: g.tensor_copy(out_ap, in_ap), reads=[in_b], writes=[out_b])

    def memset(self, e, b, ap, val):
        return self.op(e, lambda g: g.memset(ap, val), writes=[b])


GOFF = 768
ZOFF = 768 + 1536
GATE = ZOFF + 512
NCOL = GATE + 16
CFG = {"pre1": 16, "pre2": 48, "own": 16}
N1 = 2 + CFG["pre1"] + CFG["own"]
N2 = 2 + CFG["pre2"] + CFG["own"]
DK = 128


def set_cfg(pre1, pre2, own):
    global N1, N2
    CFG.update(pre1=pre1, pre2=pre2, own=own)
    N1 = 2 + pre1 + own
    N2 = 2 + pre2 + own


def build_program():
    nc = bass.Bass("TRN2", target_bir_lowering=False)

    def din(name, shape, dt=F32):
        return nc.dram_tensor(name, list(shape), dt, kind="ExternalInput").ap()

    xs_d = [din("xs1", [N1, 128, D]), din("xs2", [N2, 128, D])]
    xh_d = [din("xh1", [N1, 4, D]), din("xh2", [N2, 4, D])]
    rope_d = din("rope2", [N2, 128, 1280])
    flg_d = din("flags", [128, 2, N2, 4])
    cc_d = din("cc", [128, 16])
    wmod_d = din("wmod", [6, 128, 8 * D])
    bmod_d = din("bmod", [128, 6 * D])
    n1w_d = din("n1w", [128, D])
    n2w_d = din("n2w", [128, D])
    win_d = din("win", [128, 8, NCOL])
    qkw_d = din("qkw", [128, 640])
    cw_d = din("convw", [128, 60])
    alog_d = din("alog", [128, 8])
    dtb_d = din("dtb", [128, 8])
    dnw_d = din("dnw", [128, 512])
    woa_d = din("woa", [64, 8, D])
    wod_d = din("wod", [128, 4, D])
    w1_d = din("w1", [128, 8, 4 * D])
    w2_d = din("w2", [128, 32, D])
    xown_d = din("xown", [CFG["own"], 128, D])
    ident_d = din("ident", [128, 128])
    tri_d = din("tri", [128, 2, 128])
    msk_d = din("masks", [128, 4, 512])
    imk_d = din("imask", [128, 4, 512])
    out_d = nc.dram_tensor("out", [CFG["own"], 128, D], F32, kind="ExternalOutput").ap()

    kT_d = nc.dram_tensor("kT_s", [128, N2 * 128], BF16).ap()
    v_d = nc.dram_tensor("v_s", [N2, 128, 130], BF16).ap()
    qT_d = nc.dram_tensor("qT_s", [4, 128, CFG["own"] * 128], BF16).ap()
    o1_d = nc.dram_tensor("o1_s", [CFG["own"], 128, 512], F32).ap()
    mdn_d = nc.dram_tensor("mdn_s", [CFG["own"], 128, 512], BF16).ap()
    mod_d = nc.dram_tensor("mod_s", [4, 128, D], F32).ap()
    oa_d = nc.dram_tensor("oa_s", [64, 8, CFG["own"] * 128], BF16).ap()
    w2b_d = nc.dram_tensor("w2b_s", [128, 32, D], BF16).ap()

    with ExitStack() as es:
        kb = KB(nc, es)
        kb.init_psum()

        def barrier():
            for e in ("pe", "act", "dve", "pool", "sp"):
                for o in ("pe", "act", "dve", "pool"):
                    if o != e:
                        kb._wait(e, (o, kb.cnt[o]))

        ident = kb.sb([128, 128], F32, "ident")
        identb = kb.sb([128, 128], BF16, "identb")
        onesf = kb.sb([128, 128], F32, "onesf")
        kb.dma("sp", ident[:, :], ident_d[:, :], writes=[ident])
        kb.cp("dve", identb, identb[:, :], ident, ident[:, :])
        kb.memset("dve", onesf, onesf[:, :], 1.0)

        S1 = kb.sb([128, D], F32, "S1")
        H1 = kb.sb([128, D], F32, "H1")
        S1c = kb.sb([128, D], F32, "S1c")
        H1c = kb.sb([128, D], F32, "H1c")

        with ExitStack() as esA:
            kbA = kb
            old_es = kb.es
            kb.es = esA
            cc = kb.sb([128, 16], F32, "cc")
            sg = kb.sb([128, 16], F32, "ccsg")
            Lc = kb.sb([128, 16, 128], F32, "Lc")
            bm = kb.sb([128, 6 * D], F32, "bm")
            nw = kb.sb([128, D], F32, "nw")
            stage = [kb.sb([128, 8 * D], F32, f"wmst{i}") for i in range(2)]
            res = kb.sb([128, D], F32, "modres")
            kb.dma("sp", cc[:, :], cc_d[:, :], writes=[cc])
            kb.dma("sp", bm[:, :], bmod_d[:, :], writes=[bm])
            kb.dma("sp", nw[:, :], n1w_d[:, :], writes=[nw])
            kb.act(sg, sg[:, :], cc, cc[:, :], AF.Exp, scale=-1.0)
            kb.ts("dve", sg, sg[:, :], sg, sg[:, :], 1.0, None, ALU.add)
            kb.op("dve", lambda g: g.reciprocal(sg[:, :], sg[:, :]), reads=[sg], writes=[sg])
            kb.tt("dve", sg, sg[:, :], sg, sg[:, :], cc, cc[:, :], ALU.mult)
            for j in range(16):
                kb.ts("dve", Lc, Lc[:, j, :], onesf, onesf[:, :], sg[:, j:j + 1], None, ALU.mult, extra_reads=[sg])

            def mod_block(blk, lofs, dst_fn):
                st = stage[blk % 2]
                for half in range(2):
                    pb = kb.bank()
                    for k in range(8):
                        kb.mm(pb, pb[:, :], Lc, Lc[:, lofs + k, :], st, st[:, k * D + half * 512:k * D + half * 512 + 512],
                              start=(k == 0), stop=(k == 7))
                    dst_fn(pb, half)

            for blk in range(6):
                st = stage[blk % 2]
                kb.dma("sp", st[:, :], wmod_d[blk, :, :], writes=[st])
                if blk == 3:
                    kb.dma("sp", nw[:, :], n2w_d[:, :], writes=[nw])

                def plain(dst):
                    def f(pb, half, dst=dst, blk=blk):
                        sl = slice(half * 512, half * 512 + 512)
                        kb.tt("dve", dst, dst[:, sl], pb, pb[:, :], bm, bm[:, blk * D + half * 512:blk * D + half * 512 + 512], ALU.add)
                    return f

                def scale(dst):
                    def f(pb, half, dst=dst, blk=blk):
                        sl = slice(half * 512, half * 512 + 512)
                        kb.tt("dve", dst, dst[:, sl], pb, pb[:, :], bm, bm[:, blk * D + half * 512:blk * D + half * 512 + 512], ALU.add)
                        kb.stt(dst, dst[:, sl], dst, dst[:, sl], 1.0, nw, nw[:, sl], ALU.add, ALU.mult)
                    return f

                def spill(idx, fn_maker):
                    def f(pb, half, idx=idx):
                        fn_maker(res)(pb, half)
                        if half == 1:
                            kb.dma("sp", mod_d[idx, :, :], res[:, :], reads=[res])
                    return f

                if blk == 0:
                    mod_block(blk, 0, plain(H1))
                    mod_block(blk, 8, plain(H1c))
                elif blk == 1:
                    mod_block(blk, 0, scale(S1))
                    mod_block(blk, 8, scale(S1c))
                elif blk == 2:
                    mod_block(blk, 0, spill(0, plain))
                elif blk == 3:
                    mod_block(blk, 0, spill(2, plain))
                elif blk == 4:
                    mod_block(blk, 0, spill(1, scale))
                else:
                    mod_block(blk, 0, spill(3, plain))
            if DEBUG:
                dbg = nc.dram_tensor("dbgA", [8, 128, D], F32, kind="ExternalOutput").ap()
                for n_, b_ in enumerate((S1, H1, S1c, H1c)):
                    dd = kb.dma("sp", dbg[n_, :, :], b_[:, :], reads=[b_])
                    kb._wait("sp", dd)
                tmpd = kb.sb([128, D], F32, "dbgtmp")
                for n_ in range(4):
                    kb.dma("sp", tmpd[:, :], mod_d[n_, :, :], writes=[tmpd])
                    dd = kb.dma("sp", dbg[4 + n_, :, :], tmpd[:, :], reads=[tmpd])
                    kb._wait("sp", dd)
            barrier()
            kb.es = old_es

        esB = ExitStack()
        kb.es = esB
        wb = kb.sb([128, 8, NCOL], BF16, "wb")
        cdiag = kb.sb([128, 12, 5, 128], BF16, "cdiag")
        flg = kb.sb([128, 2, N2, 4], F32, "flg")
        with ExitStack() as esW:
            old_es = kb.es
            kb.es = esW
            wst = kb.sb([128, 8, 708], F32, "wst")
            cw = kb.sb([128, 60], F32, "cw")
            for part in range(4):
                sl = slice(part * 708, (part + 1) * 708)
                kb.dma("sp", wst[:, :, :], win_d[:, :, sl], writes=[wst])
                kb.cp("dve" if part % 2 == 0 else "pool", wb, wb[:, :, sl], wst, wst[:, :, :])
            kb.dma("sp", cw[:, :], cw_d[:, :], writes=[cw])
            kb.dma("sp", flg[:, :, :, :], flg_d[:, :, :, :], writes=[flg])
            for blk in range(12):
                for j in range(5):
                    kb.ts("dve", cdiag, cdiag[:, blk, j, :], ident, ident[:, :], cw[:, blk * 5 + j:blk * 5 + j + 1], None,
                          ALU.mult, extra_reads=[cw])
            barrier()
            kb.es = old_es

        xt = [kb.sb([128, D], F32, f"xt{i}") for i in range(2)]
        xh = kb.sb([4, D], F32, "xh")
        junk = kb.sb([128, D], BF16, "junk")
        ssq = kb.sb([128, 1], F32, "ssq")
        rstd = kb.sb([128, 1], F32, "rstd")
        t32 = kb.sb([128, D], F32, "t32")
        hb = kb.sb([128, D], BF16, "hb")
        hT = kb.sb([128, 8, 132], BF16, "hT")
        preT = kb.sb([128, 12, 132], BF16, "preT")
        sgt = kb.sb([128, 512], F32, "sgt")
        class FS:
            def __init__(self, i):
                self.cv = kb.sb([128, 12 * 128], F32, f"cv{i}")
                for nm, w in (("eb", 4), ("beta", 4), ("xa", 4), ("ab", 4), ("en", 4), ("lp", 4), ("g_", 4), ("gcs", 8),
                              ("e_", 4), ("egl", 4), ("dgl", 4), ("el", 4), ("ngc", 4), ("negbeta", 4)):
                    setattr(self, nm, kb.sb([128, w], F32, f"{nm}{i}"))
                self.zs = kb.sb([128, 512], F32, f"zs{i}")

        fsets = [FS(0), FS(1)]

        def norm_rows(xbuf, np_, Sb, Hb):
            kb.act(junk, junk[0:np_, :], xbuf, xbuf[0:np_, :], AF.Square, accum=ssq[0:np_, :], extra_writes=[ssq])
            kb.act(rstd, rstd[0:np_, :], ssq, ssq[0:np_, :], AF.Ln, bias=EPS, scale=1.0 / D)
            kb.act(rstd, rstd[0:np_, :], rstd, rstd[0:np_, :], AF.Exp, scale=-0.5)
            kb.stt(t32, t32[0:np_, :], xbuf, xbuf[0:np_, :], rstd[0:np_, 0:1], Sb, Sb[0:np_, :], ALU.mult, ALU.mult,
                   extra_reads=[rstd])
            kb.tt("dve", hb, hb[0:np_, :], t32, t32[0:np_, :], Hb, Hb[0:np_, :], ALU.add)

        def norm_tile(ds, vis, xbuf, Sb, Hb):
            kb.dma("sp", xbuf[:, :], xs_d[ds][vis, :, :], writes=[xbuf])
            kb.dma("sp", xh[:, :], xh_d[ds][vis, :, :], writes=[xh])
            norm_rows(xbuf, 128, Sb, Hb)
            pb = kb.bank()
            pbv = pb.t[:, :].bitcast(BF16)
            for k in range(8):
                kb.tr(pb, pbv[:, k * 128:(k + 1) * 128], hb, hb[:, k * 128:(k + 1) * 128], identb, identb[:, :])
            kb.cp("act", hT, hT[:, :, 2:130], pb, pbv.rearrange("p (k t) -> p k t", k=8))
            norm_rows(xh, 4, Sb, Hb)
            pb2 = kb.bank()
            pbv2 = pb2.t[:, :].bitcast(BF16)
            for k in range(8):
                kb.tr(pb2, pbv2[:, k * 4:(k + 1) * 4], hb, hb[0:4, k * 128:(k + 1) * 128], identb, identb[0:4, 0:4])
            hv = pbv2[:, 0:32].rearrange("p (k t) -> p k t", k=8)
            kb.cp("act", hT, hT[:, :, 0:2], pb2, hv[:, :, 0:2])
            kb.cp("act", hT, hT[:, :, 130:132], pb2, hv[:, :, 2:4])

        def project_tok(col0, ncols):
            pb = kb.bank()
            for k in range(8):
                kb.mm(pb, pb[:, 0:ncols], hT, hT[:, k, 2:130], wb, wb[:, k, col0:col0 + ncols], start=(k == 0), stop=(k == 7))
            return pb

        def gdn_pre_conv(ds, vis, nblk, F):
            for g0 in range(0, nblk, 3):
                pb = kb.bank()
                for j in range(3):
                    blk = g0 + j
                    for k in range(8):
                        kb.mm(pb, pb[:, j * 132:(j + 1) * 132], wb, wb[:, k, GOFF + blk * 128:GOFF + (blk + 1) * 128],
                              hT, hT[:, k, :], start=(k == 0), stop=(k == 7))
                kb.cp("act", preT, preT[:, g0:g0 + 3, :], pb, pb[:, 0:396].rearrange("p (j t) -> p j t", j=3))
                yield
            kb.ts("dve", preT, preT[:, 0:nblk, 0:2], preT, preT[:, 0:nblk, 0:2], flg[:, ds, vis, 0:1], None, ALU.mult, extra_reads=[flg])
            kb.ts("dve", preT, preT[:, 0:nblk, 130:132], preT, preT[:, 0:nblk, 130:132], flg[:, ds, vis, 1:2], None, ALU.mult, extra_reads=[flg])
            for g0 in range(0, nblk, 4):
                pb = kb.bank()
                for j4 in range(4):
                    blk = g0 + j4
                    for j in range(5):
                        kb.mm(pb, pb[:, j4 * 128:(j4 + 1) * 128], preT, preT[:, blk, j:j + 128], cdiag, cdiag[:, blk, j, :],
                              start=(j == 0), stop=(j == 4))
                kb.act(sgt, sgt[:, :], pb, pb[:, :], AF.Exp, scale=-1.0)
                kb.ts("dve", sgt, sgt[:, :], sgt, sgt[:, :], 1.0, None, ALU.add)
                kb.op("dve", lambda g: g.reciprocal(sgt[:, :], sgt[:, :]), reads=[sgt], writes=[sgt])
                kb.tt("dve", F.cv, F.cv[:, g0 * 128:(g0 + 4) * 128], pb, pb[:, :], sgt, sgt[:, :], ALU.mult)
                yield

        tri = kb.sb([128, 2, 128], F32, "tri")
        mskb = kb.sb([128, 4, 512], BF16, "mskb")
        negA = kb.sb([128, 8], F32, "negA")
        dtb = kb.sb([128, 8], F32, "dtb")
        identrep = kb.sb([128, 512], F32, "identrep")
        imk = kb.sb([128, 4, 512], BF16, "imk")
        with ExitStack() as esG:
            old_es = kb.es
            kb.es = esG
            mst = kb.sb([128, 4, 512], F32, "mst")
            kb.dma("sp", tri[:, :, :], tri_d[:, :, :], writes=[tri])
            kb.dma("sp", mst[:, :, :], msk_d[:, :, :], writes=[mst])
            kb.cp("dve", mskb, mskb[:, :, :], mst, mst[:, :, :])
            kb.dma("sp", mst[:, :, :], imk_d[:, :, :], writes=[mst])
            kb.cp("dve", imk, imk[:, :, :], mst, mst[:, :, :])
            kb.dma("sp", negA[:, :], alog_d[:, :], writes=[negA])
            kb.dma("sp", dtb[:, :], dtb_d[:, :], writes=[dtb])
            kb.act(negA, negA[:, :], negA, negA[:, :], AF.Exp)
            kb.ts("dve", negA, negA[:, :], negA, negA[:, :], -1.0, None, ALU.mult)
            for h in range(4):
                kb.cp("dve", identrep, identrep[:, h * 128:(h + 1) * 128], ident, ident[:, :])
            barrier()
            kb.es = old_es

        def sm(name, w=4):
            return kb.sb([128, w], F32, name)

        HW = 256

        class CH:
            def __init__(self, hp):
                n = f"c{hp}"
                self.hp = hp
                for nm in ("ssk", "rk", "c1", "c2", "ssq_", "rq", "cq", "cqe"):
                    setattr(self, nm, kb.sb([128, 2], F32, n + nm))
                self.junk2 = kb.sb([128, 128], BF16, n + "junk2")
                for nm in ("kn", "vb", "kbg", "kd", "qs", "qd", "knT", "qsT", "qdT", "Nb", "Zb", "Nk", "Zk", "Xb", "Yb",
                           "Cb", "Eb", "U1b", "U2b", "wT", "vnew", "intraT"):
                    setattr(self, nm, kb.sb([128, HW], BF16, n + nm))
                for nm in ("dg", "Ds", "DT", "u_"):
                    setattr(self, nm, kb.sb([128, HW], F32, n + nm))

        chains = [CH(0), CH(1)]
        SstC = [[kb.sb([128, HW], F32, f"SstC{d_}{hp}") for hp in range(2)] for d_ in range(2)]
        SbfC = [[kb.sb([128, HW], BF16, f"SbfC{d_}{hp}") for hp in range(2)] for d_ in range(2)]
        for d_ in range(2):
            for hp in range(2):
                kb.memset("dve", SstC[d_][hp], SstC[d_][hp][:, :], 0.0)
                kb.memset("dve", SbfC[d_][hp], SbfC[d_][hp][:, :], 0.0)

        def H(hh):
            return slice(hh * 128, (hh + 1) * 128)

        def tr4(dst, src):
            pb = kb.bank()
            pv = pb.t[:, :].bitcast(BF16)
            for hh in range(4):
                kb.tr(pb, pv[:, H(hh)], src, src[:, H(hh)], identb, identb[:, :])
            kb.cp("act", dst, dst[:, :], pb, pv[:, 0:512])

        def tr2(dst, src, e="act"):
            pb = kb.bank()
            pv = pb.t[:, :].bitcast(BF16)
            for lh in range(2):
                kb.tr(pb, pv[:, H(lh)], src, src[:, H(lh)], identb, identb[:, :])
            kb.cp(e, dst, dst[:, :], pb, pv[:, 0:HW])

        def rsq(dst, src, add):
            kb.act(dst, dst[:, :], src, src[:, :], AF.Ln, bias=add)
            kb.act(dst, dst[:, :], dst, dst[:, :], AF.Exp, scale=-0.5)

        def gdn_gates(ds, vis, own, F):
            c4 = slice(ds * 4, ds * 4 + 4)
            a4 = slice(8 + ds * 4, 8 + ds * 4 + 4)
            pg = project_tok(GATE, 16)
            fg = flg[:, ds, vis, 2:3]
            kb.act(F.eb, F.eb[:, :], pg, pg[:, c4], AF.Exp, scale=-1.0)
            kb.ts("dve", F.eb, F.eb[:, :], F.eb, F.eb[:, :], 1.0, None, ALU.add)
            kb.op("dve", lambda g: g.reciprocal(F.beta[:, :], F.eb[:, :]), reads=[F.eb], writes=[F.beta])
            kb.ts("dve", F.beta, F.beta[:, :], F.beta, F.beta[:, :], fg, None, ALU.mult, extra_reads=[flg])
            kb.ts("dve", F.negbeta, F.negbeta[:, :], F.beta, F.beta[:, :], -1.0, None, ALU.mult)
            kb.tt("dve", F.xa, F.xa[:, :], pg, pg[:, a4], dtb, dtb[:, c4], ALU.add)
            kb.act(F.ab, F.ab[:, :], F.xa, F.xa[:, :], AF.Abs)
            kb.act(F.en, F.en[:, :], F.ab, F.ab[:, :], AF.Exp, scale=-1.0)
            kb.act(F.lp, F.lp[:, :], F.en, F.en[:, :], AF.Ln, bias=1.0)
            kb.stt(F.g_, F.g_[:, :], F.xa, F.xa[:, :], 0.0, F.lp, F.lp[:, :], ALU.max, ALU.add)
            kb.tt("dve", F.g_, F.g_[:, :], F.g_, F.g_[:, :], negA, negA[:, c4], ALU.mult)
            kb.ts("dve", F.g_, F.g_[:, :], F.g_, F.g_[:, :], fg, None, ALU.mult, extra_reads=[flg])
            pgc = kb.bank()
            kb.mm(pgc, pgc[:, 0:4], tri, tri[:, ds, :], F.g_, F.g_[:, :])
            kb.mm(pgc, pgc[:, 4:8], onesf, onesf[:, :], F.g_, F.g_[:, :])
            kb.cp("act", F.gcs, F.gcs[:, :], pgc, pgc[:, 0:8])
            kb.act(F.e_, F.e_[:, :], F.gcs, F.gcs[:, 0:4], AF.Exp)
            kb.act(F.egl, F.egl[:, :], F.gcs, F.gcs[:, 4:8], AF.Exp)
            kb.tt("dve", F.dgl, F.dgl[:, :], F.gcs, F.gcs[:, 4:8], F.gcs, F.gcs[:, 0:4], ALU.subtract)
            kb.act(F.el, F.el[:, :], F.dgl, F.dgl[:, :], AF.Exp)
            if own:
                kb.ts("dve", F.ngc, F.ngc[:, :], F.gcs, F.gcs[:, 0:4], -1.0, None, ALU.mult)

        def gdn_chain(ds, vis, own, C, res, F):
            hp = C.hp
            S, Sb = SstC[ds][hp], SbfC[ds][hp]
            hs = (2 * hp, 2 * hp + 1)
            g2 = slice(2 * hp, 2 * hp + 2)
            mk = slice(0, HW)

            def mm2(l_b, r_b, acc_b=None):
                pb_ = kb.bank()
                for lh in range(2):
                    kb.mm(pb_, pb_[:, H(lh)], l_b, l_b[:, H(lh)], r_b, r_b[:, H(lh)], start=True, stop=(acc_b is None))
                    if acc_b is not None:
                        kb.mm(pb_, pb_[:, H(lh)], identb, identb[:, :], acc_b, acc_b[:, H(lh)], start=False, stop=True)
                return pb_

            for lh, hh in enumerate(hs):
                kb.act(C.junk2, C.junk2[:, :], F.cv, F.cv[:, H(hh)], AF.Square, accum=C.ssk[:, lh:lh + 1], extra_writes=[C.ssk])
            rsq(C.rk, C.ssk, EPS)
            yield
            kb.tt("dve", C.c1, C.c1[:, :], C.rk, C.rk[:, :], F.beta, F.beta[:, g2], ALU.mult)
            kb.tt("dve", C.c1, C.c1[:, :], C.c1, C.c1[:, :], F.e_, F.e_[:, g2], ALU.mult)
            kb.tt("dve", C.c2, C.c2[:, :], C.rk, C.rk[:, :], F.el, F.el[:, g2], ALU.mult)
            yield
            def scl(dst, dcol, src_col, sc_b, sc_ap):
                kb.act(dst, dst[:, H(dcol)], F.cv, F.cv[:, H(src_col)], AF.Identity, scale=sc_ap, extra_reads=[sc_b])

            for lh, hh in enumerate(hs):
                scl(C.kn, lh, hh, C.rk, C.rk[:, lh:lh + 1])
                scl(C.vb, lh, 4 + hh, F.beta, F.beta[:, hh:hh + 1])
                scl(C.kbg, lh, hh, C.c1, C.c1[:, lh:lh + 1])
                scl(C.kd, lh, hh, C.c2, C.c2[:, lh:lh + 1])
                yield
            tr2(C.knT, C.kn)
            yield
            if own:
                for lh, hh in enumerate(hs):
                    kb.act(C.junk2, C.junk2[:, :], F.cv, F.cv[:, H(8 + hh)], AF.Square, accum=C.ssq_[:, lh:lh + 1], extra_writes=[C.ssq_])
                rsq(C.rq, C.ssq_, EPS)
                kb.ts("dve", C.cq, C.cq[:, :], C.rq, C.rq[:, :], float(DK) ** -0.5, None, ALU.mult)
                kb.tt("dve", C.cqe, C.cqe[:, :], C.cq, C.cq[:, :], F.e_, F.e_[:, g2], ALU.mult)
                yield
                for lh, hh in enumerate(hs):
                    scl(C.qs, lh, 8 + hh, C.cq, C.cq[:, lh:lh + 1])
                    scl(C.qd, lh, 8 + hh, C.cqe, C.cqe[:, lh:lh + 1])
                yield
                tr2(C.qsT, C.qs)
                yield
                tr2(C.qdT, C.qd)
                yield
            pA = mm2(C.knT, C.knT)
            for lh, hh in enumerate(hs):
                kb.act(C.dg, C.dg[:, H(lh)], ident, ident[:, :], AF.Identity, scale=F.gcs[:, hh:hh + 1], extra_reads=[F.gcs])
            pG = kb.bank()
            kb.mm(pG, pG[:, 0:HW], onesf, onesf[:, :], C.dg, C.dg[:, :], start=True, stop=False)
            kb.mm(pG, pG[:, 0:HW], identb, identb[:, :], mskb, mskb[:, ds * 2, mk], start=False, stop=True)
            yield
            for lh, hh in enumerate(hs):
                kb.act(C.Ds, C.Ds[:, H(lh)], pG, pG[:, H(lh)], AF.Exp, bias=F.gcs[:, hh:hh + 1], scale=-1.0, extra_reads=[F.gcs])
            yield
            for lh, hh in enumerate(hs):
                kb.stt(C.Nb, C.Nb[:, H(lh)], pA, pA[:, H(lh)], F.negbeta[:, hh:hh + 1], C.Ds, C.Ds[:, H(lh)], ALU.mult, ALU.mult,
                       extra_reads=[F.negbeta])
            yield
            tr2(C.Zb, C.Nb)
            yield
            kb.tt("pool", C.Nk, C.Nk[:, :], C.Nb, C.Nb[:, :], imk, imk[:, 0, mk], ALU.mult)
            kb.tt("pool", C.Zk, C.Zk[:, :], C.Zb, C.Zb[:, :], imk, imk[:, 0, mk], ALU.mult)
            yield
            kb.tt("dve", C.Xb, C.Xb[:, :], C.Nk, C.Nk[:, :], identrep, identrep[:, mk], ALU.add)
            kb.tt("dve", C.Yb, C.Yb[:, :], C.Zk, C.Zk[:, :], identrep, identrep[:, mk], ALU.add)
            yield
            for lvl in range(1, 4):
                pN = mm2(C.Zk, C.Nk)
                pZ = mm2(C.Nk, C.Zk)
                yield
                kb.cp("act", C.Nk, C.Nk[:, :], pN, pN[:, 0:HW])
                kb.cp("dve", C.Zk, C.Zk[:, :], pZ, pZ[:, 0:HW])
                yield
                pX = mm2(C.Yb, C.Nk, C.Xb)
                pP = mm2(C.Xb, C.Zk, C.Yb)
                yield
                kb.cp("act", C.Xb, C.Xb[:, :], pX, pX[:, 0:HW])
                kb.cp("dve", C.Yb, C.Yb[:, :], pP, pP[:, 0:HW])
                yield
            for mi in (1, 2, 3):
                last = mi == 3
                kb.tt("pool", C.Cb, C.Cb[:, :], C.Nb, C.Nb[:, :], imk, imk[:, mi, mk], ALU.mult)
                if not last:
                    kb.tt("pool", C.Eb, C.Eb[:, :], C.Zb, C.Zb[:, :], imk, imk[:, mi, mk], ALU.mult)
                yield
                if not last:
                    pU1 = mm2(C.Eb, C.Xb)
                pU2 = mm2(C.Cb, C.Yb)
                yield
                if not last:
                    kb.cp("dve", C.U1b, C.U1b[:, :], pU1, pU1[:, 0:HW])
                kb.cp("act", C.U2b, C.U2b[:, :], pU2, pU2[:, 0:HW])
                yield
                if not last:
                    pX = mm2(C.Yb, C.U1b, C.Xb)
                pP = mm2(C.Xb, C.U2b, C.Yb)
                yield
                if not last:
                    kb.cp("act", C.Xb, C.Xb[:, :], pX, pX[:, 0:HW])
                kb.cp("dve", C.Yb, C.Yb[:, :], pP, pP[:, 0:HW])
                yield
            pU = mm2(C.Yb, C.vb)
            pW = mm2(C.kbg, C.Yb)
            yield
            kb.cp("act", C.u_, C.u_[:, :], pU, pU[:, 0:HW])
            kb.cp("dve", C.wT, C.wT[:, :], pW, pW[:, 0:HW])
            yield
            pWS = mm2(C.wT, Sb)
            yield
            kb.tt("dve", C.vnew, C.vnew[:, :], C.u_, C.u_[:, :], pWS, pWS[:, 0:HW], ALU.subtract)
            yield
            if own:
                pGI = kb.bank()
                kb.mm(pGI, pGI[:, 0:HW], onesf, onesf[:, :], C.dg, C.dg[:, :], start=True, stop=False)
                kb.mm(pGI, pGI[:, 0:HW], identb, identb[:, :], mskb, mskb[:, ds * 2 + 1, mk], start=False, stop=True)
                pQK = mm2(C.knT, C.qsT)
                yield
                for lh, hh in enumerate(hs):
                    kb.act(C.DT, C.DT[:, H(lh)], pGI, pGI[:, H(lh)], AF.Exp, bias=F.ngc[:, hh:hh + 1], extra_reads=[F.ngc])
                yield
                kb.tt("dve", C.intraT, C.intraT[:, :], pQK, pQK[:, 0:HW], C.DT, C.DT[:, :], ALU.mult)
                yield
                pO = kb.reserve()
                for lh in range(2):
                    kb.mm(pO, pO[:, H(lh)], C.qdT, C.qdT[:, H(lh)], Sb, Sb[:, H(lh)], start=True, stop=False)
                    kb.mm(pO, pO[:, H(lh)], C.intraT, C.intraT[:, H(lh)], C.vnew, C.vnew[:, H(lh)], start=False, stop=True)
                res[hp] = pO
                yield
            pS = mm2(C.kd, C.vnew)
            yield
            for lh, hh in enumerate(hs):
                kb.stt(S, S[:, H(lh)], S, S[:, H(lh)], F.egl[:, hh:hh + 1], pS, pS[:, H(lh)], ALU.mult, ALU.add, extra_reads=[F.egl])
            kb.cp("act", Sb, Sb[:, :], S, S[:, :])
            yield

        qkw = kb.sb([128, 640], F32, "qkw")
        dnw = kb.sb([128, 512], F32, "dnw")
        kb.dma("sp", qkw[:, :], qkw_d[:, :], writes=[qkw])
        kb.dma("sp", dnw[:, :], dnw_d[:, :], writes=[dnw])
        rp = kb.sb([128, 1280], F32, "rp")
        sqt = kb.sb([128, 640], F32, "sqt")
        ss10 = kb.sb([128, 10], F32, "ss10")
        rr10 = kb.sb([128, 10], F32, "rr10")
        xw = kb.sb([128, 640], F32, "xw")
        sw = kb.sb([128, 640], F32, "sw")
        t1 = kb.sb([128, 640], F32, "t1")
        qkb = kb.sb([128, 640], BF16, "qkb")
        kTt = kb.sb([128, 128], BF16, "kTt")
        qTt = kb.sb([128, 512], BF16, "qTt")
        vt = kb.sb([128, 130], BF16, "vt")
        kb.memset("dve", vt, vt[:, :], 1.0)

        def attn_proj(vis, own, m):
            kb.dma("sp", rp[:, :], rope_d[vis, :, :], writes=[rp])
            c0 = 0 if own else 512
            nh0 = 0 if own else 8
            pKV = project_tok(512, 256)
            pQ = project_tok(0, 512) if own else None
            if own:
                kb.act(sqt, sqt[:, 0:512], pQ, pQ[:, :], AF.Square)
            kb.act(sqt, sqt[:, 512:640], pKV, pKV[:, 0:128], AF.Square)
            kb.op("dve", lambda g: g.tensor_reduce(ss10[:, nh0:10], sqt[:, c0:640].rearrange("p (h d) -> p h d", d=64), AX.X, ALU.add),
                  reads=[sqt], writes=[ss10])
            kb.act(rr10, rr10[:, nh0:10], ss10, ss10[:, nh0:10], AF.Ln, bias=EPS, scale=1.0 / 64)
            kb.act(rr10, rr10[:, nh0:10], rr10, rr10[:, nh0:10], AF.Exp, scale=-0.5)
            for j in range(nh0, 10):
                src_b, src_ap = (pQ, pQ[:, j * 64:(j + 1) * 64]) if j < 8 else (pKV, pKV[:, (j - 8) * 64:(j - 7) * 64])
                kb.ts("dve", xw, xw[:, j * 64:(j + 1) * 64], src_b, src_ap, rr10[:, j:j + 1], None, ALU.mult, extra_reads=[rr10])
            kb.tt("dve", xw, xw[:, c0:640], xw, xw[:, c0:640], qkw, qkw[:, c0:640], ALU.mult)
            yield
            xv = xw[:, c0:640].rearrange("p (n two s) -> p n two s", two=2, s=16)
            sv = sw[:, c0:640].rearrange("p (n two s) -> p n two s", two=2, s=16)
            kb.cp("pool", sw, sv[:, :, 0, :], xw, xv[:, :, 1, :])
            kb.cp("pool", sw, sv[:, :, 1, :], xw, xv[:, :, 0, :])
            kb.tt("dve", t1, t1[:, c0:640], xw, xw[:, c0:640], rp, rp[:, c0:640], ALU.mult)
            kb.tt("dve", sw, sw[:, c0:640], sw, sw[:, c0:640], rp, rp[:, 640 + c0:1280], ALU.mult)
            kb.tt("dve", qkb, qkb[:, c0:640], t1, t1[:, c0:640], sw, sw[:, c0:640], ALU.add)
            yield
            pb = kb.bank()
            pv = pb.t[:, :].bitcast(BF16)
            kb.tr(pb, pv[:, 0:128], qkb, qkb[:, 512:640], identb, identb[:, :])
            kb.cp("act", kTt, kTt[:, :], pb, pv[:, 0:128])
            kb.dma("sp", kT_d[:, vis * 128:(vis + 1) * 128], kTt[:, :], reads=[kTt])
            kb.cp("act", vt, vt[:, :].rearrange("p (k c) -> p k c", k=2)[:, :, 0:64], pKV,
                  pKV[:, 128:256].rearrange("p (k c) -> p k c", k=2))
            kb.dma("sp", v_d[vis, :, :], vt[:, :], reads=[vt])
            yield
            if own:
                tr4(qTt, qkb)
                for pr in range(4):
                    kb.dma("sp", qT_d[pr, :, m * 128:(m + 1) * 128], qTt[:, H(pr)], reads=[qTt])

        o1t = kb.sb([128, 512], F32, "o1t")
        junk2 = kb.sb([128, 128], BF16, "junk2o")
        o1r = kb.sb([128, 512], F32, "o1r")
        osum = kb.sb([128, 512], F32, "osum")
        ssd, rsd = sm("ssd"), sm("rsd")
        od = kb.sb([128, 512], BF16, "od")
        odT = kb.sb([128, 512], BF16, "odT")

        def own_out_scan1(pOs, m):
            for hp in range(2):
                kb.cp("act" if hp == 0 else "dve", o1t, o1t[:, hp * HW:(hp + 1) * HW], pOs[hp], pOs[hp][:, 0:HW])
                kb.release(pOs[hp])
            kb.dma("sp", o1_d[m, :, :], o1t[:, :], reads=[o1t])

        def own_out_scan2(pOs, m, F):
            zs = F.zs
            kb._wait("sp", o1t.r.get(("dma", o1t.name)))
            kb.dma("sp", o1r[:, :], o1_d[m, :, :], writes=[o1r])
            for hp in range(2):
                kb.tt("dve", osum, osum[:, hp * HW:(hp + 1) * HW], o1r, o1r[:, hp * HW:(hp + 1) * HW], pOs[hp], pOs[hp][:, 0:HW], ALU.add)
                kb.release(pOs[hp])
            for hh in range(4):
                kb.act(junk2, junk2[:, :], osum, osum[:, H(hh)], AF.Square, accum=ssd[:, hh:hh + 1], extra_writes=[ssd])
            kb.act(rsd, rsd[:, :], ssd, ssd[:, :], AF.Ln, bias=EPS, scale=1.0 / 128)
            kb.act(rsd, rsd[:, :], rsd, rsd[:, :], AF.Exp, scale=-0.5)
            for hh in range(4):
                kb.stt(od, od[:, H(hh)], osum, osum[:, H(hh)], rsd[:, hh:hh + 1], zs, zs[:, H(hh)], ALU.mult, ALU.mult,
                       extra_reads=[rsd])
            tr4(odT, od)
            kb.dma("sp", mdn_d[m, :, :], odT[:, :], reads=[odT])

        nown, pre1, pre2 = CFG["own"], CFG["pre1"], CFG["pre2"]

        def vinfo(ds, vis):
            npre = pre1 if ds == 0 else pre2
            return vis < 2, vis >= 2 + npre, vis - 2 - npre

        def front(ds, vis, F):
            isctx, own, sidx = vinfo(ds, vis)
            norm_tile(ds, vis, xt[ds], S1c if isctx else S1, H1c if isctx else H1)
            yield
            if ds == 1:
                yield from attn_proj(vis, own, sidx if own else 0)
                if own:
                    pZ = project_tok(ZOFF, 512)
                    kb.act(F.zs, F.zs[:, :], pZ, pZ[:, :], AF.Exp, scale=-1.0)
                    kb.ts("dve", F.zs, F.zs[:, :], F.zs, F.zs[:, :], 1.0, None, ALU.add)
                    kb.op("dve", lambda g: g.reciprocal(F.zs[:, :], F.zs[:, :]), reads=[F.zs], writes=[F.zs])
                    kb.tt("dve", F.zs, F.zs[:, :], pZ, pZ[:, :], F.zs, F.zs[:, :], ALU.mult)
                    kb.tt("dve", F.zs, F.zs[:, :], F.zs, F.zs[:, :], dnw, dnw[:, :], ALU.mult)
                    yield
            yield from gdn_pre_conv(ds, vis, 12 if own else 8, F)
            gdn_gates(ds, vis, own, F)
            yield

        def run_all(gens):
            gens = list(gens)
            while gens:
                for g in list(gens):
                    try:
                        next(g)
                    except StopIteration:
                        gens.remove(g)

        order = []
        v1 = 0
        n2_head = 2 + pre2
        for v2 in range(N2):
            if v2 == n2_head:
                while v1 < N1:
                    order.append((0, v1))
                    v1 += 1
            order.append((1, v2))
            while v1 < N1 and v1 * n2_head < (v2 + 1) * N1 and v2 < n2_head:
                order.append((0, v1))
                v1 += 1
        assert len(order) == N1 + N2

        run_all([front(order[0][0], order[0][1], fsets[0])])
        for n_, (ds, vis) in enumerate(order):
            F = fsets[n_ % 2]
            isctx, own, sidx = vinfo(ds, vis)
            res = [None, None]
            gens = [gdn_chain(ds, vis, own, chains[0], res, F), gdn_chain(ds, vis, own, chains[1], res, F)]
            if n_ + 1 < len(order):
                gens.append(front(order[n_ + 1][0], order[n_ + 1][1], fsets[(n_ + 1) % 2]))
            run_all(gens)
            if own and ds == 0:
                own_out_scan1(res, nown - 1 - sidx)
            if own and ds == 1:
                own_out_scan2(res, sidx, F)
        barrier()
        for e in ("sp",):
            for bb in (kTt, vt, qTt, odT, o1t):
                for dd in list(bb.r.values()):
                    kb._wait(e, dd)

        if DEBUG:
            dbgM = nc.dram_tensor("dbgM", [nown, 128, 512], BF16, kind="ExternalOutput").ap()
            dbgK = nc.dram_tensor("dbgK", [128, N2 * 128], BF16, kind="ExternalOutput").ap()
            dbgV = nc.dram_tensor("dbgV", [N2, 128, 130], BF16, kind="ExternalOutput").ap()
            dbgQ = nc.dram_tensor("dbgQ", [4, 128, nown * 128], BF16, kind="ExternalOutput").ap()
            tb = kb.sb([128, N2 * 128], BF16, "dbgtb")
            for src, dst, shp in ((kT_d[:, :], dbgK[:, :], None),):
                kb.dma("sp", tb[:, :], src, writes=[tb])
                dd = kb.dma("sp", dst, tb[:, :], reads=[tb]); kb._wait("sp", dd)
            for m in range(nown):
                kb.dma("sp", tb[:, 0:512], mdn_d[m, :, :], writes=[tb])
                dd = kb.dma("sp", dbgM[m, :, :], tb[:, 0:512], reads=[tb]); kb._wait("sp", dd)
            for v in range(N2):
                kb.dma("sp", tb[:, 0:130], v_d[v, :, :], writes=[tb])
                dd = kb.dma("sp", dbgV[v, :, :], tb[:, 0:130], reads=[tb]); kb._wait("sp", dd)
            for pr in range(4):
                kb.dma("sp", tb[:, 0:nown * 128], qT_d[pr, :, :], writes=[tb])
                dd = kb.dma("sp", dbgQ[pr, :, :], tb[:, 0:nown * 128], reads=[tb]); kb._wait("sp", dd)
        barrier()
        esB.close()
        kb.es = es

        nq = nown * 128
        NK = N2
        esCD = ExitStack()
        kb.es = esCD
        w1b = kb.sb([128, 8, 4 * D], BF16, "w1b")
        woab = kb.sb([64, 8, D], BF16, "woab")
        wodb = kb.sb([128, 4, D], BF16, "wodb")
        esW2 = ExitStack()
        kb.es = esW2
        wst2 = [kb.sb([128, 2048], F32, f"wst2_{i}") for i in range(2)]
        w2c = kb.sb([128, 2048], BF16, "w2c")
        kb.es = esCD
        ci = 0

        def conv_chunk(dst_b, dst_ap, src_ap, np_=128):
            nonlocal ci
            st = wst2[ci % 2]
            ci += 1
            kb.dma("pool", st[0:np_, :], src_ap, writes=[st])
            kb.cp("dve", dst_b, dst_ap, st, st[0:np_, :])

        def emit_weight_prep():
            for k in range(8):
                for hf in range(2):
                    conv_chunk(w1b, w1b[:, k, hf * 2048:(hf + 1) * 2048], w1_d[:, k, hf * 2048:(hf + 1) * 2048])
            for c in range(4):
                conv_chunk(woab, woab[:, c * 2:(c + 1) * 2, :].rearrange("p a b -> p (a b)"),
                           woa_d[:, c * 2:(c + 1) * 2, :].rearrange("p a b -> p (a b)"), 64)
            for c in range(2):
                conv_chunk(wodb, wodb[:, c * 2:(c + 1) * 2, :].rearrange("p a b -> p (a b)"), wod_d[:, c * 2:(c + 1) * 2, :].rearrange("p a b -> p (a b)"))
            for c in range(16):
                conv_chunk(w2c, w2c[:, :], w2_d[:, c * 2:(c + 1) * 2, :].rearrange("p a b -> p (a b)"))
                kb.dma("pool", w2b_d[:, c * 2:(c + 1) * 2, :].rearrange("p a b -> p (a b)"), w2c[:, :], reads=[w2c])

        esC = ExitStack()
        kb.es = esC
        KTd = [kb.sb([128, NK * 128], BF16, f"KTd{i}") for i in range(2)]
        Vs = kb.sb([128, NK, 130], BF16, "Vs")
        QTs = kb.sb([128, 4, nq], BF16, "QTs")
        for kv in range(2):
            kb.dma("sp", KTd[kv][0:64, :], kT_d[kv * 64:(kv + 1) * 64, :], writes=[KTd[kv]])
            kb.dma("sp", KTd[kv][64:128, :], kT_d[kv * 64:(kv + 1) * 64, :], writes=[KTd[kv]])
        kb.dma("sp", Vs[:, :, :], v_d.rearrange("v p c -> p v c"), writes=[Vs])
        for pr in range(4):
            kb.dma("sp", QTs[:, pr, :], qT_d[pr, :, :], writes=[QTs])
        emit_weight_prep()
        QB = min(512, nq)
        PTs = [kb.sb([128, QB], BF16, f"PT{i}") for i in range(6)]
        osb = kb.sb([65, QB], F32, "osb")
        rc = kb.sb([65, QB], F32, "rc")
        oab = kb.sb([64, QB], BF16, "oab")
        pti = 0
        osbs = [osb, kb.sb([65, QB], F32, "osb1")]
        rcs = [rc, kb.sb([65, QB], F32, "rc1")]
        oabs = [oab, kb.sb([64, QB], BF16, "oab1")]
        for q0 in range(0, nq, QB):
            for pr in range(4):
                kv = pr // 2
                pOs = [kb.reserve(), kb.reserve()]

                def qk(kt):
                    outs = []
                    for half in range(2):
                        ps_ = slice(half * 64, half * 64 + 64)
                        pS_ = kb.bank()
                        kb.mm(pS_, pS_[:, 0:QB], KTd[kv], KTd[kv][ps_, kt * 128:(kt + 1) * 128], QTs, QTs[ps_, pr, q0:q0 + QB])
                        outs.append(pS_)
                    return outs

                LOOK = 1
                pend = [qk(kt) for kt in range(min(LOOK, NK))]
                for kt in range(NK):
                    pSs = pend.pop(0)
                    PTp = []
                    for half in range(2):
                        PT = PTs[pti % len(PTs)]
                        pti += 1
                        kb.act(PT, PT[:, :], pSs[half], pSs[half][:, 0:QB], AF.Exp, scale=0.125)
                        PTp.append(PT)
                    if kt + LOOK < NK:
                        pend.append(qk(kt + LOOK))
                    for half in range(2):
                        kb.mm(pOs[half], pOs[half][0:65, 0:QB], Vs, Vs[:, kt, kv * 65:(kv + 1) * 65], PTp[half], PTp[half][:, :],
                              start=(kt == 0), stop=(kt == NK - 1))
                for half in range(2):
                    hh = pr * 2 + half
                    o_, r_, ob_ = osbs[half], rcs[half], oabs[half]
                    kb.cp("act", o_, o_[:, :], pOs[half], pOs[half][0:65, 0:QB])
                    kb.release(pOs[half])
                    kb.op("dve", lambda g, r_=r_, o_=o_: g.reciprocal(r_[64:65, :], o_[64:65, :]), reads=[o_], writes=[r_])
                    pB = kb.bank()
                    kb.mm(pB, pB[0:64, 0:QB], onesf, onesf[64:65, 0:64], r_, r_[64:65, :])
                    kb.tt("dve", ob_, ob_[:, :], o_, o_[0:64, :], pB, pB[0:64, 0:QB], ALU.mult)
                    kb.dma("sp", oa_d[:, hh, q0:q0 + QB], ob_[:, :], reads=[ob_])
        barrier()
        for ob_ in oabs:
            for dd in list(ob_.r.values()):
                kb._wait("sp", dd)
        for dd in list(w2c.r.values()):
            kb._wait("sp", dd)
            kb._wait("pool", dd)
        esC.close()
        kb.es = esCD

        esD = ExitStack()
        kb.es = esD
        G1 = kb.sb([128, D], F32, "G1")
        S2 = kb.sb([128, D], F32, "S2")
        H2 = kb.sb([128, D], F32, "H2")
        G2 = kb.sb([128, D], F32, "G2")
        for i_, bb in enumerate((G1, S2, H2, G2)):
            kb.dma("sp", bb[:, :], mod_d[i_, :, :], writes=[bb])
        TB = min(2, nown)
        NB = TB * 128
        oat = kb.sb([64, 8, NB], BF16, "oat")
        mdt = kb.sb([128, TB, 512], BF16, "mdt")
        xo = kb.sb([128, D], F32, "xo")
        x1b = kb.sb([128, TB, D], F32, "x1b")
        junkD = kb.sb([128, D], BF16, "junkD")
        ssqD = kb.sb([128, 1], F32, "ssqD")
        rstdD = kb.sb([128, 1], F32, "rstdD")
        t32D = kb.sb([128, D], F32, "t32D")
        hbD = kb.sb([128, D], BF16, "hbD")
        h2T = kb.sb([128, 8, NB], BF16, "h2T")
        w2s = [kb.sb([128, 4, D], BF16, f"w2s{i}") for i in range(2)]
        rls = [kb.sb([128, NB], F32, f"rl{i}") for i in range(2)]
        aT = [kb.sb([128, NB], BF16, f"aT{i}") for i in range(3)]
        yo = kb.sb([128, D], F32, "yo")
        ai = 0
        out_deps = []
        for blk0 in range(0, nown, TB):
            t0 = blk0 * 128
            kb.dma("sp", oat[:, :, :], oa_d[:, :, t0:t0 + NB], writes=[oat])
            for t in range(TB):
                kb.dma("sp", mdt[:, t, :], mdn_d[blk0 + t, :, :], writes=[mdt])
            for t in range(TB):
                m = blk0 + t
                kb.dma("sp", xo[:, :], xown_d[m, :, :], writes=[xo])
                for half in range(2):
                    cs = slice(half * 512, half * 512 + 512)
                    pM = kb.bank()
                    for hh in range(8):
                        kb.mm(pM, pM[:, :], oat, oat[:, hh, t * 128:(t + 1) * 128], woab, woab[:, hh, cs], start=(hh == 0), stop=False)
                    for hd in range(4):
                        kb.mm(pM, pM[:, :], mdt, mdt[:, t, hd * 128:(hd + 1) * 128], wodb, wodb[:, hd, cs], start=False, stop=(hd == 3))
                    kb.tt("dve", x1b, x1b[:, t, cs], pM, pM[:, :], G1, G1[:, cs], ALU.mult)
                kb.tt("dve", x1b, x1b[:, t, :], x1b, x1b[:, t, :], xo, xo[:, :], ALU.add)
                kb.act(junkD, junkD[:, :], x1b, x1b[:, t, :], AF.Square, accum=ssqD[:, :], extra_writes=[ssqD])
                kb.act(rstdD, rstdD[:, :], ssqD, ssqD[:, :], AF.Ln, bias=EPS, scale=1.0 / D)
                kb.act(rstdD, rstdD[:, :], rstdD, rstdD[:, :], AF.Exp, scale=-0.5)
                kb.stt(t32D, t32D[:, :], x1b, x1b[:, t, :], rstdD[:, 0:1], S2, S2[:, :], ALU.mult, ALU.mult, extra_reads=[rstdD])
                kb.tt("dve", hbD, hbD[:, :], t32D, t32D[:, :], H2, H2[:, :], ALU.add)
                pb = kb.bank()
                pbv = pb.t[:, :].bitcast(BF16)
                for k in range(8):
                    kb.tr(pb, pbv[:, k * 128:(k + 1) * 128], hbD, hbD[:, k * 128:(k + 1) * 128], identb, identb[:, :])
                kb.cp("act", h2T, h2T[:, :, t * 128:(t + 1) * 128], pb, pbv.rearrange("p (k t) -> p k t", k=8))
            acc = [[kb.reserve() for _ in range(2)] for _ in range(TB)]
            def mlp1(f):
                pA1_ = kb.bank()
                for k in range(8):
                    kb.mm(pA1_, pA1_[:, 0:NB], w1b, w1b[:, k, f * 128:(f + 1) * 128], h2T, h2T[:, k, :], start=(k == 0), stop=(k == 7))
                return pA1_

            pendA = [mlp1(0)]
            for f in range(32):
                fg, fj = f // 4, f % 4
                ws = w2s[fg % 2]
                if fj == 0:
                    kb.dma("sp", ws[:, :, :], w2b_d[:, fg * 4:(fg + 1) * 4, :], writes=[ws])
                pA1 = pendA.pop(0)
                r_ = rls[f % 2]
                kb.act(r_, r_[:, :], pA1, pA1[:, 0:NB], AF.Relu)
                a_ = aT[ai % 3]
                ai += 1
                kb.tt("dve", a_, a_[:, :], r_, r_[:, :], r_, r_[:, :], ALU.mult)
                if f + 1 < 32:
                    pendA.append(mlp1(f + 1))
                for t in range(TB):
                    for half in range(2):
                        kb.mm(acc[t][half], acc[t][half][:, :], a_, a_[:, t * 128:(t + 1) * 128], ws, ws[:, fj, half * 512:half * 512 + 512],
                              start=(f == 0), stop=(f == 31))
            for t in range(TB):
                m = blk0 + t
                for half in range(2):
                    cs = slice(half * 512, half * 512 + 512)
                    kb.tt("dve", yo, yo[:, cs], acc[t][half], acc[t][half][:, :], G2, G2[:, cs], ALU.mult)
                    kb.release(acc[t][half])
                kb.tt("dve", yo, yo[:, :], yo, yo[:, :], x1b, x1b[:, t, :], ALU.add)
                out_deps.append(kb.dma("sp", out_d[m, :, :], yo[:, :], reads=[yo]))
        for dd in out_deps:
            kb._wait("sp", dd)
        for dd in list(yo.r.values()):
            kb._wait("sp", dd)
        barrier()
        esD.close()
        esW2.close()
        esCD.close()
        kb.es = es
        return nc, kb, es


def _rep(v, n=128):
    v = np.asarray(v, np.float32)
    return np.ascontiguousarray(np.broadcast_to(v[None, :], (n, v.shape[0])))


def _win_perm(dirs=(0, 1)):
    perm = list(range(0, 768))
    for base in (1280, 1792, 768):
        for h in range(4):
            perm += list(range(base + h * 128, base + (h + 1) * 128))
    perm += list(range(2304, 2816))
    for base in (2816, 2824):
        for dr in dirs:
            perm += [base + dr * 4 + h for h in range(4)]
    return np.array(perm)


def _halo_rows(seq, t0):
    out = np.zeros((4, seq.shape[1]), np.float32)
    fl = 1.0 if t0 - 2 >= 0 else 0.0
    fr = 1.0 if t0 + 130 <= seq.shape[0] else 0.0
    if fl:
        out[0:2] = seq[t0 - 2:t0]
    if fr:
        out[2:4] = seq[t0 + 128:t0 + 130]
    return out, fl, fr


def _rope_table(tile):
    out = np.zeros((128, 1280), np.float32)
    if tile is None:
        out[:, 0:640] = 1.0
        return out
    t = tile * 128 + np.arange(128)
    pos = [(t // 64).astype(np.float32), (t % 64).astype(np.float32)]
    inv = (np.float32(10000.0) ** (-np.arange(16, dtype=np.float32) / np.float32(16))).astype(np.float32)
    C = np.zeros((128, 64), np.float32)
    S = np.zeros((128, 64), np.float32)
    for blk in range(2):
        ang = (pos[blk][:, None] * inv[None, :]).astype(np.float32)
        c, sn = np.cos(ang).astype(np.float32), np.sin(ang).astype(np.float32)
        C[:, blk * 32:blk * 32 + 16] = c
        C[:, blk * 32 + 16:blk * 32 + 32] = c
        S[:, blk * 32:blk * 32 + 16] = -sn
        S[:, blk * 32 + 16:blk * 32 + 32] = sn
    out[:, 0:640] = np.tile(C, (1, 10))
    out[:, 640:1280] = np.tile(S, (1, 10))
    return out


def _const_tables(dirs):
    p = np.arange(128)[:, None]
    f = np.arange(128)[None, :]
    tri = np.zeros((128, 2, 128), np.float32)
    masks = np.zeros((128, 4, 512), np.float32)
    for ds, dr in enumerate(dirs):
        if dr == 0:
            tri[:, ds, :] = (p <= f)
            okS, okIT = (p > f), (f >= p)
        else:
            tri[:, ds, :] = (p >= f)
            okS, okIT = (p < f), (f <= p)
        masks[:, ds * 2 + 0, :] = np.tile(np.where(okS, 0.0, BIG), (1, 4))
        masks[:, ds * 2 + 1, :] = np.tile(np.where(okIT, 0.0, -BIG), (1, 4))
    im = np.zeros((128, 4, 512), np.float32)
    im[:, 0, :] = np.tile((p // 16) == (f // 16), (1, 4))
    for mi, sz in ((1, 16), (2, 32), (3, 64)):
        same = (p // (2 * sz)) == (f // (2 * sz))
        im[:, mi, :] = np.tile(same & (((p % (2 * sz)) >= sz) != ((f % (2 * sz)) >= sz)), (1, 4))
    return tri, masks, im


def _plan(i):
    own_lo = 16 * i
    fwd_pre = list(range(0, own_lo))
    bwd_pre = list(range(63, own_lo + 15, -1))
    fwd_own = list(range(own_lo, own_lo + 16))
    bwd_own = fwd_own[::-1]
    fwd_ctx, bwd_ctx = [0, 1], [1, 0]
    if i < 2:
        dirs = (0, 1)
        s1_ctx, s1_pre, s1_own = fwd_ctx, fwd_pre, fwd_own
        s2_ctx, s2_pre, s2_own = bwd_ctx, bwd_pre, bwd_own
    else:
        dirs = (1, 0)
        s1_ctx, s1_pre, s1_own = bwd_ctx, bwd_pre, bwd_own
        s2_ctx, s2_pre, s2_own = fwd_ctx, fwd_pre, fwd_own
    scan1 = [("ctx", t, 1.0) for t in s1_ctx]
    scan1 += [("lat", 0, 0.0)] * (16 - len(s1_pre)) + [("lat", t, 1.0) for t in s1_pre]
    scan1 += [("lat", t, 1.0) for t in s1_own]
    scan2 = [("ctx", t, 1.0) for t in s2_ctx]
    scan2 += [("lat", t, 0.0) for t in s1_pre] if len(s2_pre) < 48 else []
    scan2 += [("lat", t, 1.0) for t in s2_pre]
    scan2 += [("lat", t, 1.0) for t in s2_own]
    assert len(scan1) == N1 and len(scan2) == N2, (len(scan1), len(scan2))
    assert sorted(t for k, t, _ in scan2 if k == "lat") == list(range(64))
    return dirs, scan1, scan2, s2_own


def kernel(**inputs):
    set_cfg(16, 48, 16)
    g = lambda k: np.asarray(inputs[k], np.float32)
    x, c, ctx, c_ctx = g("x"), g("c"), g("ctx"), g("c_ctx")
    w_mod, w_in, conv_w = g("w_mod")[0], g("w_in")[0], g("conv_w")[0]
    w_out, w1, w2 = g("w_out")[0], g("w_mlp1")[0], g("w_mlp2")[0]
    a_log, dt_bias = g("a_log")[0], g("dt_bias")[0]

    shared = {
        "wmod": np.ascontiguousarray(w_mod.reshape(8, 128, 6, D).transpose(2, 1, 0, 3).reshape(6, 128, 8 * D)),
        "bmod": _rep(g("b_mod")[0]), "n1w": _rep(g("norm1_w")[0]), "n2w": _rep(g("norm2_w")[0]),
        "qkw": _rep(np.concatenate([np.tile(g("q_norm_w")[0], 8), np.tile(g("k_norm_w")[0], 2)])),
        "dnw": _rep(np.tile(g("dn_norm_w")[0], 4)),
        "woa": np.ascontiguousarray(w_out[0:512].reshape(8, 64, D).transpose(1, 0, 2)),
        "wod": np.ascontiguousarray(w_out[512:1024].reshape(4, 128, D).transpose(1, 0, 2)),
        "w1": np.ascontiguousarray(w1.reshape(8, 128, 4 * D).transpose(1, 0, 2)),
        "w2": np.ascontiguousarray(w2.reshape(32, 128, D).transpose(1, 0, 2)),
        "ident": np.eye(128, dtype=np.float32),
    }
    cw = np.zeros((128, 12, 5), np.float32)
    blk = 0
    for base in (512, 1024, 0):
        for h in range(4):
            cw[:, blk, :] = conv_w[:, base + h * 128:base + (h + 1) * 128].T
            blk += 1
    shared["convw"] = np.ascontiguousarray(cw.reshape(128, 60))
    per_dirs = {}
    for dirs in ((0, 1), (1, 0)):
        tri, masks, im = _const_tables(dirs)
        per_dirs[dirs] = {
            "tri": tri, "masks": masks, "imask": im,
            "win": np.ascontiguousarray(w_in[:, _win_perm(dirs)].reshape(8, 128, NCOL).transpose(1, 0, 2)),
            "alog": _rep(np.concatenate([a_log[dirs[0]], a_log[dirs[1]]])),
            "dtb": _rep(np.concatenate([dt_bias[dirs[0]], dt_bias[dirs[1]]])),
        }
    rope_cache = {t: _rope_table(t) for t in list(range(64)) + [None]}

    in_maps, own_maps = [], []
    for core in range(8):
        b, i = core // 4, core % 4
        dirs, scan1, scan2, own_tiles = _plan(i)
        feed = dict(shared)
        feed.update(per_dirs[dirs])
        feed["cc"] = np.ascontiguousarray(np.concatenate([c[b].reshape(8, 128).T, c_ctx.reshape(8, 128).T], axis=1))
        flags = np.zeros((128, 2, N2, 4), np.float32)
        seqs = {"ctx": ctx[b], "lat": x[b]}
        for ds, (scan, n) in enumerate(((scan1, N1), (scan2, N2))):
            xs = np.zeros((n, 128, D), np.float32)
            xh = np.zeros((n, 4, D), np.float32)
            for vis, (kind, t, gflag) in enumerate(scan):
                seq = seqs[kind]
                xs[vis] = seq[t * 128:(t + 1) * 128]
                xh[vis], fl, fr = _halo_rows(seq, t * 128)
                flags[:, ds, vis, 0:3] = (fl, fr, gflag)
            feed["xs%d" % (ds + 1)] = xs
            feed["xh%d" % (ds + 1)] = xh
        feed["flags"] = flags
        feed["rope2"] = np.stack([rope_cache[None if kind == "ctx" else t] for kind, t, _ in scan2])
        feed["xown"] = np.stack([x[b, t * 128:(t + 1) * 128] for t in own_tiles])
        in_maps.append(feed)
        own_maps.append((b, own_tiles))

    nc, kb, _ = build_program()
    res = run_bass_kernel_spmd(nc, in_maps, core_ids=list(range(8)))
    out = np.zeros((2, T, D), np.float32)
    for core, (b, own_tiles) in enumerate(own_maps):
        o = np.asarray(res.results[core]["out"], np.float32)
        for m, t in enumerate(own_tiles):
            out[b, t * 128:(t + 1) * 128] = o[m]
    return out
```

```python
import numpy as np
from contextlib import ExitStack
import concourse.bass as bass
import concourse.mybir as mybir
from concourse.bass_utils import run_bass_kernel_spmd

F32 = mybir.dt.float32
BF16 = mybir.dt.bfloat16
AF = mybir.ActivationFunctionType
ALU = mybir.AluOpType
AX = mybir.AxisListType

D = 1024
T = 8192
NCTX = 256
NT = T // 128
QT = 16
EPS = 1e-6
BIG = 1.0e9
DEBUG = False
SAME_ENGINE_SYNC = True


class Buf:
    __slots__ = ("t", "name", "w", "r", "dsem", "dcnt")

    def __init__(self, t, name):
        self.t = t
        self.name = name
        self.w = None
        self.r = {}
        self.dsem = None
        self.dcnt = 0

    def __getitem__(self, k):
        return self.t[k]


class KB:
    def __init__(self, nc, es):
        self.nc = nc
        self.es = es
        self.eng = {"pe": nc.tensor, "act": nc.scalar, "dve": nc.vector, "pool": nc.gpsimd, "sp": nc.sync}
        self.sem = {e: es.enter_context(nc.semaphore("sem_" + e)) for e in ("pe", "act", "dve", "pool")}
        self.cnt = {e: 0 for e in self.sem}
        self.seen = {e: {} for e in self.eng}
        self.nbuf = 0
        self.banks = []
        self.bank_i = 0
        self.ninst = 0
        self.reserved = set()

    def sb(self, shape, dtype, name=None):
        self.nbuf += 1
        name = "s_" + (name or f"b{self.nbuf}")
        return Buf(self.es.enter_context(self.nc.sbuf_tensor(name, list(shape), dtype)), name)

    def init_psum(self):
        for i in range(8):
            t = self.es.enter_context(self.nc.psum_tensor(f"psb{i}", [128, 512], F32))
            self.banks.append(Buf(t, f"psb{i}"))

    def bank(self):
        while True:
            b = self.banks[self.bank_i % 8]
            self.bank_i += 1
            if b.name not in self.reserved:
                return b

    def reserve(self):
        b = self.bank()
        self.reserved.add(b.name)
        return b

    def release(self, b):
        self.reserved.discard(b.name)

    def _wait(self, e, dep):
        if dep is None:
            return
        kind = dep[0]
        if kind == "dma":
            key = ("dma", dep[3])
            semh = dep[1]
            val = dep[2]
        else:
            if kind == e and (e == "pe" or not SAME_ENGINE_SYNC):
                return
            key = kind
            semh = self.sem[kind]
            val = dep[1]
        if self.seen[e].get(key, 0) >= val:
            return
        self.eng[e].wait_ge(semh, val)
        self.seen[e][key] = val

    def _deps(self, e, reads, writes):
        for b in reads:
            self._wait(e, b.w)
        for b in writes:
            self._wait(e, b.w)
            for d in list(b.r.values()):
                self._wait(e, d)

    def op(self, e, fn, reads=(), writes=()):
        self._deps(e, reads, writes)
        ins = fn(self.eng[e])
        self.cnt[e] += 1
        self.ninst += 1
        ins.then_inc(self.sem[e], 1)
        dep = (e, self.cnt[e])
        for b in reads:
            b.r[e] = dep
        for b in writes:
            b.w = dep
            b.r = {}
        return ins

    def dma(self, q, out, in_, reads=(), writes=()):
        b = writes[0] if writes else reads[0]
        if b.dsem is None:
            b.dsem = self.es.enter_context(self.nc.semaphore("dsem_" + b.name))
        if b.dcnt > 0:
            self._wait(q, ("dma", b.dsem, b.dcnt, b.name))
        self._deps(q, reads, writes)
        ins = self.eng[q].dma_start(out=out, in_=in_)
        b.dcnt += 16
        self.ninst += 1
        ins.then_inc(b.dsem, 16)
        dep = ("dma", b.dsem, b.dcnt, b.name)
        for x in writes:
            x.w = dep
            x.r = {}
        for x in reads:
            x.r[("dma", b.name)] = dep
        return dep

    def mm(self, out_b, out_ap, lhsT_b, lhsT_ap, rhs_b, rhs_ap, start=True, stop=True):
        return self.op("pe", lambda g: g.matmul(out_ap, lhsT_ap, rhs_ap, start=start, stop=stop),
                       reads=[lhsT_b, rhs_b], writes=[out_b])

    def tr(self, out_b, out_ap, in_b, in_ap, id_b, id_ap):
        return self.op("pe", lambda g: g.transpose(out_ap, in_ap, id_ap), reads=[in_b, id_b], writes=[out_b])

    def act(self, out_b, out_ap, in_b, in_ap, func, bias=None, scale=None, accum=None, extra_reads=(), extra_writes=()):
        kw = {}
        if bias is not None:
            kw["bias"] = bias
        if scale is not None:
            kw["scale"] = scale
        if accum is not None:
            kw["accum_out"] = accum
        return self.op("act", lambda g: g.activation(out_ap, in_ap, func, **kw),
                       reads=[in_b] + list(extra_reads), writes=[out_b] + list(extra_writes))

    def tt(self, e, out_b, out_ap, a_b, a_ap, b_b, b_ap, op):
        return self.op(e, lambda g: g.tensor_tensor(out_ap, a_ap, b_ap, op), reads=[a_b, b_b], writes=[out_b])

    def ts(self, e, out_b, out_ap, a_b, a_ap, s1, s2, op0, op1=None, extra_reads=()):
        if op1 is None:
            f = lambda g: g.tensor_scalar(out_ap, a_ap, s1, None, op0)
        else:
            f = lambda g: g.tensor_scalar(out_ap, a_ap, s1, s2, op0, op1)
        return self.op(e, f, reads=[a_b] + list(extra_reads), writes=[out_b])

    def stt(self, out_b, out_ap, a_b, a_ap, scalar, b_b, b_ap, op0, op1, extra_reads=()):
        return self.op("dve", lambda g: g.scalar_tensor_tensor(out_ap, a_ap, scalar, b_ap, op0, op1),
                       reads=[a_b, b_b] + list(extra_reads), writes=[out_b])

    def cp(self, e, out_b, out_ap, in_b, in_ap):
        if e == "act":
            return self.op("act", lambda g: g.copy(out_ap, in_ap), reads=[in_b], writes=[out_b])
        return self.op(e, lambda g: g.tensor_copy(out_ap, in_ap), reads=[in_b], writes=[out_b])

    def memset(self, e, b, ap, val):
        return self.op(e, lambda g: g.memset(ap, val), writes=[b])


GOFF = 768
ZOFF = 768 + 1536
GATE = ZOFF + 512
NCOL = GATE + 16
CFG = {"pre1": 16, "pre2": 48, "own": 16}
N1 = 2 + CFG["pre1"] + CFG["own"]
N2 = 2 + CFG["pre2"] + CFG["own"]
DK = 128


def set_cfg(pre1, pre2, own):
    global N1, N2
    CFG.update(pre1=pre1, pre2=pre2, own=own)
    N1 = 2 + pre1 + own
    N2 = 2 + pre2 + own


def build_program():
    nc = bass.Bass("TRN2", target_bir_lowering=False)

    def din(name, shape, dt=F32):
        return nc.dram_tensor(name, list(shape), dt, kind="ExternalInput").ap()

    xs_d = [din("xs1", [N1, 128, D]), din("xs2", [N2, 128, D])]
    xh_d = [din("xh1", [N1, 4, D]), din("xh2", [N2, 4, D])]
    rope_d = din("rope2", [N2, 128, 1280])
    flg_d = din("flags", [128, 2, N2, 4])
    cc_d = din("cc", [128, 16])
    wmod_d = din("wmod", [6, 128, 8 * D])
    bmod_d = din("bmod", [128, 6 * D])
    n1w_d = din("n1w", [128, D])
    n2w_d = din("n2w", [128, D])
    win_d = din("win", [128, 8, NCOL])
    qkw_d = din("qkw", [128, 640])
    cw_d = din("convw", [128, 60])
    alog_d = din("alog", [128, 8])
    dtb_d = din("dtb", [128, 8])
    dnw_d = din("dnw", [128, 512])
    woa_d = din("woa", [64, 8, D])
    wod_d = din("wod", [128, 4, D])
    w1_d = din("w1", [128, 8, 4 * D])
    w2_d = din("w2", [128, 32, D])
    xown_d = din("xown", [CFG["own"], 128, D])
    ident_d = din("ident", [128, 128])
    tri_d = din("tri", [128, 2, 128])
    msk_d = din("masks", [128, 4, 512])
    imk_d = din("imask", [128, 4, 512])
    out_d = nc.dram_tensor("out", [CFG["own"], 128, D], F32, kind="ExternalOutput").ap()

    kT_d = nc.dram_tensor("kT_s", [128, N2 * 128], BF16).ap()
    v_d = nc.dram_tensor("v_s", [N2, 128, 130], BF16).ap()
    qT_d = nc.dram_tensor("qT_s", [4, 128, CFG["own"] * 128], BF16).ap()
    o1_d = nc.dram_tensor("o1_s", [CFG["own"], 128, 512], F32).ap()
    mdn_d = nc.dram_tensor("mdn_s", [CFG["own"], 128, 512], BF16).ap()
    mod_d = nc.dram_tensor("mod_s", [4, 128, D], F32).ap()
    oa_d = nc.dram_tensor("oa_s", [64, 8, CFG["own"] * 128], BF16).ap()
    w2b_d = nc.dram_tensor("w2b_s", [128, 32, D], BF16).ap()

    with ExitStack() as es:
        kb = KB(nc, es)
        kb.init_psum()

        def barrier():
            for e in ("pe", "act", "dve", "pool", "sp"):
                for o in ("pe", "act", "dve", "pool"):
                    if o != e:
                        kb._wait(e, (o, kb.cnt[o]))

        ident = kb.sb([128, 128], F32, "ident")
        identb = kb.sb([128, 128], BF16, "identb")
        onesf = kb.sb([128, 128], F32, "onesf")
        kb.dma("sp", ident[:, :], ident_d[:, :], writes=[ident])
        kb.cp("dve", identb, identb[:, :], ident, ident[:, :])
        kb.memset("dve", onesf, onesf[:, :], 1.0)

        S1 = kb.sb([128, D], F32, "S1")
        H1 = kb.sb([128, D], F32, "H1")
        S1c = kb.sb([128, D], F32, "S1c")
        H1c = kb.sb([128, D], F32, "H1c")

        with ExitStack() as esA:
            kbA = kb
            old_es = kb.es
            kb.es = esA
            cc = kb.sb([128, 16], F32, "cc")
            sg = kb.sb([128, 16], F32, "ccsg")
            Lc = kb.sb([128, 16, 128], F32, "Lc")
            bm = kb.sb([128, 6 * D], F32, "bm")
            nw = kb.sb([128, D], F32, "nw")
            stage = [kb.sb([128, 8 * D], F32, f"wmst{i}") for i in range(2)]
            res = kb.sb([128, D], F32, "modres")
            kb.dma("sp", cc[:, :], cc_d[:, :], writes=[cc])
            kb.dma("sp", bm[:, :], bmod_d[:, :], writes=[bm])
            kb.dma("sp", nw[:, :], n1w_d[:, :], writes=[nw])
            kb.act(sg, sg[:, :], cc, cc[:, :], AF.Exp, scale=-1.0)
            kb.ts("dve", sg, sg[:, :], sg, sg[:, :], 1.0, None, ALU.add)
            kb.op("dve", lambda g: g.reciprocal(sg[:, :], sg[:, :]), reads=[sg], writes=[sg])
            kb.tt("dve", sg, sg[:, :], sg, sg[:, :], cc, cc[:, :], ALU.mult)
            for j in range(16):
                kb.ts("dve", Lc, Lc[:, j, :], onesf, onesf[:, :], sg[:, j:j + 1], None, ALU.mult, extra_reads=[sg])

            def mod_block(blk, lofs, dst_fn):
                st = stage[blk % 2]
                for half in range(2):
                    pb = kb.bank()
                    for k in range(8):
                        kb.mm(pb, pb[:, :], Lc, Lc[:, lofs + k, :], st, st[:, k * D + half * 512:k * D + half * 512 + 512],
                              start=(k == 0), stop=(k == 7))
                    dst_fn(pb, half)

            for blk in range(6):
                st = stage[blk % 2]
                kb.dma("sp", st[:, :], wmod_d[blk, :, :], writes=[st])
                if blk == 3:
                    kb.dma("sp", nw[:, :], n2w_d[:, :], writes=[nw])

                def plain(dst):
                    def f(pb, half, dst=dst, blk=blk):
                        sl = slice(half * 512, half * 512 + 512)
                        kb.tt("dve", dst, dst[:, sl], pb, pb[:, :], bm, bm[:, blk * D + half * 512:blk * D + half * 512 + 512], ALU.add)
                    return f

                def scale(dst):
                    def f(pb, half, dst=dst, blk=blk):
                        sl = slice(half * 512, half * 512 + 512)
                        kb.tt("dve", dst, dst[:, sl], pb, pb[:, :], bm, bm[:, blk * D + half * 512:blk * D + half * 512 + 512], ALU.add)
                        kb.stt(dst, dst[:, sl], dst, dst[:, sl], 1.0, nw, nw[:, sl], ALU.add, ALU.mult)
                    return f

                def spill(idx, fn_maker):
                    def f(pb, half, idx=idx):
                        fn_maker(res)(pb, half)
                        if half == 1:
                            kb.dma("sp", mod_d[idx, :, :], res[:, :], reads=[res])
                    return f

                if blk == 0:
                    mod_block(blk, 0, plain(H1))
                    mod_block(blk, 8, plain(H1c))
                elif blk == 1:
                    mod_block(blk, 0, scale(S1))
                    mod_block(blk, 8, scale(S1c))
                elif blk == 2:
                    mod_block(blk, 0, spill(0, plain))
                elif blk == 3:
                    mod_block(blk, 0, spill(2, plain))
                elif blk == 4:
                    mod_block(blk, 0, spill(1, scale))
                else:
                    mod_block(blk, 0, spill(3, plain))
            if DEBUG:
                dbg = nc.dram_tensor("dbgA", [8, 128, D], F32, kind="ExternalOutput").ap()
                for n_, b_ in enumerate((S1, H1, S1c, H1c)):
                    dd = kb.dma("sp", dbg[n_, :, :], b_[:, :], reads=[b_])
                    kb._wait("sp", dd)
                tmpd = kb.sb([128, D], F32, "dbgtmp")
                for n_ in range(4):
                    kb.dma("sp", tmpd[:, :], mod_d[n_, :, :], writes=[tmpd])
                    dd = kb.dma("sp", dbg[4 + n_, :, :], tmpd[:, :], reads=[tmpd])
                    kb._wait("sp", dd)
            barrier()
            kb.es = old_es

        esB = ExitStack()
        kb.es = esB
        wb = kb.sb([128, 8, NCOL], BF16, "wb")
        cdiag = kb.sb([128, 12, 5, 128], BF16, "cdiag")
        flg = kb.sb([128, 2, N2, 4], F32, "flg")
        with ExitStack() as esW:
            old_es = kb.es
            kb.es = esW
            wst = kb.sb([128, 8, 708], F32, "wst")
            cw = kb.sb([128, 60], F32, "cw")
            for part in range(4):
                sl = slice(part * 708, (part + 1) * 708)
                kb.dma("sp", wst[:, :, :], win_d[:, :, sl], writes=[wst])
                kb.cp("dve" if part % 2 == 0 else "pool", wb, wb[:, :, sl], wst, wst[:, :, :])
            kb.dma("sp", cw[:, :], cw_d[:, :], writes=[cw])
            kb.dma("sp", flg[:, :, :, :], flg_d[:, :, :, :], writes=[flg])
            for blk in range(12):
                for j in range(5):
                    kb.ts("dve", cdiag, cdiag[:, blk, j, :], ident, ident[:, :], cw[:, blk * 5 + j:blk * 5 + j + 1], None,
                          ALU.mult, extra_reads=[cw])
            barrier()
            kb.es = old_es

        xt = [kb.sb([128, D], F32, f"xt{i}") for i in range(2)]
        xh = kb.sb([4, D], F32, "xh")
        junk = kb.sb([128, D], BF16, "junk")
        ssq = kb.sb([128, 1], F32, "ssq")
        rstd = kb.sb([128, 1], F32, "rstd")
        t32 = kb.sb([128, D], F32, "t32")
        hb = kb.sb([128, D], BF16, "hb")
        hT = kb.sb([128, 8, 132], BF16, "hT")
        preT = kb.sb([128, 12, 132], BF16, "preT")
        sgt = kb.sb([128, 512], F32, "sgt")
        class FS:
            def __init__(self, i):
                self.cv = kb.sb([128, 12 * 128], F32, f"cv{i}")
                for nm, w in (("eb", 4), ("beta", 4), ("xa", 4), ("ab", 4), ("en", 4), ("lp", 4), ("g_", 4), ("gcs", 8),
                              ("e_", 4), ("egl", 4), ("dgl", 4), ("el", 4), ("ngc", 4), ("negbeta", 4)):
                    setattr(self, nm, kb.sb([128, w], F32, f"{nm}{i}"))
                self.zs = kb.sb([128, 512], F32, f"zs{i}")

        fsets = [FS(0), FS(1)]

        def norm_rows(xbuf, np_, Sb, Hb):
            kb.act(junk, junk[0:np_, :], xbuf, xbuf[0:np_, :], AF.Square, accum=ssq[0:np_, :], extra_writes=[ssq])
            kb.act(rstd, rstd[0:np_, :], ssq, ssq[0:np_, :], AF.Ln, bias=EPS, scale=1.0 / D)
            kb.act(rstd, rstd[0:np_, :], rstd, rstd[0:np_, :], AF.Exp, scale=-0.5)
            kb.stt(t32, t32[0:np_, :], xbuf, xbuf[0:np_, :], rstd[0:np_, 0:1], Sb, Sb[0:np_, :], ALU.mult, ALU.mult,
                   extra_reads=[rstd])
            kb.tt("dve", hb, hb[0:np_, :], t32, t32[0:np_, :], Hb, Hb[0:np_, :], ALU.add)

        def norm_tile(ds, vis, xbuf, Sb, Hb):
            kb.dma("sp", xbuf[:, :], xs_d[ds][vis, :, :], writes=[xbuf])
            kb.dma("sp", xh[:, :], xh_d[ds][vis, :, :], writes=[xh])
            norm_rows(xbuf, 128, Sb, Hb)
            pb = kb.bank()
            pbv = pb.t[:, :].bitcast(BF16)
            for k in range(8):
                kb.tr(pb, pbv[:, k * 128:(k + 1) * 128], hb, hb[:, k * 128:(k + 1) * 128], identb, identb[:, :])
            kb.cp("act", hT, hT[:, :, 2:130], pb, pbv.rearrange("p (k t) -> p k t", k=8))
            norm_rows(xh, 4, Sb, Hb)
            pb2 = kb.bank()
            pbv2 = pb2.t[:, :].bitcast(BF16)
            for k in range(8):
                kb.tr(pb2, pbv2[:, k * 4:(k + 1) * 4], hb, hb[0:4, k * 128:(k + 1) * 128], identb, identb[0:4, 0:4])
            hv = pbv2[:, 0:32].rearrange("p (k t) -> p k t", k=8)
            kb.cp("act", hT, hT[:, :, 0:2], pb2, hv[:, :, 0:2])
            kb.cp("act", hT, hT[:, :, 130:132], pb2, hv[:, :, 2:4])

        def project_tok(col0, ncols):
            pb = kb.bank()
            for k in range(8):
                kb.mm(pb, pb[:, 0:ncols], hT, hT[:, k, 2:130], wb, wb[:, k, col0:col0 + ncols], start=(k == 0), stop=(k == 7))
            return pb

        def gdn_pre_conv(ds, vis, nblk, F):
            for g0 in range(0, nblk, 3):
                pb = kb.bank()
                for j in range(3):
                    blk = g0 + j
                    for k in range(8):
                        kb.mm(pb, pb[:, j * 132:(j + 1) * 132], wb, wb[:, k, GOFF + blk * 128:GOFF + (blk + 1) * 128],
                              hT, hT[:, k, :], start=(k == 0), stop=(k == 7))
                kb.cp("act", preT, preT[:, g0:g0 + 3, :], pb, pb[:, 0:396].rearrange("p (j t) -> p j t", j=3))
                yield
            kb.ts("dve", preT, preT[:, 0:nblk, 0:2], preT, preT[:, 0:nblk, 0:2], flg[:, ds, vis, 0:1], None, ALU.mult, extra_reads=[flg])
            kb.ts("dve", preT, preT[:, 0:nblk, 130:132], preT, preT[:, 0:nblk, 130:132], flg[:, ds, vis, 1:2], None, ALU.mult, extra_reads=[flg])
            for g0 in range(0, nblk, 4):
                pb = kb.bank()
                for j4 in range(4):
                    blk = g0 + j4
                    for j in range(5):
                        kb.mm(pb, pb[:, j4 * 128:(j4 + 1) * 128], preT, preT[:, blk, j:j + 128], cdiag, cdiag[:, blk, j, :],
                              start=(j == 0), stop=(j == 4))
                kb.act(sgt, sgt[:, :], pb, pb[:, :], AF.Exp, scale=-1.0)
                kb.ts("dve", sgt, sgt[:, :], sgt, sgt[:, :], 1.0, None, ALU.add)
                kb.op("dve", lambda g: g.reciprocal(sgt[:, :], sgt[:, :]), reads=[sgt], writes=[sgt])
                kb.tt("dve", F.cv, F.cv[:, g0 * 128:(g0 + 4) * 128], pb, pb[:, :], sgt, sgt[:, :], ALU.mult)
                yield

        tri = kb.sb([128, 2, 128], F32, "tri")
        mskb = kb.sb([128, 4, 512], BF16, "mskb")
        negA = kb.sb([128, 8], F32, "negA")
        dtb = kb.sb([128, 8], F32, "dtb")
        identrep = kb.sb([128, 512], F32, "identrep")
        imk = kb.sb([128, 4, 512], BF16, "imk")
        with ExitStack() as esG:
            old_es = kb.es
            kb.es = esG
            mst = kb.sb([128, 4, 512], F32, "mst")
            kb.dma("sp", tri[:, :, :], tri_d[:, :, :], writes=[tri])
            kb.dma("sp", mst[:, :, :], msk_d[:, :, :], writes=[mst])
            kb.cp("dve", mskb, mskb[:, :, :], mst, mst[:, :, :])
            kb.dma("sp", mst[:, :, :], imk_d[:, :, :], writes=[mst])
            kb.cp("dve", imk, imk[:, :, :], mst, mst[:, :, :])
            kb.dma("sp", negA[:, :], alog_d[:, :], writes=[negA])
            kb.dma("sp", dtb[:, :], dtb_d[:, :], writes=[dtb])
            kb.act(negA, negA[:, :], negA, negA[:, :], AF.Exp)
            kb.ts("dve", negA, negA[:, :], negA, negA[:, :], -1.0, None, ALU.mult)
            for h in range(4):
                kb.cp("dve", identrep, identrep[:, h * 128:(h + 1) * 128], ident, ident[:, :])
            barrier()
            kb.es = old_es

        def sm(name, w=4):
            return kb.sb([128, w], F32, name)

        HW = 256

        class CH:
            def __init__(self, hp):
                n = f"c{hp}"
                self.hp = hp
                for nm in ("ssk", "rk", "c1", "c2", "ssq_", "rq", "cq", "cqe"):
                    setattr(self, nm, kb.sb([128, 2], F32, n + nm))
                self.junk2 = kb.sb([128, 128], BF16, n + "junk2")
                for nm in ("kn", "vb", "kbg", "kd", "qs", "qd", "knT", "qsT", "qdT", "Nb", "Zb", "Nk", "Zk", "Xb", "Yb",
                           "Cb", "Eb", "U1b", "U2b", "wT", "vnew", "intraT"):
                    setattr(self, nm, kb.sb([128, HW], BF16, n + nm))
                for nm in ("dg", "Ds", "DT", "u_"):
                    setattr(self, nm, kb.sb([128, HW], F32, n + nm))

        chains = [CH(0), CH(1)]
        SstC = [[kb.sb([128, HW], F32, f"SstC{d_}{hp}") for hp in range(2)] for d_ in range(2)]
        SbfC = [[kb.sb([128, HW], BF16, f"SbfC{d_}{hp}") for hp in range(2)] for d_ in range(2)]
        for d_ in range(2):
            for hp in range(2):
                kb.memset("dve", SstC[d_][hp], SstC[d_][hp][:, :], 0.0)
                kb.memset("dve", SbfC[d_][hp], SbfC[d_][hp][:, :], 0.0)

        def H(hh):
            return slice(hh * 128, (hh + 1) * 128)

        def tr4(dst, src):
            pb = kb.bank()
            pv = pb.t[:, :].bitcast(BF16)
            for hh in range(4):
                kb.tr(pb, pv[:, H(hh)], src, src[:, H(hh)], identb, identb[:, :])
            kb.cp("act", dst, dst[:, :], pb, pv[:, 0:512])

        def tr2(dst, src, e="act"):
            pb = kb.bank()
            pv = pb.t[:, :].bitcast(BF16)
            for lh in range(2):
                kb.tr(pb, pv[:, H(lh)], src, src[:, H(lh)], identb, identb[:, :])
            kb.cp(e, dst, dst[:, :], pb, pv[:, 0:HW])

        def rsq(dst, src, add):
            kb.act(dst, dst[:, :], src, src[:, :], AF.Ln, bias=add)
            kb.act(dst, dst[:, :], dst, dst[:, :], AF.Exp, scale=-0.5)

        def gdn_gates(ds, vis, own, F):
            c4 = slice(ds * 4, ds * 4 + 4)
            a4 = slice(8 + ds * 4, 8 + ds * 4 + 4)
            pg = project_tok(GATE, 16)
            fg = flg[:, ds, vis, 2:3]
            kb.act(F.eb, F.eb[:, :], pg, pg[:, c4], AF.Exp, scale=-1.0)
            kb.ts("dve", F.eb, F.eb[:, :], F.eb, F.eb[:, :], 1.0, None, ALU.add)
            kb.op("dve", lambda g: g.reciprocal(F.beta[:, :], F.eb[:, :]), reads=[F.eb], writes=[F.beta])
            kb.ts("dve", F.beta, F.beta[:, :], F.beta, F.beta[:, :], fg, None, ALU.mult, extra_reads=[flg])
            kb.ts("dve", F.negbeta, F.negbeta[:, :], F.beta, F.beta[:, :], -1.0, None, ALU.mult)
            kb.tt("dve", F.xa, F.xa[:, :], pg, pg[:, a4], dtb, dtb[:, c4], ALU.add)
            kb.act(F.ab, F.ab[:, :], F.xa, F.xa[:, :], AF.Abs)
            kb.act(F.en, F.en[:, :], F.ab, F.ab[:, :], AF.Exp, scale=-1.0)
            kb.act(F.lp, F.lp[:, :], F.en, F.en[:, :], AF.Ln, bias=1.0)
            kb.stt(F.g_, F.g_[:, :], F.xa, F.xa[:, :], 0.0, F.lp, F.lp[:, :], ALU.max, ALU.add)
            kb.tt("dve", F.g_, F.g_[:, :], F.g_, F.g_[:, :], negA, negA[:, c4], ALU.mult)
            kb.ts("dve", F.g_, F.g_[:, :], F.g_, F.g_[:, :], fg, None, ALU.mult, extra_reads=[flg])
            pgc = kb.bank()
            kb.mm(pgc, pgc[:, 0:4], tri, tri[:, ds, :], F.g_, F.g_[:, :])
            kb.mm(pgc, pgc[:, 4:8], onesf, onesf[:, :], F.g_, F.g_[:, :])
            kb.cp("act", F.gcs, F.gcs[:, :], pgc, pgc[:, 0:8])
            kb.act(F.e_, F.e_[:, :], F.gcs, F.gcs[:, 0:4], AF.Exp)
            kb.act(F.egl, F.egl[:, :], F.gcs, F.gcs[:, 4:8], AF.Exp)
            kb.tt("dve", F.dgl, F.dgl[:, :], F.gcs, F.gcs[:, 4:8], F.gcs, F.gcs[:, 0:4], ALU.subtract)
            kb.act(F.el, F.el[:, :], F.dgl, F.dgl[:, :], AF.Exp)
            if own:
                kb.ts("dve", F.ngc, F.ngc[:, :], F.gcs, F.gcs[:, 0:4], -1.0, None, ALU.mult)

        def gdn_chain(ds, vis, own, C, res, F):
            hp = C.hp
            S, Sb = SstC[ds][hp], SbfC[ds][hp]
            hs = (2 * hp, 2 * hp + 1)
            g2 = slice(2 * hp, 2 * hp + 2)
            mk = slice(0, HW)

            def mm2(l_b, r_b, acc_b=None):
                pb_ = kb.bank()
                for lh in range(2):
                    kb.mm(pb_, pb_[:, H(lh)], l_b, l_b[:, H(lh)], r_b, r_b[:, H(lh)], start=True, stop=(acc_b is None))
                    if acc_b is not None:
                        kb.mm(pb_, pb_[:, H(lh)], identb, identb[:, :], acc_b, acc_b[:, H(lh)], start=False, stop=True)
                return pb_

            for lh, hh in enumerate(hs):
                kb.act(C.junk2, C.junk2[:, :], F.cv, F.cv[:, H(hh)], AF.Square, accum=C.ssk[:, lh:lh + 1], extra_writes=[C.ssk])
            rsq(C.rk, C.ssk, EPS)
            yield
            kb.tt("dve", C.c1, C.c1[:, :], C.rk, C.rk[:, :], F.beta, F.beta[:, g2], ALU.mult)
            kb.tt("dve", C.c1, C.c1[:, :], C.c1, C.c1[:, :], F.e_, F.e_[:, g2], ALU.mult)
            kb.tt("dve", C.c2, C.c2[:, :], C.rk, C.rk[:, :], F.el, F.el[:, g2], ALU.mult)
            yield
            def scl(dst, dcol, src_col, sc_b, sc_ap):
                kb.act(dst, dst[:, H(dcol)], F.cv, F.cv[:, H(src_col)], AF.Identity, scale=sc_ap, extra_reads=[sc_b])

            for lh, hh in enumerate(hs):
                scl(C.kn, lh, hh, C.rk, C.rk[:, lh:lh + 1])
                scl(C.vb, lh, 4 + hh, F.beta, F.beta[:, hh:hh + 1])
                scl(C.kbg, lh, hh, C.c1, C.c1[:, lh:lh + 1])
                scl(C.kd, lh, hh, C.c2, C.c2[:, lh:lh + 1])
                yield
            tr2(C.knT, C.kn)
            yield
            if own:
                for lh, hh in enumerate(hs):
                    kb.act(C.junk2, C.junk2[:, :], F.cv, F.cv[:, H(8 + hh)], AF.Square, accum=C.ssq_[:, lh:lh + 1], extra_writes=[C.ssq_])
                rsq(C.rq, C.ssq_, EPS)
                kb.ts("dve", C.cq, C.cq[:, :], C.rq, C.rq[:, :], float(DK) ** -0.5, None, ALU.mult)
                kb.tt("dve", C.cqe, C.cqe[:, :], C.cq, C.cq[:, :], F.e_, F.e_[:, g2], ALU.mult)
                yield
                for lh, hh in enumerate(hs):
                    scl(C.qs, lh, 8 + hh, C.cq, C.cq[:, lh:lh + 1])
                    scl(C.qd, lh, 8 + hh, C.cqe, C.cqe[:, lh:lh + 1])
                yield
                tr2(C.qsT, C.qs)
                yield
                tr2(C.qdT, C.qd)
                yield
            pA = mm2(C.knT, C.knT)
            for lh, hh in enumerate(hs):
                kb.act(C.dg, C.dg[:, H(lh)], ident, ident[:, :], AF.Identity, scale=F.gcs[:, hh:hh + 1], extra_reads=[F.gcs])
            pG = kb.bank()
            kb.mm(pG, pG[:, 0:HW], onesf, onesf[:, :], C.dg, C.dg[:, :], start=True, stop=False)
            kb.mm(pG, pG[:, 0:HW], identb, identb[:, :], mskb, mskb[:, ds * 2, mk], start=False, stop=True)
            yield
            for lh, hh in enumerate(hs):
                kb.act(C.Ds, C.Ds[:, H(lh)], pG, pG[:, H(lh)], AF.Exp, bias=F.gcs[:, hh:hh + 1], scale=-1.0, extra_reads=[F.gcs])
            yield
            for lh, hh in enumerate(hs):
                kb.stt(C.Nb, C.Nb[:, H(lh)], pA, pA[:, H(lh)], F.negbeta[:, hh:hh + 1], C.Ds, C.Ds[:, H(lh)], ALU.mult, ALU.mult,
                       extra_reads=[F.negbeta])
            yield
            tr2(C.Zb, C.Nb)
            yield
            kb.tt("pool", C.Nk, C.Nk[:, :], C.Nb, C.Nb[:, :], imk, imk[:, 0, mk], ALU.mult)
            kb.tt("pool", C.Zk, C.Zk[:, :], C.Zb, C.Zb[:, :], imk, imk[:, 0, mk], ALU.mult)
            yield
            kb.tt("dve", C.Xb, C.Xb[:, :], C.Nk, C.Nk[:, :], identrep, identrep[:, mk], ALU.add)
            yield
            for lvl in range(1, 4):
                pN = mm2(C.Zk, C.Nk)
                if lvl < 3:
                    pZ = mm2(C.Nk, C.Zk)
                yield
                kb.cp("act", C.Nk, C.Nk[:, :], pN, pN[:, 0:HW])
                if lvl < 3:
                    kb.cp("dve", C.Zk, C.Zk[:, :], pZ, pZ[:, 0:HW])
                else:
                    tr2(C.Zk, C.Nk, e="dve")
                yield
                pX = mm2(C.Zk, C.Xb)
                yield
                kb.tt("dve", C.Xb, C.Xb[:, :], C.Xb, C.Xb[:, :], pX, pX[:, 0:HW], ALU.add)
                yield
            for mi in (1, 2, 3):
                kb.tt("pool", C.Eb, C.Eb[:, :], C.Zb, C.Zb[:, :], imk, imk[:, mi, mk], ALU.mult)
                yield
                pU1 = mm2(C.Eb, C.Xb)
                tr2(C.Yb, C.Xb, e="dve")
                yield
                kb.cp("act", C.U1b, C.U1b[:, :], pU1, pU1[:, 0:HW])
                yield
                pX = mm2(C.Yb, C.U1b)
                yield
                kb.tt("dve", C.Xb, C.Xb[:, :], C.Xb, C.Xb[:, :], pX, pX[:, 0:HW], ALU.add)
                yield
            tr2(C.Yb, C.Xb)
            yield
            pU = mm2(C.Yb, C.vb)
            pW = mm2(C.kbg, C.Yb)
            yield
            kb.cp("act", C.u_, C.u_[:, :], pU, pU[:, 0:HW])
            kb.cp("dve", C.wT, C.wT[:, :], pW, pW[:, 0:HW])
            yield
            pWS = mm2(C.wT, Sb)
            yield
            kb.tt("dve", C.vnew, C.vnew[:, :], C.u_, C.u_[:, :], pWS, pWS[:, 0:HW], ALU.subtract)
            yield
            if own:
                pGI = kb.bank()
                kb.mm(pGI, pGI[:, 0:HW], onesf, onesf[:, :], C.dg, C.dg[:, :], start=True, stop=False)
                kb.mm(pGI, pGI[:, 0:HW], identb, identb[:, :], mskb, mskb[:, ds * 2 + 1, mk], start=False, stop=True)
                pQK = mm2(C.knT, C.qsT)
                yield
                for lh, hh in enumerate(hs):
                    kb.act(C.DT, C.DT[:, H(lh)], pGI, pGI[:, H(lh)], AF.Exp, bias=F.ngc[:, hh:hh + 1], extra_reads=[F.ngc])
                yield
                kb.tt("dve", C.intraT, C.intraT[:, :], pQK, pQK[:, 0:HW], C.DT, C.DT[:, :], ALU.mult)
                yield
                pO = kb.reserve()
                for lh in range(2):
                    kb.mm(pO, pO[:, H(lh)], C.qdT, C.qdT[:, H(lh)], Sb, Sb[:, H(lh)], start=True, stop=False)
                    kb.mm(pO, pO[:, H(lh)], C.intraT, C.intraT[:, H(lh)], C.vnew, C.vnew[:, H(lh)], start=False, stop=True)
                res[hp] = pO
                yield
            pS = mm2(C.kd, C.vnew)
            yield
            for lh, hh in enumerate(hs):
                kb.stt(S, S[:, H(lh)], S, S[:, H(lh)], F.egl[:, hh:hh + 1], pS, pS[:, H(lh)], ALU.mult, ALU.add, extra_reads=[F.egl])
            kb.cp("act", Sb, Sb[:, :], S, S[:, :])
            yield

        qkw = kb.sb([128, 640], F32, "qkw")
        dnw = kb.sb([128, 512], F32, "dnw")
        kb.dma("sp", qkw[:, :], qkw_d[:, :], writes=[qkw])
        kb.dma("sp", dnw[:, :], dnw_d[:, :], writes=[dnw])
        rp = kb.sb([128, 1280], F32, "rp")
        sqt = kb.sb([128, 640], F32, "sqt")
        ss10 = kb.sb([128, 10], F32, "ss10")
        rr10 = kb.sb([128, 10], F32, "rr10")
        xw = kb.sb([128, 640], F32, "xw")
        sw = kb.sb([128, 640], F32, "sw")
        t1 = kb.sb([128, 640], F32, "t1")
        qkb = kb.sb([128, 640], BF16, "qkb")
        kTt = kb.sb([128, 128], BF16, "kTt")
        qTt = kb.sb([128, 512], BF16, "qTt")
        vt = kb.sb([128, 130], BF16, "vt")
        kb.memset("dve", vt, vt[:, :], 1.0)

        def attn_proj(vis, own, m):
            kb.dma("sp", rp[:, :], rope_d[vis, :, :], writes=[rp])
            c0 = 0 if own else 512
            nh0 = 0 if own else 8
            pKV = project_tok(512, 256)
            pQ = project_tok(0, 512) if own else None
            if own:
                kb.act(sqt, sqt[:, 0:512], pQ, pQ[:, :], AF.Square)
            kb.act(sqt, sqt[:, 512:640], pKV, pKV[:, 0:128], AF.Square)
            kb.op("dve", lambda g: g.tensor_reduce(ss10[:, nh0:10], sqt[:, c0:640].rearrange("p (h d) -> p h d", d=64), AX.X, ALU.add),
                  reads=[sqt], writes=[ss10])
            kb.act(rr10, rr10[:, nh0:10], ss10, ss10[:, nh0:10], AF.Ln, bias=EPS, scale=1.0 / 64)
            kb.act(rr10, rr10[:, nh0:10], rr10, rr10[:, nh0:10], AF.Exp, scale=-0.5)
            for j in range(nh0, 10):
                src_b, src_ap = (pQ, pQ[:, j * 64:(j + 1) * 64]) if j < 8 else (pKV, pKV[:, (j - 8) * 64:(j - 7) * 64])
                kb.ts("dve", xw, xw[:, j * 64:(j + 1) * 64], src_b, src_ap, rr10[:, j:j + 1], None, ALU.mult, extra_reads=[rr10])
            kb.tt("dve", xw, xw[:, c0:640], xw, xw[:, c0:640], qkw, qkw[:, c0:640], ALU.mult)
            yield
            xv = xw[:, c0:640].rearrange("p (n two s) -> p n two s", two=2, s=16)
            sv = sw[:, c0:640].rearrange("p (n two s) -> p n two s", two=2, s=16)
            kb.cp("pool", sw, sv[:, :, 0, :], xw, xv[:, :, 1, :])
            kb.cp("pool", sw, sv[:, :, 1, :], xw, xv[:, :, 0, :])
            kb.tt("dve", t1, t1[:, c0:640], xw, xw[:, c0:640], rp, rp[:, c0:640], ALU.mult)
            kb.tt("dve", sw, sw[:, c0:640], sw, sw[:, c0:640], rp, rp[:, 640 + c0:1280], ALU.mult)
            kb.tt("dve", qkb, qkb[:, c0:640], t1, t1[:, c0:640], sw, sw[:, c0:640], ALU.add)
            yield
            pb = kb.bank()
            pv = pb.t[:, :].bitcast(BF16)
            kb.tr(pb, pv[:, 0:128], qkb, qkb[:, 512:640], identb, identb[:, :])
            kb.cp("act", kTt, kTt[:, :], pb, pv[:, 0:128])
            kb.dma("sp", kT_d[:, vis * 128:(vis + 1) * 128], kTt[:, :], reads=[kTt])
            kb.cp("act", vt, vt[:, :].rearrange("p (k c) -> p k c", k=2)[:, :, 0:64], pKV,
                  pKV[:, 128:256].rearrange("p (k c) -> p k c", k=2))
            kb.dma("sp", v_d[vis, :, :], vt[:, :], reads=[vt])
            yield
            if own:
                tr4(qTt, qkb)
                for pr in range(4):
                    kb.dma("sp", qT_d[pr, :, m * 128:(m + 1) * 128], qTt[:, H(pr)], reads=[qTt])

        o1t = kb.sb([128, 512], F32, "o1t")
        junk2 = kb.sb([128, 128], BF16, "junk2o")
        o1r = kb.sb([128, 512], F32, "o1r")
        osum = kb.sb([128, 512], F32, "osum")
        ssd, rsd = sm("ssd"), sm("rsd")
        od = kb.sb([128, 512], BF16, "od")
        odT = kb.sb([128, 512], BF16, "odT")

        def own_out_scan1(pOs, m):
            for hp in range(2):
                kb.cp("act" if hp == 0 else "dve", o1t, o1t[:, hp * HW:(hp + 1) * HW], pOs[hp], pOs[hp][:, 0:HW])
                kb.release(pOs[hp])
            kb.dma("sp", o1_d[m, :, :], o1t[:, :], reads=[o1t])

        def own_out_scan2(pOs, m, F):
            zs = F.zs
            kb._wait("sp", o1t.r.get(("dma", o1t.name)))
            kb.dma("sp", o1r[:, :], o1_d[m, :, :], writes=[o1r])
            for hp in range(2):
                kb.tt("dve", osum, osum[:, hp * HW:(hp + 1) * HW], o1r, o1r[:, hp * HW:(hp + 1) * HW], pOs[hp], pOs[hp][:, 0:HW], ALU.add)
                kb.release(pOs[hp])
            for hh in range(4):
                kb.act(junk2, junk2[:, :], osum, osum[:, H(hh)], AF.Square, accum=ssd[:, hh:hh + 1], extra_writes=[ssd])
            kb.act(rsd, rsd[:, :], ssd, ssd[:, :], AF.Ln, bias=EPS, scale=1.0 / 128)
            kb.act(rsd, rsd[:, :], rsd, rsd[:, :], AF.Exp, scale=-0.5)
            for hh in range(4):
                kb.stt(od, od[:, H(hh)], osum, osum[:, H(hh)], rsd[:, hh:hh + 1], zs, zs[:, H(hh)], ALU.mult, ALU.mult,
                       extra_reads=[rsd])
            tr4(odT, od)
            kb.dma("sp", mdn_d[m, :, :], odT[:, :], reads=[odT])

        nown, pre1, pre2 = CFG["own"], CFG["pre1"], CFG["pre2"]

        def vinfo(ds, vis):
            npre = pre1 if ds == 0 else pre2
            return vis < 2, vis >= 2 + npre, vis - 2 - npre

        def front(ds, vis, F):
            isctx, own, sidx = vinfo(ds, vis)
            norm_tile(ds, vis, xt[ds], S1c if isctx else S1, H1c if isctx else H1)
            yield
            if ds == 1:
                yield from attn_proj(vis, own, sidx if own else 0)
                if own:
                    pZ = project_tok(ZOFF, 512)
                    kb.act(F.zs, F.zs[:, :], pZ, pZ[:, :], AF.Exp, scale=-1.0)
                    kb.ts("dve", F.zs, F.zs[:, :], F.zs, F.zs[:, :], 1.0, None, ALU.add)
                    kb.op("dve", lambda g: g.reciprocal(F.zs[:, :], F.zs[:, :]), reads=[F.zs], writes=[F.zs])
                    kb.tt("dve", F.zs, F.zs[:, :], pZ, pZ[:, :], F.zs, F.zs[:, :], ALU.mult)
                    kb.tt("dve", F.zs, F.zs[:, :], F.zs, F.zs[:, :], dnw, dnw[:, :], ALU.mult)
                    yield
            yield from gdn_pre_conv(ds, vis, 12 if own else 8, F)
            gdn_gates(ds, vis, own, F)
            yield

        def run_all(gens):
            gens = list(gens)
            while gens:
                for g in list(gens):
                    try:
                        next(g)
                    except StopIteration:
                        gens.remove(g)

        order = []
        v1 = 0
        n2_head = 2 + pre2
        for v2 in range(N2):
            if v2 == n2_head:
                while v1 < N1:
                    order.append((0, v1))
                    v1 += 1
            order.append((1, v2))
            while v1 < N1 and v1 * n2_head < (v2 + 1) * N1 and v2 < n2_head:
                order.append((0, v1))
                v1 += 1
        assert len(order) == N1 + N2

        run_all([front(order[0][0], order[0][1], fsets[0])])
        for n_, (ds, vis) in enumerate(order):
            F = fsets[n_ % 2]
            isctx, own, sidx = vinfo(ds, vis)
            res = [None, None]
            gens = [gdn_chain(ds, vis, own, chains[0], res, F), gdn_chain(ds, vis, own, chains[1], res, F)]
            if n_ + 1 < len(order):
                gens.append(front(order[n_ + 1][0], order[n_ + 1][1], fsets[(n_ + 1) % 2]))
            run_all(gens)
            if own and ds == 0:
                own_out_scan1(res, nown - 1 - sidx)
            if own and ds == 1:
                own_out_scan2(res, sidx, F)
        barrier()
        for e in ("sp",):
            for bb in (kTt, vt, qTt, odT, o1t):
                for dd in list(bb.r.values()):
                    kb._wait(e, dd)

        if DEBUG:
            dbgM = nc.dram_tensor("dbgM", [nown, 128, 512], BF16, kind="ExternalOutput").ap()
            dbgK = nc.dram_tensor("dbgK", [128, N2 * 128], BF16, kind="ExternalOutput").ap()
            dbgV = nc.dram_tensor("dbgV", [N2, 128, 130], BF16, kind="ExternalOutput").ap()
            dbgQ = nc.dram_tensor("dbgQ", [4, 128, nown * 128], BF16, kind="ExternalOutput").ap()
            tb = kb.sb([128, N2 * 128], BF16, "dbgtb")
            for src, dst, shp in ((kT_d[:, :], dbgK[:, :], None),):
                kb.dma("sp", tb[:, :], src, writes=[tb])
                dd = kb.dma("sp", dst, tb[:, :], reads=[tb]); kb._wait("sp", dd)
            for m in range(nown):
                kb.dma("sp", tb[:, 0:512], mdn_d[m, :, :], writes=[tb])
                dd = kb.dma("sp", dbgM[m, :, :], tb[:, 0:512], reads=[tb]); kb._wait("sp", dd)
            for v in range(N2):
                kb.dma("sp", tb[:, 0:130], v_d[v, :, :], writes=[tb])
                dd = kb.dma("sp", dbgV[v, :, :], tb[:, 0:130], reads=[tb]); kb._wait("sp", dd)
            for pr in range(4):
                kb.dma("sp", tb[:, 0:nown * 128], qT_d[pr, :, :], writes=[tb])
                dd = kb.dma("sp", dbgQ[pr, :, :], tb[:, 0:nown * 128], reads=[tb]); kb._wait("sp", dd)
        barrier()
        esB.close()
        kb.es = es

        nq = nown * 128
        NK = N2
        esCD = ExitStack()
        kb.es = esCD
        w1b = kb.sb([128, 8, 4 * D], BF16, "w1b")
        woab = kb.sb([64, 8, D], BF16, "woab")
        wodb = kb.sb([128, 4, D], BF16, "wodb")
        esW2 = ExitStack()
        kb.es = esW2
        wst2 = [kb.sb([128, 2048], F32, f"wst2_{i}") for i in range(2)]
        w2c = kb.sb([128, 2048], BF16, "w2c")
        kb.es = esCD
        ci = 0

        def conv_chunk(dst_b, dst_ap, src_ap, np_=128):
            nonlocal ci
            st = wst2[ci % 2]
            ci += 1
            kb.dma("pool", st[0:np_, :], src_ap, writes=[st])
            kb.cp("dve", dst_b, dst_ap, st, st[0:np_, :])

        def emit_weight_prep():
            for k in range(8):
                for hf in range(2):
                    conv_chunk(w1b, w1b[:, k, hf * 2048:(hf + 1) * 2048], w1_d[:, k, hf * 2048:(hf + 1) * 2048])
            for c in range(4):
                conv_chunk(woab, woab[:, c * 2:(c + 1) * 2, :].rearrange("p a b -> p (a b)"),
                           woa_d[:, c * 2:(c + 1) * 2, :].rearrange("p a b -> p (a b)"), 64)
            for c in range(2):
                conv_chunk(wodb, wodb[:, c * 2:(c + 1) * 2, :].rearrange("p a b -> p (a b)"), wod_d[:, c * 2:(c + 1) * 2, :].rearrange("p a b -> p (a b)"))
            for c in range(16):
                conv_chunk(w2c, w2c[:, :], w2_d[:, c * 2:(c + 1) * 2, :].rearrange("p a b -> p (a b)"))
                kb.dma("pool", w2b_d[:, c * 2:(c + 1) * 2, :].rearrange("p a b -> p (a b)"), w2c[:, :], reads=[w2c])

        esC = ExitStack()
        kb.es = esC
        KTd = [kb.sb([128, NK * 128], BF16, f"KTd{i}") for i in range(2)]
        Vs = kb.sb([128, NK, 130], BF16, "Vs")
        QTs = kb.sb([128, 4, nq], BF16, "QTs")
        for kv in range(2):
            kb.dma("sp", KTd[kv][0:64, :], kT_d[kv * 64:(kv + 1) * 64, :], writes=[KTd[kv]])
            kb.dma("sp", KTd[kv][64:128, :], kT_d[kv * 64:(kv + 1) * 64, :], writes=[KTd[kv]])
        kb.dma("sp", Vs[:, :, :], v_d.rearrange("v p c -> p v c"), writes=[Vs])
        for pr in range(4):
            kb.dma("sp", QTs[:, pr, :], qT_d[pr, :, :], writes=[QTs])
        emit_weight_prep()
        QB = min(512, nq)
        PTs = [kb.sb([128, QB], BF16, f"PT{i}") for i in range(6)]
        osb = kb.sb([65, QB], F32, "osb")
        rc = kb.sb([65, QB], F32, "rc")
        oab = kb.sb([64, QB], BF16, "oab")
        pti = 0
        osbs = [osb, kb.sb([65, QB], F32, "osb1")]
        rcs = [rc, kb.sb([65, QB], F32, "rc1")]
        oabs = [oab, kb.sb([64, QB], BF16, "oab1")]
        for q0 in range(0, nq, QB):
            for pr in range(4):
                kv = pr // 2
                pOs = [kb.reserve(), kb.reserve()]

                def qk(kt):
                    outs = []
                    for half in range(2):
                        ps_ = slice(half * 64, half * 64 + 64)
                        pS_ = kb.bank()
                        kb.mm(pS_, pS_[:, 0:QB], KTd[kv], KTd[kv][ps_, kt * 128:(kt + 1) * 128], QTs, QTs[ps_, pr, q0:q0 + QB])
                        outs.append(pS_)
                    return outs

                LOOK = 1
                pend = [qk(kt) for kt in range(min(LOOK, NK))]
                for kt in range(NK):
                    pSs = pend.pop(0)
                    PTp = []
                    for half in range(2):
                        PT = PTs[pti % len(PTs)]
                        pti += 1
                        kb.act(PT, PT[:, :], pSs[half], pSs[half][:, 0:QB], AF.Exp, scale=0.125)
                        PTp.append(PT)
                    if kt + LOOK < NK:
                        pend.append(qk(kt + LOOK))
                    for half in range(2):
                        kb.mm(pOs[half], pOs[half][0:65, 0:QB], Vs, Vs[:, kt, kv * 65:(kv + 1) * 65], PTp[half], PTp[half][:, :],
                              start=(kt == 0), stop=(kt == NK - 1))
                for half in range(2):
                    hh = pr * 2 + half
                    o_, r_, ob_ = osbs[half], rcs[half], oabs[half]
                    kb.cp("act", o_, o_[:, :], pOs[half], pOs[half][0:65, 0:QB])
                    kb.release(pOs[half])
                    kb.op("dve", lambda g, r_=r_, o_=o_: g.reciprocal(r_[64:65, :], o_[64:65, :]), reads=[o_], writes=[r_])
                    pB = kb.bank()
                    kb.mm(pB, pB[0:64, 0:QB], onesf, onesf[64:65, 0:64], r_, r_[64:65, :])
                    kb.tt("dve", ob_, ob_[:, :], o_, o_[0:64, :], pB, pB[0:64, 0:QB], ALU.mult)
                    kb.dma("sp", oa_d[:, hh, q0:q0 + QB], ob_[:, :], reads=[ob_])
        barrier()
        for ob_ in oabs:
            for dd in list(ob_.r.values()):
                kb._wait("sp", dd)
        for dd in list(w2c.r.values()):
            kb._wait("sp", dd)
            kb._wait("pool", dd)
        esC.close()
        kb.es = esCD

        esD = ExitStack()
        kb.es = esD
        G1 = kb.sb([128, D], F32, "G1")
        S2 = kb.sb([128, D], F32, "S2")
        H2 = kb.sb([128, D], F32, "H2")
        G2 = kb.sb([128, D], F32, "G2")
        for i_, bb in enumerate((G1, S2, H2, G2)):
            kb.dma("sp", bb[:, :], mod_d[i_, :, :], writes=[bb])
        TB = min(2, nown)
        NB = TB * 128
        oat = kb.sb([64, 8, NB], BF16, "oat")
        mdt = kb.sb([128, TB, 512], BF16, "mdt")
        xo = kb.sb([128, D], F32, "xo")
        x1b = kb.sb([128, TB, D], F32, "x1b")
        junkD = kb.sb([128, D], BF16, "junkD")
        ssqD = kb.sb([128, 1], F32, "ssqD")
        rstdD = kb.sb([128, 1], F32, "rstdD")
        t32D = kb.sb([128, D], F32, "t32D")
        hbD = kb.sb([128, D], BF16, "hbD")
        h2T = kb.sb([128, 8, NB], BF16, "h2T")
        w2s = [kb.sb([128, 4, D], BF16, f"w2s{i}") for i in range(2)]
        rls = [kb.sb([128, NB], F32, f"rl{i}") for i in range(2)]
        aT = [kb.sb([128, NB], BF16, f"aT{i}") for i in range(3)]
        yo = kb.sb([128, D], F32, "yo")
        ai = 0
        out_deps = []
        for blk0 in range(0, nown, TB):
            t0 = blk0 * 128
            kb.dma("sp", oat[:, :, :], oa_d[:, :, t0:t0 + NB], writes=[oat])
            for t in range(TB):
                kb.dma("sp", mdt[:, t, :], mdn_d[blk0 + t, :, :], writes=[mdt])
            for t in range(TB):
                m = blk0 + t
                kb.dma("sp", xo[:, :], xown_d[m, :, :], writes=[xo])
                for half in range(2):
                    cs = slice(half * 512, half * 512 + 512)
                    pM = kb.bank()
                    for hh in range(8):
                        kb.mm(pM, pM[:, :], oat, oat[:, hh, t * 128:(t + 1) * 128], woab, woab[:, hh, cs], start=(hh == 0), stop=False)
                    for hd in range(4):
                        kb.mm(pM, pM[:, :], mdt, mdt[:, t, hd * 128:(hd + 1) * 128], wodb, wodb[:, hd, cs], start=False, stop=(hd == 3))
                    kb.tt("dve", x1b, x1b[:, t, cs], pM, pM[:, :], G1, G1[:, cs], ALU.mult)
                kb.tt("dve", x1b, x1b[:, t, :], x1b, x1b[:, t, :], xo, xo[:, :], ALU.add)
                kb.act(junkD, junkD[:, :], x1b, x1b[:, t, :], AF.Square, accum=ssqD[:, :], extra_writes=[ssqD])
                kb.act(rstdD, rstdD[:, :], ssqD, ssqD[:, :], AF.Ln, bias=EPS, scale=1.0 / D)
                kb.act(rstdD, rstdD[:, :], rstdD, rstdD[:, :], AF.Exp, scale=-0.5)
                kb.stt(t32D, t32D[:, :], x1b, x1b[:, t, :], rstdD[:, 0:1], S2, S2[:, :], ALU.mult, ALU.mult, extra_reads=[rstdD])
                kb.tt("dve", hbD, hbD[:, :], t32D, t32D[:, :], H2, H2[:, :], ALU.add)
                pb = kb.bank()
                pbv = pb.t[:, :].bitcast(BF16)
                for k in range(8):
                    kb.tr(pb, pbv[:, k * 128:(k + 1) * 128], hbD, hbD[:, k * 128:(k + 1) * 128], identb, identb[:, :])
                kb.cp("act", h2T, h2T[:, :, t * 128:(t + 1) * 128], pb, pbv.rearrange("p (k t) -> p k t", k=8))
            acc = [[kb.reserve() for _ in range(2)] for _ in range(TB)]
            def mlp1(f):
                pA1_ = kb.bank()
                for k in range(8):
                    kb.mm(pA1_, pA1_[:, 0:NB], w1b, w1b[:, k, f * 128:(f + 1) * 128], h2T, h2T[:, k, :], start=(k == 0), stop=(k == 7))
                return pA1_

            pendA = [mlp1(0)]
            for f in range(32):
                fg, fj = f // 4, f % 4
                ws = w2s[fg % 2]
                if fj == 0:
                    kb.dma("sp", ws[:, :, :], w2b_d[:, fg * 4:(fg + 1) * 4, :], writes=[ws])
                pA1 = pendA.pop(0)
                r_ = rls[f % 2]
                kb.act(r_, r_[:, :], pA1, pA1[:, 0:NB], AF.Relu)
                a_ = aT[ai % 3]
                ai += 1
                kb.tt("dve", a_, a_[:, :], r_, r_[:, :], r_, r_[:, :], ALU.mult)
                if f + 1 < 32:
                    pendA.append(mlp1(f + 1))
                for t in range(TB):
                    for half in range(2):
                        kb.mm(acc[t][half], acc[t][half][:, :], a_, a_[:, t * 128:(t + 1) * 128], ws, ws[:, fj, half * 512:half * 512 + 512],
                              start=(f == 0), stop=(f == 31))
            for t in range(TB):
                m = blk0 + t
                for half in range(2):
                    cs = slice(half * 512, half * 512 + 512)
                    kb.tt("dve", yo, yo[:, cs], acc[t][half], acc[t][half][:, :], G2, G2[:, cs], ALU.mult)
                    kb.release(acc[t][half])
                kb.tt("dve", yo, yo[:, :], yo, yo[:, :], x1b, x1b[:, t, :], ALU.add)
                out_deps.append(kb.dma("sp", out_d[m, :, :], yo[:, :], reads=[yo]))
        for dd in out_deps:
            kb._wait("sp", dd)
        for dd in list(yo.r.values()):
            kb._wait("sp", dd)
        barrier()
        esD.close()
        esW2.close()
        esCD.close()
        kb.es = es
        return nc, kb, es


def _rep(v, n=128):
    v = np.asarray(v, np.float32)
    return np.ascontiguousarray(np.broadcast_to(v[None, :], (n, v.shape[0])))


def _win_perm(dirs=(0, 1)):
    perm = list(range(0, 768))
    for base in (1280, 1792, 768):
        for h in range(4):
            perm += list(range(base + h * 128, base + (h + 1) * 128))
    perm += list(range(2304, 2816))
    for base in (2816, 2824):
        for dr in dirs:
            perm += [base + dr * 4 + h for h in range(4)]
    return np.array(perm)


def _halo_rows(seq, t0):
    out = np.zeros((4, seq.shape[1]), np.float32)
    fl = 1.0 if t0 - 2 >= 0 else 0.0
    fr = 1.0 if t0 + 130 <= seq.shape[0] else 0.0
    if fl:
        out[0:2] = seq[t0 - 2:t0]
    if fr:
        out[2:4] = seq[t0 + 128:t0 + 130]
    return out, fl, fr


def _rope_table(tile):
    out = np.zeros((128, 1280), np.float32)
    if tile is None:
        out[:, 0:640] = 1.0
        return out
    t = tile * 128 + np.arange(128)
    pos = [(t // 64).astype(np.float32), (t % 64).astype(np.float32)]
    inv = (np.float32(10000.0) ** (-np.arange(16, dtype=np.float32) / np.float32(16))).astype(np.float32)
    C = np.zeros((128, 64), np.float32)
    S = np.zeros((128, 64), np.float32)
    for blk in range(2):
        ang = (pos[blk][:, None] * inv[None, :]).astype(np.float32)
        c, sn = np.cos(ang).astype(np.float32), np.sin(ang).astype(np.float32)
        C[:, blk * 32:blk * 32 + 16] = c
        C[:, blk * 32 + 16:blk * 32 + 32] = c
        S[:, blk * 32:blk * 32 + 16] = -sn
        S[:, blk * 32 + 16:blk * 32 + 32] = sn
    out[:, 0:640] = np.tile(C, (1, 10))
    out[:, 640:1280] = np.tile(S, (1, 10))
    return out


def _const_tables(dirs):
    p = np.arange(128)[:, None]
    f = np.arange(128)[None, :]
    tri = np.zeros((128, 2, 128), np.float32)
    masks = np.zeros((128, 4, 512), np.float32)
    for ds, dr in enumerate(dirs):
        if dr == 0:
            tri[:, ds, :] = (p <= f)
            okS, okIT = (p > f), (f >= p)
        else:
            tri[:, ds, :] = (p >= f)
            okS, okIT = (p < f), (f <= p)
        masks[:, ds * 2 + 0, :] = np.tile(np.where(okS, 0.0, BIG), (1, 4))
        masks[:, ds * 2 + 1, :] = np.tile(np.where(okIT, 0.0, -BIG), (1, 4))
    im = np.zeros((128, 4, 512), np.float32)
    im[:, 0, :] = np.tile((p // 16) == (f // 16), (1, 4))
    for mi, sz in ((1, 16), (2, 32), (3, 64)):
        same = (p // (2 * sz)) == (f // (2 * sz))
        im[:, mi, :] = np.tile(same & (((p % (2 * sz)) >= sz) != ((f % (2 * sz)) >= sz)), (1, 4))
    return tri, masks, im


def _plan(i):
    own_lo = 16 * i
    fwd_pre = list(range(0, own_lo))
    bwd_pre = list(range(63, own_lo + 15, -1))
    fwd_own = list(range(own_lo, own_lo + 16))
    bwd_own = fwd_own[::-1]
    fwd_ctx, bwd_ctx = [0, 1], [1, 0]
    if i < 2:
        dirs = (0, 1)
        s1_ctx, s1_pre, s1_own = fwd_ctx, fwd_pre, fwd_own
        s2_ctx, s2_pre, s2_own = bwd_ctx, bwd_pre, bwd_own
    else:
        dirs = (1, 0)
        s1_ctx, s1_pre, s1_own = bwd_ctx, bwd_pre, bwd_own
        s2_ctx, s2_pre, s2_own = fwd_ctx, fwd_pre, fwd_own
    scan1 = [("ctx", t, 1.0) for t in s1_ctx]
    scan1 += [("lat", 0, 0.0)] * (16 - len(s1_pre)) + [("lat", t, 1.0) for t in s1_pre]
    scan1 += [("lat", t, 1.0) for t in s1_own]
    scan2 = [("ctx", t, 1.0) for t in s2_ctx]
    scan2 += [("lat", t, 0.0) for t in s1_pre] if len(s2_pre) < 48 else []
    scan2 += [("lat", t, 1.0) for t in s2_pre]
    scan2 += [("lat", t, 1.0) for t in s2_own]
    assert len(scan1) == N1 and len(scan2) == N2, (len(scan1), len(scan2))
    assert sorted(t for k, t, _ in scan2 if k == "lat") == list(range(64))
    return dirs, scan1, scan2, s2_own


def kernel(**inputs):
    set_cfg(16, 48, 16)
    g = lambda k: np.asarray(inputs[k], np.float32)
    x, c, ctx, c_ctx = g("x"), g("c"), g("ctx"), g("c_ctx")
    w_mod, w_in, conv_w = g("w_mod")[0], g("w_in")[0], g("conv_w")[0]
    w_out, w1, w2 = g("w_out")[0], g("w_mlp1")[0], g("w_mlp2")[0]
    a_log, dt_bias = g("a_log")[0], g("dt_bias")[0]

    shared = {
        "wmod": np.ascontiguousarray(w_mod.reshape(8, 128, 6, D).transpose(2, 1, 0, 3).reshape(6, 128, 8 * D)),
        "bmod": _rep(g("b_mod")[0]), "n1w": _rep(g("norm1_w")[0]), "n2w": _rep(g("norm2_w")[0]),
        "qkw": _rep(np.concatenate([np.tile(g("q_norm_w")[0], 8), np.tile(g("k_norm_w")[0], 2)])),
        "dnw": _rep(np.tile(g("dn_norm_w")[0], 4)),
        "woa": np.ascontiguousarray(w_out[0:512].reshape(8, 64, D).transpose(1, 0, 2)),
        "wod": np.ascontiguousarray(w_out[512:1024].reshape(4, 128, D).transpose(1, 0, 2)),
        "w1": np.ascontiguousarray(w1.reshape(8, 128, 4 * D).transpose(1, 0, 2)),
        "w2": np.ascontiguousarray(w2.reshape(32, 128, D).transpose(1, 0, 2)),
        "ident": np.eye(128, dtype=np.float32),
    }
    cw = np.zeros((128, 12, 5), np.float32)
    blk = 0
    for base in (512, 1024, 0):
        for h in range(4):
            cw[:, blk, :] = conv_w[:, base + h * 128:base + (h + 1) * 128].T
            blk += 1
    shared["convw"] = np.ascontiguousarray(cw.reshape(128, 60))
    per_dirs = {}
    for dirs in ((0, 1), (1, 0)):
        tri, masks, im = _const_tables(dirs)
        per_dirs[dirs] = {
            "tri": tri, "masks": masks, "imask": im,
            "win": np.ascontiguousarray(w_in[:, _win_perm(dirs)].reshape(8, 128, NCOL).transpose(1, 0, 2)),
            "alog": _rep(np.concatenate([a_log[dirs[0]], a_log[dirs[1]]])),
            "dtb": _rep(np.concatenate([dt_bias[dirs[0]], dt_bias[dirs[1]]])),
        }
    rope_cache = {t: _rope_table(t) for t in list(range(64)) + [None]}

    in_maps, own_maps = [], []
    for core in range(8):
        b, i = core // 4, core % 4
        dirs, scan1, scan2, own_tiles = _plan(i)
        feed = dict(shared)
        feed.update(per_dirs[dirs])
        feed["cc"] = np.ascontiguousarray(np.concatenate([c[b].reshape(8, 128).T, c_ctx.reshape(8, 128).T], axis=1))
        flags = np.zeros((128, 2, N2, 4), np.float32)
        seqs = {"ctx": ctx[b], "lat": x[b]}
        for ds, (scan, n) in enumerate(((scan1, N1), (scan2, N2))):
            xs = np.zeros((n, 128, D), np.float32)
            xh = np.zeros((n, 4, D), np.float32)
            for vis, (kind, t, gflag) in enumerate(scan):
                seq = seqs[kind]
                xs[vis] = seq[t * 128:(t + 1) * 128]
                xh[vis], fl, fr = _halo_rows(seq, t * 128)
                flags[:, ds, vis, 0:3] = (fl, fr, gflag)
            feed["xs%d" % (ds + 1)] = xs
            feed["xh%d" % (ds + 1)] = xh
        feed["flags"] = flags
        feed["rope2"] = np.stack([rope_cache[None if kind == "ctx" else t] for kind, t, _ in scan2])
        feed["xown"] = np.stack([x[b, t * 128:(t + 1) * 128] for t in own_tiles])
        in_maps.append(feed)
        own_maps.append((b, own_tiles))

    nc, kb, _ = build_program()
    res = run_bass_kernel_spmd(nc, in_maps, core_ids=list(range(8)))
    out = np.zeros((2, T, D), np.float32)
    for core, (b, own_tiles) in enumerate(own_maps):
        o = np.asarray(res.results[core]["out"], np.float32)
        for m, t in enumerate(own_tiles):
            out[b, t * 128:(t + 1) * 128] = o[m]
    return out
```
